# Optimizing a Trainium2 kernel written in Bass

```python
import math
import jax
import jax.numpy as jnp
from jax import lax
import numpy as np


D_MODEL = 2048
BATCH = 4
SEQ = 8192
DEPTH = 1

ATTN_HEADS = 16
ATTN_HEAD_DIM = D_MODEL // ATTN_HEADS
ATTN_WIDTH = ATTN_HEADS * ATTN_HEAD_DIM
DILATED_PATTERNS = ((128, 1), (512, 4), (2048, 16))
ATTN_BLOCK = 128

SSM_EXPAND = 2
SSM_INNER = SSM_EXPAND * D_MODEL
SSM_HEAD_DIM = 64
SSM_HEADS = SSM_INNER // SSM_HEAD_DIM
SSM_GROUPS = 8
SSM_STATE = 128
SSM_CONV = 4
SSM_CHUNK = 128
SSM_CONV_DIM = SSM_INNER + 2 * SSM_GROUPS * SSM_STATE

RMS_EPS = 1e-6
IN_SIZES = (ATTN_WIDTH, ATTN_WIDTH, ATTN_WIDTH, ATTN_WIDTH,
            SSM_INNER, SSM_CONV_DIM, SSM_HEADS, D_MODEL, D_MODEL)
N_IN = 4 * ATTN_WIDTH + SSM_INNER + SSM_CONV_DIM + SSM_HEADS + 2 * D_MODEL

kernel_name = 'hybrid_dilated_attn_ssd_block'


def rms_norm(x, w):
    xf = x.astype(jnp.float32)
    y = xf * lax.rsqrt(jnp.mean(xf * xf, axis=-1, keepdims=True) + RMS_EPS)
    return (y * w.astype(jnp.float32)).astype(x.dtype)


def alibi_slopes(n_heads):
    return jnp.asarray([2.0 ** (-8.0 * (h + 1) / n_heads) for h in range(n_heads)], jnp.float32)


def dilated_window_attention(q, k, v, window, dilation, slopes):
    b, s, h, e = q.shape
    sub_len = s // dilation
    span = window // dilation
    blk = ATTN_BLOCK
    nb = -(-sub_len // blk)
    padded = nb * blk

    def to_sub(t):
        t = t.reshape(b, sub_len, dilation, h, e).transpose(0, 2, 3, 1, 4)
        t = jnp.pad(t, ((0, 0), (0, 0), (0, 0), (0, padded - sub_len), (0, 0)))
        return t.reshape(b, dilation, h, nb, blk, e)

    def with_prev(t):
        prev = jnp.pad(t[:, :, :, :-1], ((0, 0), (0, 0), (0, 0), (1, 0), (0, 0), (0, 0)))
        return jnp.concatenate([prev, t], axis=4)

    qb = to_sub(q)
    kc = with_prev(to_sub(k))
    vc = with_prev(to_sub(v))
    scores = jnp.einsum('brhiqe,brhike->brhiqk', qb, kc).astype(jnp.float32) * (e ** -0.5)

    qi = jnp.arange(blk)[:, None]
    ki = jnp.arange(2 * blk)[None, :]
    dist = qi - ki + blk
    key_idx = jnp.arange(nb)[:, None, None] * blk - blk + ki
    valid = (dist >= 0) & (dist <= span) & (key_idx >= 0)
    alibi = -slopes[:, None, None, None] * (dist * dilation).astype(jnp.float32)
    scores = jnp.where(valid, scores + alibi, -jnp.inf)
    lse = jax.nn.logsumexp(scores, axis=-1)
    p = jnp.exp(scores - lse[..., None])
    o = jnp.einsum('brhiqk,brhike->brhiqe', p.astype(v.dtype), vc)

    o = o.reshape(b, dilation, h, padded, e)[:, :, :, :sub_len]
    o = o.transpose(0, 3, 1, 2, 4).reshape(b, s, h, e)
    lse = lse.reshape(b, dilation, h, padded)[:, :, :, :sub_len]
    lse = lse.transpose(0, 3, 1, 2).reshape(b, s, h)
    return o, lse


def causal_depthwise_conv(x, w, bias):
    c = x.shape[-1]
    y = lax.conv_general_dilated(x, w[:, None, :].astype(x.dtype), window_strides=(1,),
                                 padding=[(SSM_CONV - 1, 0)],
                                 dimension_numbers=('NWC', 'WIO', 'NWC'),
                                 feature_group_count=c)
    return y + bias


def ssd_chunked_scan(xs, dt, a, bm, cm):
    b, s, g, j, p = xs.shape
    n = bm.shape[-1]
    L = SSM_CHUNK
    nc = s // L
    xdt = xs.astype(jnp.float32) * dt[..., None]
    da = dt * a

    def chunks(t):
        return jnp.moveaxis(t.reshape(b, nc, L, *t.shape[2:]), 1, 0)

    causal = jnp.tril(jnp.ones((L, L), dtype=bool))

    def step(state, inp):
        xc, dac, bc, cc = inp
        acum = jnp.cumsum(dac, axis=1)
        acum_t = jnp.moveaxis(acum, 1, -1)
        seg = acum_t[..., :, None] - acum_t[..., None, :]
        decay = jnp.exp(jnp.where(causal, seg, -jnp.inf))
        cb = jnp.einsum('blgn,bsgn->bgls', cc, bc)
        y_diag = jnp.einsum('bgls,bgjls,bsgjp->blgjp', cb, decay, xc)
        y_off = jnp.einsum('blgn,bgjpn,blgj->blgjp', cc, state, jnp.exp(acum))
        last = acum[:, -1]
        w_s = jnp.exp(last[:, None] - acum)
        new_state = state * jnp.exp(last)[..., None, None] + \
            jnp.einsum('blgj,blgjp,blgn->bgjpn', w_s, xc, bc)
        return new_state, y_diag + y_off

    state0 = jnp.zeros((b, g, j, p, n), jnp.float32)
    _, ys = lax.scan(step, state0, (chunks(xdt), chunks(da),
                                     chunks(bm.astype(jnp.float32)), chunks(cm.astype(jnp.float32))))
    return jnp.moveaxis(ys, 0, 1).reshape(b, s, g, j, p)


def hybrid_layer(x, norm_w, w_in, conv_w, conv_b, dt_bias, a_log, d_skip, ssm_norm_w,
                 w_attn_branch, w_ssm_branch, w_out):
    b, s, _ = x.shape
    hpg = SSM_HEADS // SSM_GROUPS
    hn = rms_norm(x, norm_w)
    proj = hn @ w_in
    split_points = []
    acc = 0
    for size in IN_SIZES[:-1]:
        acc += size
        split_points.append(acc)
    q, k, v, z_a, z_s, xbc, dt_raw, g_a, g_s = jnp.split(proj, split_points, axis=-1)

    q = q.reshape(b, s, ATTN_HEADS, ATTN_HEAD_DIM)
    k = k.reshape(b, s, ATTN_HEADS, ATTN_HEAD_DIM)
    v = v.reshape(b, s, ATTN_HEADS, ATTN_HEAD_DIM)
    slopes = alibi_slopes(ATTN_HEADS)
    outs = []
    lses = []
    for window, dilation in DILATED_PATTERNS:
        o, l = dilated_window_attention(q, k, v, window, dilation, slopes)
        outs.append(o)
        lses.append(l)
    wts = jax.nn.softmax(jnp.stack(lses), axis=0)
    o_a = jnp.einsum('pbsh,pbshe->bshe', wts.astype(q.dtype), jnp.stack(outs))
    o_a = o_a.reshape(b, s, ATTN_WIDTH) * jax.nn.silu(z_a)

    xbc = jax.nn.silu(causal_depthwise_conv(xbc, conv_w, conv_b))
    xs, bm, cm = jnp.split(xbc, [SSM_INNER, SSM_INNER + SSM_GROUPS * SSM_STATE], axis=-1)
    xs = xs.reshape(b, s, SSM_GROUPS, hpg, SSM_HEAD_DIM)
    bm = bm.reshape(b, s, SSM_GROUPS, SSM_STATE)
    cm = cm.reshape(b, s, SSM_GROUPS, SSM_STATE)
    dt = jax.nn.softplus(dt_raw.astype(jnp.float32) + dt_bias.astype(jnp.float32))
    dt = dt.reshape(b, s, SSM_GROUPS, hpg)
    a = -jnp.exp(a_log.astype(jnp.float32)).reshape(SSM_GROUPS, hpg)
    y = ssd_chunked_scan(xs, dt, a, bm, cm)
    y = y + d_skip.astype(jnp.float32).reshape(SSM_GROUPS, hpg)[..., None] * xs.astype(jnp.float32)
    y = y.reshape(b, s, SSM_INNER).astype(x.dtype)
    y = rms_norm(y * jax.nn.silu(z_s), ssm_norm_w)

    merged = jax.nn.sigmoid(g_a) * (o_a @ w_attn_branch) + jax.nn.sigmoid(g_s) * (y @ w_ssm_branch)
    return x + merged @ w_out


def setup_inputs(seed: int = 0) -> dict:
    key = jax.random.key(seed)
    ks = jax.random.split(key, 16)
    f32 = jnp.float32

    def dense(k, fan_in, fan_out):
        return jax.random.normal(k, (DEPTH, fan_in, fan_out), f32) * fan_in ** -0.5

    def gain(k, n):
        return 1.0 + 0.02 * jax.random.normal(k, (DEPTH, n), f32)

    x = jax.random.normal(ks[0], (BATCH, SEQ, D_MODEL), f32)
    norm_w = gain(ks[1], D_MODEL)
    w_in = dense(ks[2], D_MODEL, N_IN)
    conv_w = jax.random.normal(ks[3], (DEPTH, SSM_CONV, SSM_CONV_DIM), f32) * SSM_CONV ** -0.5
    conv_b = 0.01 * jax.random.normal(ks[4], (DEPTH, SSM_CONV_DIM), f32)
    u = jax.random.uniform(ks[5], (DEPTH, SSM_HEADS), f32)
    dt0 = jnp.exp(u * (math.log(0.1) - math.log(0.001)) + math.log(0.001))
    dt_bias = dt0 + jnp.log(-jnp.expm1(-dt0))
    a_log = jnp.log(jax.random.uniform(ks[6], (DEPTH, SSM_HEADS), f32, 1.0, 16.0))
    d_skip = gain(ks[7], SSM_HEADS)
    ssm_norm_w = gain(ks[8], SSM_INNER)
    w_attn_branch = dense(ks[9], ATTN_WIDTH, D_MODEL)
    w_ssm_branch = dense(ks[10], SSM_INNER, D_MODEL)
    w_out = dense(ks[11], D_MODEL, D_MODEL)
    final_norm_w = 1.0 + 0.02 * jax.random.normal(ks[12], (D_MODEL,), f32)
    return {'x': x, 'norm_w': norm_w, 'w_in': w_in, 'conv_w': conv_w, 'conv_b': conv_b,
            'dt_bias': dt_bias, 'a_log': a_log, 'd_skip': d_skip, 'ssm_norm_w': ssm_norm_w,
            'w_attn_branch': w_attn_branch, 'w_ssm_branch': w_ssm_branch, 'w_out': w_out,
            'final_norm_w': final_norm_w}


def reference(x, norm_w, w_in, conv_w, conv_b, dt_bias, a_log, d_skip, ssm_norm_w,
              w_attn_branch, w_ssm_branch, w_out, final_norm_w):
    for layer in range(DEPTH):
        x = hybrid_layer(x, norm_w[layer], w_in[layer], conv_w[layer], conv_b[layer],
                         dt_bias[layer], a_log[layer], d_skip[layer], ssm_norm_w[layer],
                         w_attn_branch[layer], w_ssm_branch[layer], w_out[layer])
    return rms_norm(x, final_norm_w)
```

```python
import numpy as np
import ml_dtypes
from contextlib import ExitStack
import concourse.bass as bass
import concourse.mybir as mybir
from concourse.bass_utils import run_bass_kernel_spmd

F32 = mybir.dt.float32
BF16 = mybir.dt.bfloat16
AF = mybir.ActivationFunctionType
ALU = mybir.AluOpType

D = 2048
NIN = 22592
SEQ = 8192
NCORE = 8
TOKM = 4096
EXT = 8192
T = 512
NT = EXT // T
HALO0 = 2048
EPS = 1e-6
NEG = -30000.0
DBG_CHUNKS = None
DBG_GROUPS = 8
DBG_SU = 8
DBG_LIMIT = None
DBG_F = None
DBG_SKIP_P0 = False

C_Q, C_K, C_V, C_ZA, C_ZS, C_X, C_B, C_C, C_DT, C_GA, C_GS = 0, 2048, 4096, 6144, 8192, 12288, 16384, 17408, 18432, 18496, 20544

WG = []
def _add(name, src, row0, col0, width):
    WG.append((name, src, row0, col0, width))
    return len(WG) - 1
G_Q = [_add("q%d" % i, "w_in", 0, C_Q + 512 * i, 512) for i in range(4)]
G_K = [_add("k%d" % i, "w_in", 0, C_K + 512 * i, 512) for i in range(4)]
G_V = [_add("v%d" % i, "w_in", 0, C_V + 512 * i, 512) for i in range(4)]
G_ZA = [_add("za%d" % i, "w_in", 0, C_ZA + 512 * i, 512) for i in range(4)]
G_ZS = [_add("zs%d" % i, "w_in", 0, C_ZS + 512 * i, 512) for i in range(8)]
G_XBC = [_add("xbc%d" % i, "w_in", 0, C_X + 512 * i, 512) for i in range(12)]
G_DT = _add("dt", "w_in", 0, C_DT, 64)
G_GA = [_add("ga%d" % i, "w_in", 0, C_GA + 512 * i, 512) for i in range(4)]
G_GS = [_add("gs%d" % i, "w_in", 0, C_GS + 512 * i, 512) for i in range(4)]
G_WA = [_add("wa%d" % i, "w_attn", 0, 512 * i, 512) for i in range(4)]
G_WS = [[_add("ws%d_%d" % (i, kh), "w_ssm", 2048 * kh, 512 * i, 512) for kh in range(2)] for i in range(4)]
G_WO = [_add("wo%d" % i, "w_out", 0, 512 * i, 512) for i in range(4)]
NG = len(WG)
P0_GROUPS = G_XBC + [G_DT]
P1A_JOBS = [(g, q) for g in (G_K + G_V + G_Q + G_ZA + G_ZS) for q in range(4)]
P2_JOBS = [(g, q) for g in (G_GA + G_GS + G_WA + [x for pr in G_WS for x in pr] + G_WO) for q in range(4)]

P_NORMW = 0
P_FNW = 2048
P_CW = 4096
P_CB = P_CW + 192
P_SNW = P_CB + 48
P_DTB = P_SNW + 32
P_ALOG = P_DTB + 64
P_DSK = P_ALOG + 64
P_PV = P_DSK + 64
P_ID = P_PV + 1
P_TM = P_ID + 128
P_UM = P_TM + 128
P_ONE = P_UM + 128
NPAR = P_ONE + 128


class Op:
    __slots__ = ("eng", "fn", "deps", "dma", "sem", "val", "has_dep", "cnt", "semi")

    def __init__(self, eng, fn, dma):
        self.eng = eng; self.fn = fn; self.dma = dma; self.deps = []
        self.sem = None; self.val = 0; self.has_dep = False; self.cnt = 0; self.semi = 0


class Prog:
    ENGS = ("pe", "act", "dve", "pool", "sp")
    NDS = 14
    CHUNK = 20000

    def __init__(self):
        self.ops = {e: [] for e in self.ENGS}
        self.res = {}
        self.ndma = {e: 0 for e in self.ENGS}
        self.pend = {}

    def barrier(self):
        deps = []
        for e in self.ENGS:
            last_c = None
            dm = {}
            for o in self.ops[e]:
                if o.dma:
                    dm[o.semi] = o
                else:
                    last_c = o
            if last_c is not None:
                deps.append(last_c)
            deps.extend(dm.values())
        self.pend = {e: list(deps) for e in self.ENGS}
        print("[barrier] ops so far:", getattr(self, "nadd", 0))

    def add(self, eng, fn, reads=(), writes=(), dma=False):
        op = Op(eng, fn, dma)
        self.nadd = getattr(self, "nadd", 0) + 1
        if DBG_LIMIT is not None and self.nadd > DBG_LIMIT:
            return op
        deps = list(self.pend.pop(eng, []))
        for k in reads:
            st = self.res.get(k)
            if st is not None and st[0] is not None:
                deps.append(st[0])
        for k in writes:
            st = self.res.get(k)
            if st is not None:
                if st[0] is not None:
                    deps.append(st[0])
                deps.extend(st[1].values())
                deps.extend(st[2])
        seen = set()
        for d in deps:
            if d is op or id(d) in seen:
                continue
            seen.add(id(d))
            if (not d.dma) and d.eng == eng and eng == "pe":
                continue
            op.deps.append(d)
            d.has_dep = True
        for k in reads:
            st = self.res.setdefault(k, [None, {}, []])
            if dma:
                st[2].append(op)
            else:
                st[1][eng] = op
        for k in writes:
            self.res[k] = [op, {}, []]
        if dma:
            n = self.ndma[eng]
            self.ndma[eng] = n + 1
            op.semi = n % self.NDS
            op.val = 16 * (n // self.NDS + 1)
        self.ops[eng].append(op)
        return op

    def emit(self, nc, es, final_waits):
        csem = {}
        for e in ("pe", "act", "dve", "pool"):
            nd = sum(1 for o in self.ops[e] if (not o.dma) and o.has_dep)
            csem[e] = [es.enter_context(nc.semaphore("c_%s_%d" % (e, i))) for i in range(nd // self.CHUNK + 1)]
            c = 0
            for o in self.ops[e]:
                if (not o.dma) and o.has_dep:
                    o.semi = c // self.CHUNK
                    o.cnt = c % self.CHUNK + 1
                    c += 1
        dsem = {}
        for e in self.ENGS:
            if self.ndma[e]:
                dsem[e] = [es.enter_context(nc.semaphore("d_%s_%d" % (e, i))) for i in range(self.NDS)]
        block = es.enter_context(nc.Block())
        prog = self

        wlog = self.wlog = []
        def run(e, h):
            seen = {}
            def wait(sem, val):
                key = id(sem)
                if seen.get(key, 0) >= val:
                    return
                seen[key] = val
                if DBG_LIMIT is not None:
                    wlog.append((e, str(getattr(sem, "name", sem)), val))
                h.wait_ge(sem, val)
            for o in prog.ops[e]:
                for d in o.deps:
                    if d.dma:
                        wait(dsem[d.eng][d.semi], d.val)
                    else:
                        wait(csem[d.eng][d.semi], d.cnt)
                if o.dma:
                    if o.val > 16:
                        wait(dsem[e][o.semi], o.val - 16)
                    ins = o.fn(h)
                    ins.then_inc(dsem[e][o.semi], 16)
                else:
                    ins = o.fn(h)
                    if o.has_dep:
                        ins.then_inc(csem[e][o.semi], 1)
            if e in final_waits:
                for d in final_waits[e]:
                    wait(dsem[d.eng][d.semi], d.val)

        @block.tensor
        def _(h):
            run("pe", h)

        @block.scalar
        def _(h):
            run("act", h)

        @block.vector
        def _(h):
            run("dve", h)

        @block.gpsimd
        def _(h):
            run("pool", h)

        @block.sync
        def _(h):
            run("sp", h)


class Ring:
    def __init__(self, items):
        self.items = items; self.i = 0

    def next(self):
        it = self.items[self.i % len(self.items)]
        self.i += 1
        return it


def build_program(stop=None, dbg=False, tiles=None):
    nc = bass.Bass("TRN2", target_bir_lowering=False)
    P = Prog()

    def din(name, shape, dt=F32):
        return nc.dram_tensor(name, shape, dt, kind="ExternalInput").ap()

    def dscr(name, shape, dt=BF16):
        return nc.dram_tensor(name, shape, dt, kind="ExternalOutput" if dbg else "Internal").ap()

    xe = din("xe", [EXT, D])
    wsrc = {"w_in": din("w_in", [D, NIN]), "w_attn": din("w_attn", [2048, D]),
            "w_ssm": din("w_ssm", [4096, D]), "w_out": din("w_out", [D, D])}
    params = din("params", [128, NPAR])
    biasA = din("biasA", [16, 128, 768])
    bias0 = din("bias0", [16, 128, 768])
    out = nc.dram_tensor("out", [TOKM, D], F32, kind="ExternalOutput").ap()

    wb = dscr("wb", [NG, 128, 16 * 512])
    Qs = dscr("Qs", [16, 128, TOKM])
    Ks = dscr("Ks", [16, 128, 6144])
    Vs = dscr("Vs", [6144, 2048])
    SZa = dscr("SZa", [16, 128, TOKM])
    SZs = dscr("SZs", [TOKM, 4096])
    HnT = dscr("HnT", [16, 128, TOKM])
    Xtok = dscr("Xtok", [EXT, 5120])
    BTs = dscr("BTs", [8, 128, TOKM])
    CTs = dscr("CTs", [8, 128, TOKM])
    DTs = dscr("DTs", [EXT, 64], F32)
    DAs = dscr("DAs", [EXT, 64], F32)
    Yn = dscr("Yn", [TOKM, 4096])
    OaT = dscr("OaT", [16, 128, TOKM])

    with ExitStack() as es:
        sb = lambda name, shape, dt=F32: es.enter_context(nc.sbuf_tensor(name, shape, dt))
        par = sb("par", [128, NPAR])
        idb = sb("idb", [128, 128], BF16)
        oneb = sb("oneb", [128, 128], BF16)
        anegt = sb("anegt", [128, 64])
        P.add("sp", lambda h: h.dma_start(out=par[:], in_=params[:, :]), writes=["par"], dma=True)
        P.add("dve", lambda h: h.tensor_copy(out=idb[:], in_=par[:, P_ID:P_ID + 128]), reads=["par"], writes=["idb"])
        P.add("dve", lambda h: h.tensor_copy(out=oneb[:], in_=par[:, P_ONE:P_ONE + 128]), reads=["par"], writes=["oneb"])
        P.add("act", lambda h: h.activation(out=anegt[:], in_=par[:, P_ALOG:P_ALOG + 64], func=AF.Exp), reads=["par"], writes=["aneg0"])
        P.add("dve", lambda h: h.tensor_scalar(out=anegt[:], in0=anegt[:], scalar1=-1.0, scalar2=None, op0=ALU.mult), reads=["aneg0"], writes=["aneg"])
        Tm = par[:, P_TM:P_TM + 128]
        Um = par[:, P_UM:P_UM + 128]
        ones32 = par[:, P_ONE:P_ONE + 128]

        with ExitStack() as e0:
            s0 = lambda name, shape, dt=F32: e0.enter_context(nc.sbuf_tensor(name, shape, dt))
            wf = [s0("wf%d" % i, [128, 16, 512]) for i in range(2)]
            wc = [s0("wc%d" % i, [128, 16, 512], BF16) for i in range(2)]
            cast_engs = ["act", "dve"]
            for g in ([] if DBG_SKIP_P0 else P0_GROUPS):
                name, src, row0, col0, width = WG[g]
                sl = g % 2
                srcap = wsrc[src][row0:row0 + 2048, col0:col0 + width].rearrange("(kc p) c -> p kc c", p=128)
                for hlf in range(2):
                    P.add("sp", lambda h, sl=sl, srcap=srcap, hlf=hlf, width=width: h.dma_start(
                        out=wf[sl][:, hlf * 8:(hlf + 1) * 8, 0:width], in_=srcap[:, hlf * 8:(hlf + 1) * 8, :]),
                        writes=[("wf", sl, hlf)], dma=True)
                for hlf in range(2):
                    ce = cast_engs[(g + hlf) % 2]
                    if ce == "act":
                        fn = lambda h, sl=sl, hlf=hlf, width=width: h.copy(out=wc[sl][:, hlf * 8:(hlf + 1) * 8, 0:width], in_=wf[sl][:, hlf * 8:(hlf + 1) * 8, 0:width])
                    else:
                        fn = lambda h, sl=sl, hlf=hlf, width=width: h.tensor_copy(out=wc[sl][:, hlf * 8:(hlf + 1) * 8, 0:width], in_=wf[sl][:, hlf * 8:(hlf + 1) * 8, 0:width])
                    P.add(ce, fn, reads=[("wf", sl, hlf)], writes=[("wc", sl, hlf)])
                dst = wb[g].rearrange("p (kc c) -> p kc c", c=512)
                P.add("pool", lambda h, sl=sl, dst=dst, width=width: h.dma_start(out=dst[:, :, 0:width], in_=wc[sl][:, :, 0:width]),
                      reads=[("wc", sl, 0), ("wc", sl, 1)], writes=[("wb", g)], dma=True)

        if stop == "p0":
            P.emit(nc, es, {})
            return nc
        P.barrier()
        def load_w(wring, g):
            wt, key = wring.next()
            width = WG[g][4]
            src = wb[g].rearrange("p (kc c) -> p kc c", c=512)
            for hlf in range(2):
                P.add("sp", lambda h, wt=wt, src=src, hlf=hlf, width=width: h.dma_start(
                    out=wt[:, hlf * 8:(hlf + 1) * 8, 0:width], in_=src[:, hlf * 8:(hlf + 1) * 8, 0:width]),
                    reads=[("wb", g)] + [("wb", g, q) for q in range(4)], writes=[(key, hlf)], dma=True)
            return wt, [(key, 0), (key, 1)]


        def make_converter(sfn, jobs, tag):
            wfq = Ring([(sfn("cvf%s%d" % (tag, i), [128, 4, 512]), "cvf%s%d" % (tag, i)) for i in range(3)])
            wcq = Ring([(sfn("cvc%s%d" % (tag, i), [128, 4, 512], BF16), "cvc%s%d" % (tag, i)) for i in range(2)])
            st = {"ld": 0, "loaded": []}

            def issue_load():
                if st["ld"] >= len(jobs):
                    return
                g, q = jobs[st["ld"]]
                st["ld"] += 1
                name, src, row0, col0, width = WG[g]
                wf, wfk = wfq.next()
                srcap = wsrc[src][row0 + q * 512:row0 + (q + 1) * 512, col0:col0 + width].rearrange("(kc p) c -> p kc c", p=128)
                P.add("act", lambda h: h.dma_start(out=wf[:, :, 0:width], in_=srcap), writes=[wfk], dma=True)
                st["loaded"].append((g, q, wf, wfk, width))

            def tick():
                issue_load()
                if not st["loaded"]:
                    return False
                if len(st["loaded"]) <= 2 and st["ld"] < len(jobs):
                    return True
                g, q, wf, wfk, width = st["loaded"].pop(0)
                wc, wck = wcq.next()
                P.add("act", lambda h: h.copy(out=wc[:, :, 0:width], in_=wf[:, :, 0:width]), reads=[wfk], writes=[wck])
                dst = wb[g].rearrange("p (kc c) -> p kc c", c=512)
                P.add("act", lambda h: h.dma_start(out=dst[:, 4 * q:4 * q + 4, 0:width], in_=wc[:, :, 0:width]), reads=[wck], writes=[("wb", g, q)], dma=True)
                return True

            def flush():
                while tick():
                    pass
            return tick, flush

        evac_rr = [0]

        def evac_eng():
            evac_rr[0] += 1
            return "act" if evac_rr[0] % 2 else "dve"

        def copy_op(eng, out_ap, in_ap, reads, writes):
            if eng == "act":
                return P.add("act", lambda h: h.copy(out=out_ap, in_=in_ap), reads=reads, writes=writes)
            return P.add(eng, lambda h: h.tensor_copy(out=out_ap, in_=in_ap), reads=reads, writes=writes)

        with ExitStack() as e1:
            s1 = lambda name, shape, dt=F32: e1.enter_context(nc.sbuf_tensor(name, shape, dt))
            p1 = lambda name, shape, dt=F32: e1.enter_context(nc.psum_tensor(name, shape, dt))
            wring = Ring([(s1("w1_%d" % i, [128, 16, 512], BF16), "w1_%d" % i) for i in range(3)])
            xbuf = Ring([(s1("xb%d" % i, [128, 2048]), "xb%d" % i) for i in range(2)])
            hnb = Ring([(s1("hnb%d" % i, [128, 2048], BF16), "hnb%d" % i) for i in range(2)])
            junk = s1("junk", [128, 2048], BF16)
            hnT = Ring([(s1("hnT%d" % i, [128, 16, 512], BF16), "hnT%d" % i) for i in range(2)])
            stat = Ring([(s1("stat%d" % i, [128, 4]), "stat%d" % i) for i in range(2)])
            stage = Ring([(s1("stg%d" % i, [128, 512], BF16), "stg%d" % i) for i in range(4)])
            XR = Ring([(s1("XR%d" % i, [128, 515]), "XR%d" % i) for i in range(2)])
            cacc = Ring([(s1("cacc%d" % i, [128, 512]), "cacc%d" % i) for i in range(2)])
            tail = s1("tail", [128, 48, 3])
            xcT = Ring([(s1("xcT%d" % i, [128, 512], BF16), "xcT%d" % i) for i in range(10)])
            xtk = Ring([(s1("xtk%d" % i, [128, 4, 512], BF16), "xtk%d" % i) for i in range(2)])
            dtw = Ring([(s1("dtw%d" % i, [128, 6, 256]), "dtw%d" % i) for i in range(2)])
            psb = Ring([(p1("ps1_%d" % i, [128, 512]), "ps1_%d" % i) for i in range(5)])
            ptr = Ring([(p1("pt1_%d" % i, [128, 1024], BF16), "pt1_%d" % i) for i in range(3)])

            P.add("dve", lambda h: h.memset(tail[:], 0.0), writes=[("tail", c) for c in range(48)])
            cv_tick, cv_flush = make_converter(s1, [] if DBG_SKIP_P0 else P1A_JOBS, "a")

            def emit_hn(ti):
                t0 = ti * T
                tm0 = t0 - 4096
                hT, hTk = hnT.next()
                for blk in range(4):
                    xb, xk = xbuf.next()
                    hb, hk = hnb.next()
                    stt, sk = stat.next()
                    r0 = t0 + blk * 128
                    P.add("sp", lambda h, xb=xb, r0=r0: h.dma_start(out=xb[:], in_=xe[r0:r0 + 128, :]), writes=[xk], dma=True)
                    P.add("act", lambda h, xb=xb, stt=stt: h.activation(out=junk[:], in_=xb[:], func=AF.Square, accum_out=stt[:, 0:1]),
                          reads=[xk], writes=["junk", (sk, 0)])
                    P.add("dve", lambda h, stt=stt: h.tensor_scalar(out=stt[:, 1:2], in0=stt[:, 0:1], scalar1=1.0 / D, scalar2=EPS, op0=ALU.mult, op1=ALU.add),
                          reads=[(sk, 0)], writes=[(sk, 1)])
                    P.add("act", lambda h, stt=stt: h.activation(out=stt[:, 2:3], in_=stt[:, 1:2], func=AF.Sqrt), reads=[(sk, 1)], writes=[(sk, 2)])
                    P.add("dve", lambda h, stt=stt: h.reciprocal(out=stt[:, 3:4], in_=stt[:, 2:3]), reads=[(sk, 2)], writes=[(sk, 3)])
                    P.add("dve", lambda h, xb=xb, hb=hb, stt=stt: h.scalar_tensor_tensor(
                        out=hb[:], in0=xb[:], scalar=stt[:, 3:4], in1=par[:, P_NORMW:P_NORMW + 2048], op0=ALU.mult, op1=ALU.mult),
                        reads=[xk, (sk, 3), "par"], writes=[hk])
                    for q4 in range(4):
                        pt, pk = ptr.next()
                        def tr(h, pt=pt, hb=hb, q4=q4):
                            for j in range(4):
                                kc = q4 * 4 + j
                                ins = h.transpose(out=pt[:, j * 128:(j + 1) * 128], in_=hb[:, kc * 128:(kc + 1) * 128], identity=idb[:])
                            return ins
                        P.add("pe", tr, reads=[hk, "idb"], writes=[pk])
                        copy_op(evac_eng(), hT[:, q4 * 4:(q4 + 1) * 4, blk * 128:(blk + 1) * 128],
                                pt[:, 0:512].rearrange("p (j t) -> p j t", j=4), [pk], [(hTk, blk, q4)])
                hT_keys = [(hTk, b, q) for b in range(4) for q in range(4)]
                if ti >= 8:
                    P.add("pool", lambda h, hT=hT, tm0=tm0: h.dma_start(out=HnT[:, :, tm0:tm0 + T].rearrange("k p t -> p k t"), in_=hT[:]),
                          reads=hT_keys, writes=[("HnT", ti)], dma=True)
                return hT, hT_keys

            tile_list = list(tiles if tiles is not None else range(NT))
            hn_ready = {}
            if tile_list:
                hn_ready[tile_list[0]] = emit_hn(tile_list[0])
            for tidx, ti in enumerate(tile_list):
                main = ti >= 8
                halo = ti >= 4
                t0 = ti * T
                tm0 = t0 - 4096
                tk0 = t0 - HALO0
                hT, hT_keys = hn_ready.pop(ti)
                def fm_group(g, consume):
                    cv_tick()
                    wt, wkeys = load_w(wring, g)
                    for cb in range(4):
                        ps, pk = psb.next()
                        def mm(h, wt=wt, ps=ps, cb=cb, hT=hT):
                            for kc in range(16):
                                ins = h.matmul(ps[:], wt[:, kc, cb * 128:(cb + 1) * 128], hT[:, kc, :], start=(kc == 0), stop=(kc == 15))
                            return ins
                        P.add("pe", mm, reads=wkeys + hT_keys, writes=[pk])
                        consume(cb, ps, pk)

                def tm_group(g, consume, width=512):
                    cv_tick()
                    wt, wkeys = load_w(wring, g)
                    for blk in range(4):
                        ps, pk = psb.next()
                        def mm(h, wt=wt, ps=ps, blk=blk, hT=hT):
                            for kc in range(16):
                                ins = h.matmul(ps[:, 0:width], hT[:, kc, blk * 128:(blk + 1) * 128], wt[:, kc, 0:width], start=(kc == 0), stop=(kc == 15))
                            return ins
                        P.add("pe", mm, reads=wkeys + hT_keys, writes=[pk])
                        consume(blk, ps, pk)

                def store_fm(dst3, hidx_base, toff, func=None, nm=None):
                    def consume(cb, ps, pk):
                        sg, sgk = stage.next()
                        if func is None:
                            copy_op(evac_eng(), sg[:], ps[:], [pk], [sgk])
                        else:
                            P.add("act", lambda h: h.activation(out=sg[:], in_=ps[:], func=func), reads=[pk], writes=[sgk])
                        hidx = hidx_base + cb
                        P.add("pool", lambda h: h.dma_start(out=dst3[hidx, :, toff:toff + T], in_=sg[:]), reads=[sgk], writes=[(nm, ti, hidx)], dma=True)
                    return consume

                def store_tm(dst2, col0, roff, func=None, nm=None):
                    def consume(blk, ps, pk):
                        sg, sgk = stage.next()
                        if func is None:
                            copy_op(evac_eng(), sg[:], ps[:], [pk], [sgk])
                        else:
                            P.add("act", lambda h: h.activation(out=sg[:], in_=ps[:], func=func), reads=[pk], writes=[sgk])
                        r0 = roff + blk * 128
                        P.add("pool", lambda h: h.dma_start(out=dst2[r0:r0 + 128, col0:col0 + 512], in_=sg[:]), reads=[sgk], writes=[(nm, ti, col0 // 512, blk)], dma=True)
                    return consume

                if main:
                    for i in range(4):
                        fm_group(G_Q[i], store_fm(Qs, 4 * i, tm0, nm="Qs"))
                if halo:
                    for i in range(4):
                        fm_group(G_K[i], store_fm(Ks, 4 * i, tk0, nm="Ks"))
                    for i in range(4):
                        tm_group(G_V[i], store_tm(Vs, 512 * i, tk0, nm="Vs"))
                if main:
                    for i in range(4):
                        fm_group(G_ZA[i], store_fm(SZa, 4 * i, tm0, AF.Silu, nm="SZa"))
                    for i in range(8):
                        tm_group(G_ZS[i], store_tm(SZs, 512 * i, tm0, AF.Silu, nm="SZs"))

                pend_tr = None
                pend_silu = []
                for i in range(12):
                    xcs = []
                    def consume(cb, ps, pk, i=i, xcs=xcs):
                        cbi = 4 * i + cb
                        xr, xrk = XR.next()
                        ca, cak = cacc.next()
                        xc, xck = xcT.next()
                        P.add("act", lambda h: h.copy(out=xr[:, 3:515], in_=ps[:]), reads=[pk], writes=[(xrk, 1)])
                        while pend_silu:
                            pend_silu.pop(0)()
                        P.add("dve", lambda h: h.tensor_copy(out=xr[:, 0:3], in_=tail[:, cbi, :]), reads=[("tail", cbi)], writes=[(xrk, 0)])
                        P.add("dve", lambda h: h.tensor_copy(out=tail[:, cbi, :], in_=xr[:, 512:515]), reads=[(xrk, 1), (xrk, 0)], writes=[("tail", cbi)])
                        cw = lambda k: par[:, P_CW + cbi * 4 + k:P_CW + cbi * 4 + k + 1]
                        P.add("dve", lambda h: h.tensor_scalar(out=ca[:], in0=xr[:, 3:515], scalar1=cw(3), scalar2=par[:, P_CB + cbi:P_CB + cbi + 1],
                                                               op0=ALU.mult, op1=ALU.add), reads=[(xrk, 1), (xrk, 0), "par"], writes=[cak])
                        for k in (2, 1, 0):
                            P.add("dve", lambda h, k=k: h.scalar_tensor_tensor(out=ca[:], in0=xr[:, k:k + 512], scalar=cw(k), in1=ca[:],
                                                                                op0=ALU.mult, op1=ALU.add), reads=[cak, (xrk, 1), (xrk, 0)], writes=[cak])
                        def do_silu(xc=xc, ca=ca, cak=cak, xck=xck):
                            P.add("act", lambda h: h.activation(out=xc[:], in_=ca[:], func=AF.Silu), reads=[cak], writes=[xck])
                        pend_silu.append(do_silu)
                        xcs.append((xc, xck))
                        if i >= 8 and main:
                            gi = cbi - 32 if i < 10 else cbi - 40
                            dst = BTs if i < 10 else CTs
                            def do_store(xc=xc, xck=xck, dst=dst, gi=gi, tm0=tm0, ti=ti, cbi=cbi):
                                P.add("pool", lambda h: h.dma_start(out=dst[gi, :, tm0:tm0 + T], in_=xc[:]), reads=[xck], writes=[("BC", ti, cbi)], dma=True)
                            pend_silu.append(do_store)
                    fm_group(G_XBC[i], consume)
                    if pend_tr is not None:
                        pend_tr()
                        pend_tr = None
                    if i < 10:
                        def do_tr(i=i, xcs=xcs, t0=t0):
                            xt_, xtkk = xtk.next()
                            for ch in range(4):
                                pt, pk = ptr.next()
                                def tr(h, pt=pt, ch=ch, xcs=xcs):
                                    for j in range(4):
                                        ins = h.transpose(out=pt[:, j * 128:(j + 1) * 128], in_=xcs[j][0][:, ch * 128:(ch + 1) * 128], identity=idb[:])
                                    return ins
                                P.add("pe", tr, reads=[k for _, k in xcs] + ["idb"], writes=[pk])
                                copy_op(evac_eng(), xt_[:, ch, :], pt[:, 0:512], [pk], [(xtkk, ch)])
                            P.add("pool", lambda h, xt_=xt_, i=i, t0=t0: h.dma_start(
                                out=Xtok[t0:t0 + T, 512 * i:512 * (i + 1)].rearrange("(c p) f -> p c f", p=128), in_=xt_[:]),
                                reads=[(xtkk, c) for c in range(4)], writes=[("Xtok", ti, i)], dma=True)
                        pend_tr = do_tr
                    if i == 5 and tidx + 1 < len(tile_list):
                        hn_ready[tile_list[tidx + 1]] = emit_hn(tile_list[tidx + 1])
                    pass
                while pend_silu:
                    pend_silu.pop(0)()
                if pend_tr is not None:
                    pend_tr()
                    pend_tr = None

                dw, dwk = dtw.next()
                wt, wkeys = load_w(wring, G_DT)
                ps, pk = psb.next()
                def mmdt(h, wt=wt, ps=ps, hT=hT):
                    for blk in range(4):
                        for kc in range(16):
                            ins = h.matmul(ps[:, blk * 64:(blk + 1) * 64], hT[:, kc, blk * 128:(blk + 1) * 128], wt[:, kc, 0:64], start=(kc == 0), stop=(kc == 15))
                    return ins
                P.add("pe", mmdt, reads=wkeys + hT_keys, writes=[pk])
                dtb_bc = par[:, P_DTB:P_DTB + 64].unsqueeze(1).to_broadcast([128, 4, 64])
                an_bc = anegt[:, :].unsqueeze(1).to_broadcast([128, 4, 64])
                v3 = lambda a: a.rearrange("p (b j) -> p b j", b=4)
                z, nz, mn, ex, dtv, dav = (dw[:, i, :] for i in range(6))
                P.add("dve", lambda h, ps=ps, z=z: h.tensor_tensor(out=v3(z), in0=v3(ps[:, 0:256]), in1=dtb_bc, op=ALU.add), reads=[pk, "par"], writes=[(dwk, 0)])
                P.add("dve", lambda h, z=z, nz=nz: h.tensor_scalar(out=nz, in0=z, scalar1=-1.0, scalar2=None, op0=ALU.mult), reads=[(dwk, 0)], writes=[(dwk, 1)])
                P.add("dve", lambda h, z=z, nz=nz, mn=mn: h.tensor_tensor(out=mn, in0=z, in1=nz, op=ALU.min), reads=[(dwk, 0), (dwk, 1)], writes=[(dwk, 2)])
                P.add("act", lambda h, mn=mn, ex=ex: h.activation(out=ex, in_=mn, func=AF.Exp), reads=[(dwk, 2)], writes=[(dwk, 3)])
                P.add("act", lambda h, ex=ex: h.activation(out=ex, in_=ex, func=AF.Ln, bias=1.0), reads=[(dwk, 3)], writes=[(dwk, 3)])
                P.add("dve", lambda h, z=z, nz=nz: h.tensor_scalar(out=nz, in0=z, scalar1=0.0, scalar2=None, op0=ALU.max), reads=[(dwk, 0), (dwk, 2)], writes=[(dwk, 1)])
                P.add("dve", lambda h, nz=nz, ex=ex, dtv=dtv: h.tensor_tensor(out=dtv, in0=nz, in1=ex, op=ALU.add), reads=[(dwk, 1), (dwk, 3)], writes=[(dwk, 4)])
                P.add("dve", lambda h, dtv=dtv, dav=dav: h.tensor_tensor(out=v3(dav), in0=v3(dtv), in1=an_bc, op=ALU.mult), reads=[(dwk, 4), "aneg"], writes=[(dwk, 5)])
                P.add("pool", lambda h, dtv=dtv, t0=t0: h.dma_start(out=DTs[t0:t0 + T, :].rearrange("(b p) j -> p b j", p=128), in_=v3(dtv)), reads=[(dwk, 4)], writes=[("DT", ti)], dma=True)
                P.add("pool", lambda h, dav=dav, t0=t0: h.dma_start(out=DAs[t0:t0 + T, :].rearrange("(b p) j -> p b j", p=128), in_=v3(dav)), reads=[(dwk, 5)], writes=[("DA", ti)], dma=True)
            cv_flush()


        if stop == "p1a":
            P.emit(nc, es, {})
            return nc
        P.barrier()
        INV = 1.0 / float(np.sqrt(128.0))
        with ExitStack() as e1:
            s1 = lambda name, shape, dt=F32: e1.enter_context(nc.sbuf_tensor(name, shape, dt))
            p1 = lambda name, shape, dt=F32: e1.enter_context(nc.psum_tensor(name, shape, dt))
            xtr = Ring([(s1("bxt%d" % i, [128, 5120], BF16), "bxt%d" % i) for i in range(3)])
            ddr = Ring([(s1("bdd%d" % i, [128, 2, 64]), "bdd%d" % i) for i in range(3)])
            bcr = Ring([(s1("bbc%d" % i, [128, 2, 8, 128], BF16), "bbc%d" % i) for i in range(3)])
            szr = Ring([(s1("bsz%d" % i, [128, 4096], BF16), "bsz%d" % i) for i in range(2)])
            smr = Ring([(s1("bsm%d" % i, [128, 5, 64]), "bsm%d" % i) for i in range(3)])
            xdr = Ring([(s1("bxd%d" % i, [128, 4096], BF16), "bxd%d" % i) for i in range(2)])
            xwr = Ring([(s1("bxw%d" % i, [128, 4096], BF16), "bxw%d" % i) for i in range(2)])
            cbr = Ring([(s1("bcb%d" % i, [128, 8, 128], BF16), "bcb%d" % i) for i in range(2)])
            dur = Ring([(s1("bdu%d" % i, [128, 8, 128]), "bdu%d" % i) for i in range(2)])
            der = Ring([(s1("bde%d" % i, [128, 1024], BF16), "bde%d" % i) for i in range(2)])
            mtr = Ring([(s1("bmt%d" % i, [128, 8, 128], BF16), "bmt%d" % i) for i in range(2)])
            t1r = Ring([(s1("bt1%d" % i, [128, 512]), "bt1%d" % i) for i in range(2)])
            t2r = Ring([(s1("bt2%d" % i, [128, 512]), "bt2%d" % i) for i in range(2)])
            t3r = Ring([(s1("bt3%d" % i, [128, 512]), "bt3%d" % i) for i in range(2)])
            ygf = s1("bygf", [128, 4096])
            ynr = Ring([(s1("byn%d" % i, [128, 4096], BF16), "byn%d" % i) for i in range(1)])
            ssr = Ring([(s1("bss%d" % i, [128, 12]), "bss%d" % i) for i in range(2)])
            jk = s1("bjunk", [128, 512], BF16)
            stt = s1("bst", [128, 8, 512])
            stb = s1("bstb", [128, 8, 512], BF16)
            pa = p1("bpa", [128, 512])
            pcbr = Ring([(p1("bpcb%d" % i, [128, 512]), "bpcb%d" % i) for i in range(2)])
            pseg = p1("bpseg", [128, 1024])
            py = p1("bpy", [128, 512])
            po = p1("bpo", [128, 512])
            pst = p1("bpst", [128, 512])

            P.add("dve", lambda h: h.memset(stt[:], 0.0), writes=[("st", g) for g in range(8)])
            P.add("pool", lambda h: h.memset(stb[:], 0.0), writes=[("stb", g) for g in range(8)])
            dsk = par[:, P_DSK:P_DSK + 64]
            v8 = lambda a: a.rearrange("p (j q) -> p j q", j=8)

            b64 = lambda a: a.unsqueeze(2).to_broadcast([128, 64, 64])
            v64 = lambda a: a.rearrange("p (j q) -> p j q", j=64)
            b8 = lambda a: a.unsqueeze(2).to_broadcast([128, 8, 64])

            def header(ci):
                c = dict(ci=ci, main=ci >= 32, ti=ci // 4, r0=ci * 128, rm0=ci * 128 - 4096)
                ti, r0, rm0 = c["ti"], c["r0"], c["rm0"]
                xt, xk = xtr.next(); dd, dk = ddr.next(); sm, smk = smr.next(); xd, xdk = xdr.next(); xw, xwk = xwr.next()
                c.update(xt=xt, xk=xk, dd=dd, dk=dk, sm=sm, smk=smk, xd=xd, xdk=xdk, xw=xw, xwk=xwk)
                P.add("sp", lambda h: h.dma_start(out=xt[:], in_=Xtok[r0:r0 + 128, :]), reads=[("Xtok", ti, i) for i in range(10)], writes=[xk], dma=True)
                P.add("sp", lambda h: h.dma_start(out=dd[:, 0, :], in_=DTs[r0:r0 + 128, :]), reads=[("DT", ti)], writes=[(dk, 0)], dma=True)
                P.add("sp", lambda h: h.dma_start(out=dd[:, 1, :], in_=DAs[r0:r0 + 128, :]), reads=[("DA", ti)], writes=[(dk, 1)], dma=True)
                dtt = dd[:, 0, :]; dat = dd[:, 1, :]
                c.update(dtt=dtt, dat=dat)
                if c["main"]:
                    bc, bck = bcr.next(); sz, szk = szr.next()
                    c.update(bc=bc, bck=bck, sz=sz, szk=szk)
                    P.add("sp", lambda h: h.dma_start(out=bc[:, 0, :, :], in_=BTs[:, :, rm0:rm0 + 128].rearrange("g n t -> n g t")),
                          reads=[("BC", ti, cc) for cc in range(32, 40)], writes=[(bck, 0)], dma=True)
                    P.add("sp", lambda h: h.dma_start(out=bc[:, 1, :, :], in_=CTs[:, :, rm0:rm0 + 128].rearrange("g n t -> n g t")),
                          reads=[("BC", ti, cc) for cc in range(40, 48)], writes=[(bck, 1)], dma=True)
                    P.add("sp", lambda h: h.dma_start(out=sz[:], in_=SZs[rm0:rm0 + 128, :]),
                          reads=[("SZs", ti, i, ci % 4) for i in range(8)], writes=[szk], dma=True)
                def mm_ac(h):
                    h.matmul(pa[:, 0:64], Tm, dat, start=True, stop=True)
                    return h.matmul(pa[:, 64:128], ones32, dat, start=True, stop=True)
                P.add("pe", mm_ac, reads=[(dk, 1), "par"], writes=["pa"])
                acs, eac, ela, dws, wsv = (sm[:, i, :] for i in range(5))
                c.update(eac=eac, ela=ela)
                P.add("dve", lambda h: h.tensor_copy(out=acs, in_=pa[:, 0:64]), reads=["pa"], writes=[(smk, 0), "pa"])
                P.add("act", lambda h: h.activation(out=eac, in_=acs, func=AF.Exp), reads=[(smk, 0)], writes=[(smk, 1)])
                P.add("act", lambda h: h.activation(out=ela, in_=pa[:, 64:128], func=AF.Exp), reads=["pa"], writes=[(smk, 2), "pa"])
                P.add("dve", lambda h: h.tensor_tensor(out=dws, in0=pa[:, 64:128], in1=acs, op=ALU.subtract), reads=["pa", (smk, 0)], writes=[(smk, 3), "pa"])
                P.add("act", lambda h: h.activation(out=wsv, in_=dws, func=AF.Exp), reads=[(smk, 3)], writes=[(smk, 4)])
                if c["main"]:
                    P.add("dve", lambda h: h.tensor_tensor(out=v64(xd[:]), in0=v64(xt[:, 0:4096]), in1=b64(dtt), op=ALU.mult), reads=[xk, (dk, 0)], writes=[xdk])
                P.add("dve", lambda h: h.tensor_tensor(out=wsv, in0=wsv, in1=dtt, op=ALU.mult), reads=[(smk, 4), (dk, 0)], writes=[(smk, 4)])
                P.add("dve", lambda h: h.tensor_tensor(out=v64(xw[:]), in0=v64(xt[:, 0:4096]), in1=b64(wsv), op=ALU.mult), reads=[xk, (smk, 4)], writes=[xwk])
                if c["main"]:
                    c["ss"], c["ssk"] = ssr.next()
                    cb8, cb8k = cbr.next()
                    c.update(cb8=cb8, cb8k=cb8k)
                    for half in range(2):
                        pcb, pcbk = pcbr.next()
                        def mm_cb(h, pcb=pcb, half=half):
                            for gg in range(4):
                                g = half * 4 + gg
                                ins = h.matmul(pcb[:, gg * 128:(gg + 1) * 128], bc[:, 0, g, :], bc[:, 1, g, :], start=True, stop=True)
                            return ins
                        P.add("pe", mm_cb, reads=[(bck, 0), (bck, 1)], writes=[pcbk])
                        P.add("dve", lambda h, pcb=pcb, half=half: h.tensor_tensor(
                            out=cb8[:, half * 4:(half + 1) * 4, :], in0=pcb[:].rearrange("p (g l) -> p g l", g=4),
                            in1=Tm.unsqueeze(1).to_broadcast([128, 4, 128]), op=ALU.mult), reads=[pcbk, "par"], writes=[(cb8k, half)])
                return c

            def stageA(c, g):
                bc, bck, dat, dk = c["bc"], c["bck"], c["dat"], c["dk"]
                du, duk = dur.next(); de, dek = der.next()
                cbm = c["cb8"][:, g, :]; cbk = (c["cb8k"], g // 4)
                P.add("dve", lambda h: h.tensor_tensor(out=du[:, 0:5, :], in0=Um.unsqueeze(1).to_broadcast([128, 5, 128]),
                                                       in1=dat[:, g * 8:g * 8 + 5].unsqueeze(2).to_broadcast([128, 5, 128]), op=ALU.mult),
                      reads=[(dk, 1), "par"], writes=[(duk, "a")])
                for j in range(5, 8):
                    P.add("act", lambda h, j=j: h.activation(out=du[:, j, :], in_=Um, func=AF.Copy, scale=dat[:, g * 8 + j:g * 8 + j + 1]),
                          reads=[(dk, 1), "par"], writes=[(duk, j)])
                def mm_seg(h):
                    for j in range(8):
                        ins = h.matmul(pseg[:, j * 128:(j + 1) * 128], du[:, j, :], Tm, start=True, stop=True)
                    return ins
                P.add("pe", mm_seg, reads=[(duk, "a"), (duk, 5), (duk, 6), (duk, 7), "par"], writes=["pseg"])
                P.add("act", lambda h: h.activation(out=de[:], in_=pseg[:], func=AF.Exp), reads=["pseg"], writes=[dek])
                return dict(cbm=cbm, cbk=cbk, de=de, dek=dek)

            def stageB1(c, gc, g):
                bc, bck, xd, xdk = c["bc"], c["bck"], c["xd"], c["xdk"]
                cbm, cbk, de, dek = gc["cbm"], gc["cbk"], gc["de"], gc["dek"]
                mt, mtk = mtr.next()
                P.add("dve", lambda h: h.tensor_tensor(out=mt[:], in0=de[:].rearrange("p (j l) -> p j l", j=8), in1=cbm.unsqueeze(1).to_broadcast([128, 8, 128]), op=ALU.mult),
                      reads=[dek, cbk], writes=[mtk])
                def mm_y(h):
                    for j in range(8):
                        c0 = (g * 8 + j) * 64
                        ins = h.matmul(py[:, j * 64:(j + 1) * 64], mt[:, j, :], xd[:, c0:c0 + 64], start=True, stop=True)
                    return ins
                P.add("pe", lambda h: h.matmul(po[:], bc[:, 1, g, :], stb[:, g, :], start=True, stop=True), reads=[(bck, 1), ("stb", g)], writes=["po"])
                P.add("pe", mm_y, reads=[mtk, xdk], writes=["py"])

            def stageB2(c, g):
                xt, xk, sz, szk = c["xt"], c["xk"], c["sz"], c["szk"]
                eac, smk, ss, ssk = c["eac"], c["smk"], c["ss"], c["ssk"]
                t1, t1k = t1r.next(); t2, t2k = t2r.next(); t3, t3k = t3r.next()
                gs = slice(g * 512, (g + 1) * 512)
                P.add("dve", lambda h: h.tensor_tensor(out=v8(t1[:]), in0=v8(po[:]), in1=b8(eac[:, g * 8:(g + 1) * 8]), op=ALU.mult), reads=["po", (smk, 1)], writes=[t1k])
                P.add("pool", lambda h: h.tensor_tensor(out=v8(t3[:]), in0=v8(xt[:, gs]), in1=b8(dsk[:, g * 8:(g + 1) * 8]), op=ALU.mult), reads=[xk, "par"], writes=[t3k])
                P.add("dve", lambda h: h.tensor_tensor(out=t2[:], in0=py[:], in1=t1[:], op=ALU.add), reads=["py", t1k], writes=[t2k])
                P.add("pool", lambda h: h.tensor_tensor(out=t3[:], in0=t2[:], in1=t3[:], op=ALU.add), reads=[t2k, t3k], writes=[t3k])
                P.add("pool", lambda h: h.tensor_tensor(out=ygf[:, gs], in0=t3[:], in1=sz[:, gs], op=ALU.mult), reads=[t3k, szk], writes=[("ygf", g)])
                P.add("act", lambda h: h.activation(out=jk[:], in_=ygf[:, gs], func=AF.Square, accum_out=ss[:, g:g + 1]), reads=[("ygf", g)], writes=["bjunk", (ssk, g)])

            def finish(c):
                ss, ssk, rm0, ci = c["ss"], c["ssk"], c["rm0"], c["ci"]
                yn, ynk = ynr.next()
                P.add("dve", lambda h: h.tensor_reduce(out=ss[:, 8:9], in_=ss[:, 0:8], axis=mybir.AxisListType.X, op=ALU.add), reads=[(ssk, g) for g in range(8)], writes=[(ssk, 8)])
                P.add("dve", lambda h: h.tensor_scalar(out=ss[:, 9:10], in0=ss[:, 8:9], scalar1=1.0 / 4096, scalar2=EPS, op0=ALU.mult, op1=ALU.add), reads=[(ssk, 8)], writes=[(ssk, 9)])
                P.add("act", lambda h: h.activation(out=ss[:, 10:11], in_=ss[:, 9:10], func=AF.Sqrt), reads=[(ssk, 9)], writes=[(ssk, 10)])
                P.add("dve", lambda h: h.reciprocal(out=ss[:, 11:12], in_=ss[:, 10:11]), reads=[(ssk, 10)], writes=[(ssk, 11)])
                P.add("act", lambda h: h.activation(out=yn[:], in_=ygf[:], func=AF.Copy, scale=ss[:, 11:12]), reads=[("ygf", g) for g in range(8)] + [(ssk, 11)], writes=[ynk])
                P.add("pool", lambda h: h.dma_start(out=Yn[rm0:rm0 + 128, :], in_=yn[:]), reads=[ynk], writes=[("Yn", ci)], dma=True)

            def state_update(c):
                xt, xk, xw, xwk, ela, smk, ci = c["xt"], c["xk"], c["xw"], c["xwk"], c["ela"], c["smk"], c["ci"]
                for g in range(DBG_SU):
                    gs = slice(g * 512, (g + 1) * 512)
                    P.add("pe", lambda h, g=g, gs=gs: h.matmul(pst[:], xt[:, 4096 + g * 128:4096 + (g + 1) * 128], xw[:, gs], start=True, stop=True), reads=[xk, xwk], writes=["pst"])
                    P.add("dve", lambda h, g=g: h.tensor_tensor(out=v8(stt[:, g, :]), in0=v8(stt[:, g, :]), in1=b8(ela[:, g * 8:(g + 1) * 8]), op=ALU.mult), reads=[("st", g), (smk, 2)], writes=[("st", g)])
                    P.add("dve", lambda h, g=g: h.tensor_tensor(out=stt[:, g, :], in0=pst[:], in1=stt[:, g, :], op=ALU.add), reads=["pst", ("st", g)], writes=[("st", g)])
                    if ci == 31:
                        P.add("dve", lambda h, g=g: h.tensor_scalar(out=stt[:, g, :], in0=stt[:, g, :], scalar1=par[:, P_PV:P_PV + 1], scalar2=None, op0=ALU.mult), reads=[("st", g), "par"], writes=[("st", g)])
                    if ci >= 31 and ci < 63:
                        P.add("act", lambda h, g=g: h.copy(out=stb[:, g, :], in_=stt[:, g, :]), reads=[("st", g)], writes=[("stb", g)])

            chunks = list(DBG_CHUNKS if DBG_CHUNKS is not None else range(64))
            ctxs = {}
            if chunks:
                ctxs[0] = header(chunks[0])
            for idx, ci in enumerate(chunks):
                c = ctxs.pop(idx)
                if idx + 1 < len(chunks):
                    ctxs[idx + 1] = header(chunks[idx + 1])
                if c["main"]:
                    gcs = {0: stageA(c, 0)}
                    for g in range(8):
                        if g + 1 < 8:
                            gcs[g + 1] = stageA(c, g + 1)
                        if g >= 1:
                            stageB2(c, g - 1)
                        stageB1(c, gcs.pop(g), g)
                    stageB2(c, 7)
                    state_update(c)
                    finish(c)
                else:
                    state_update(c)

        if stop == "p1b":
            P.emit(nc, es, {})
            return nc
        P.barrier()
        with ExitStack() as e2:
            s2 = lambda name, shape, dt=F32: e2.enter_context(nc.sbuf_tensor(name, shape, dt))
            p2 = lambda name, shape, dt=F32: e2.enter_context(nc.psum_tensor(name, shape, dt))
            qtr = Ring([(s2("cq%d" % i, [128, 2048], BF16), "cq%d" % i) for i in range(2)])
            ktr = Ring([(s2("ck%d" % i, [128, 4096], BF16), "ck%d" % i) for i in range(2)])
            zar = Ring([(s2("cz%d" % i, [128, 2048], BF16), "cz%d" % i) for i in range(2)])
            vpr = Ring([(s2("cv%d" % i, [128, 3, 32, 128], BF16), "cv%d" % i) for i in range(2)])
            bir = Ring([(s2("cb%d" % i, [128, 2, 768]), "cb%d" % i) for i in range(1)])
            nar = Ring([(s2("cn%d" % i, [128, 2048]), "cn%d" % i) for i in range(2)])
            dar = Ring([(s2("cd%d" % i, [128, 2048]), "cd%d" % i) for i in range(2)])
            ssb = Ring([(s2("cs%d" % i, [128, 1024], BF16), "cs%d" % i) for i in range(2)])
            bbr = Ring([(s2("cbb%d" % i, [128, 2, 768], BF16), "cbb%d" % i) for i in range(2)])
            ptb = Ring([(s2("cp%d" % i, [128, 1024], BF16), "cp%d" % i) for i in range(2)])
            oar = Ring([(s2("co%d" % i, [128, 2048], BF16), "co%d" % i) for i in range(2)])
            pSr = Ring([(p2("cpS%d" % i, [128, 1024]), "cpS%d" % i) for i in range(2)])
            pOr = Ring([(p2("cpO%d" % i, [128, 512]), "cpO%d" % i) for i in range(2)])
            pDr = Ring([(p2("cpD%d" % i, [128, 512]), "cpD%d" % i) for i in range(2)])

            cv2_tick, cv2_flush = make_converter(s2, [] if DBG_SKIP_P0 else P2_JOBS, "b")

            def strided(ap, start, step, n=128):
                return ap[:, start:start + step * (n - 1) + 1:step]

            def do_head(st_, hd, w0, tiles_q, tiles_k):
                qt, qk = qtr.next(); kt, kk = ktr.next(); za, zk = zar.next(); vp, vk = vpr.next(); bi, bk = bir.next()
                na, nk = nar.next(); da, dak = dar.next(); oa, ok = oar.next()
                P.add("sp", lambda h, qt=qt, hd=hd, w0=w0: h.dma_start(out=qt[:], in_=Qs[hd, :, w0:w0 + 2048]), reads=[("Qs", t, hd) for t in tiles_q], writes=[qk], dma=True)
                P.add("sp", lambda h, kt=kt, hd=hd, w0=w0: h.dma_start(out=kt[:], in_=Ks[hd, :, w0:w0 + 4096]), reads=[("Ks", t, hd) for t in tiles_k], writes=[kk], dma=True)
                P.add("sp", lambda h, za=za, hd=hd, w0=w0: h.dma_start(out=za[:], in_=SZa[hd, :, w0:w0 + 2048]), reads=[("SZa", t, hd) for t in tiles_q], writes=[zk], dma=True)
                vkeys = [("Vs", t, hd // 4, b) for t in tiles_k for b in range(4)]
                vsrc = Vs[w0:w0 + 4096, hd * 128:(hd + 1) * 128]
                P.add("sp", lambda h, vp=vp, vsrc=vsrc: h.dma_start(out=vp[:, 0, :, :], in_=vsrc.rearrange("(b p) e -> p b e", p=128)), reads=vkeys, writes=[(vk, 0)], dma=True)
                for r in range(4):
                    P.add("sp", lambda h, vp=vp, vsrc=vsrc, r=r: h.dma_start(
                        out=vp[:, 1, r * 8:(r + 1) * 8, :], in_=vsrc.rearrange("(i p r) e -> p r i e", p=128, r=4)[:, r, :, :]), reads=vkeys, writes=[(vk, 1, r)], dma=True)
                for r in range(16):
                    P.add("sp", lambda h, vp=vp, vsrc=vsrc, r=r: h.dma_start(
                        out=vp[:, 2, r * 2:(r + 1) * 2, :], in_=vsrc.rearrange("(i p r) e -> p r i e", p=128, r=16)[:, r, :, :]), reads=vkeys, writes=[(vk, 2, r)], dma=True)
                vallk = [(vk, 0)] + [(vk, 1, r) for r in range(4)] + [(vk, 2, r) for r in range(16)]
                P.add("sp", lambda h, bi=bi, hd=hd: h.dma_start(out=bi[:, 0, :], in_=biasA[hd]), writes=[(bk, 0)], dma=True)
                P.add("sp", lambda h, bi=bi, hd=hd: h.dma_start(out=bi[:, 1, :], in_=bias0[hd]), writes=[(bk, 1)], dma=True)
                bib, bbk = bbr.next()
                P.add("act", lambda h, bi=bi, bib=bib: h.copy(out=bib[:], in_=bi[:]), reads=[(bk, 0), (bk, 1)], writes=[bbk])
                def stageS(pi, d, gq):
                    units = []
                    for u in range(4):
                        if d == 1:
                            qb = 16 + 4 * gq + u
                            kp = kt[:, (qb - 1) * 128:qb * 128]; kc_ = kt[:, qb * 128:(qb + 1) * 128]
                            qa = qt[:, (qb - 16) * 128:(qb - 15) * 128]
                            units.append((kp, kc_, qa, qb - 1, qb, qb == 16))
                        elif d == 4:
                            i = 4 + gq; r = u
                            kp = strided(kt, (i - 1) * 512 + r, 4); kc_ = strided(kt, i * 512 + r, 4)
                            qa = strided(qt, (i - 4) * 512 + r, 4)
                            units.append((kp, kc_, qa, r * 8 + i - 1, r * 8 + i, i == 4))
                        else:
                            r = 4 * gq + u
                            kp = strided(kt, r, 16); kc_ = strided(kt, 2048 + r, 16)
                            qa = strided(qt, r, 16)
                            units.append((kp, kc_, qa, r * 2, r * 2 + 1, True))
                    pS, pSk = pSr.next(); sS, sSk = ssb.next(); pT, pTk = ptb.next()
                    def mm_s(h):
                        for u, (kp, kc_, qa, _, _, _) in enumerate(units):
                            h.matmul(pS[:, u * 256:u * 256 + 128], kp, qa, start=True, stop=True)
                            ins = h.matmul(pS[:, u * 256 + 128:u * 256 + 256], kc_, qa, start=True, stop=True)
                        return ins
                    P.add("pe", mm_s, reads=[qk, kk], writes=[pSk])
                    bsl = slice(pi * 256, (pi + 1) * 256)
                    halo_flags = [st_ == 0 and un[5] for un in units]
                    P.add("act", lambda h: h.activation(out=sS[:], in_=pS[:], func=AF.Exp, scale=INV), reads=[pSk], writes=[sSk])
                    if all(halo_flags) or not any(halo_flags):
                        wh = 1 if halo_flags[0] else 0
                        P.add("dve", lambda h: h.tensor_tensor(
                            out=pT[:].rearrange("p (u c) -> p u c", u=4), in0=sS[:].rearrange("p (u c) -> p u c", u=4),
                            in1=bib[:, wh, bsl].unsqueeze(1).to_broadcast([128, 4, 256]), op=ALU.mult),
                            reads=[sSk, bbk], writes=[pTk])
                    else:
                        P.add("dve", lambda h: h.tensor_tensor(out=pT[:, 0:256], in0=sS[:, 0:256], in1=bib[:, 1, bsl], op=ALU.mult),
                              reads=[sSk, bbk], writes=[(pTk, "a")])
                        P.add("dve", lambda h: h.tensor_tensor(
                            out=pT[:, 256:1024].rearrange("p (u c) -> p u c", u=3), in0=sS[:, 256:1024].rearrange("p (u c) -> p u c", u=3),
                            in1=bib[:, 0, bsl].unsqueeze(1).to_broadcast([128, 3, 256]), op=ALU.mult),
                            reads=[sSk, bbk, (pTk, "a")], writes=[pTk])
                    return dict(units=units, pT=pT, pTk=pTk, pi=pi, d=d, gq=gq)

                def stageV(sc):
                    units, pT, pTk, pi, d, gq = sc["units"], sc["pT"], sc["pTk"], sc["pi"], sc["d"], sc["gq"]
                    pO, pOk = pOr.next(); pD, pDk = pDr.next()
                    def mm_pv(h):
                        for u, (_, _, _, vb0, vb1, _) in enumerate(units):
                            h.matmul(pO[:, u * 128:(u + 1) * 128], vp[:, pi, vb0, :], pT[:, u * 256:u * 256 + 128], start=True, stop=False)
                            h.matmul(pO[:, u * 128:(u + 1) * 128], vp[:, pi, vb1, :], pT[:, u * 256 + 128:u * 256 + 256], start=False, stop=True)
                        for u in range(4):
                            h.matmul(pD[:, u * 128:(u + 1) * 128], oneb[:], pT[:, u * 256:u * 256 + 128], start=True, stop=False)
                            ins = h.matmul(pD[:, u * 128:(u + 1) * 128], oneb[:], pT[:, u * 256 + 128:u * 256 + 256], start=False, stop=True)
                        return ins
                    P.add("pe", mm_pv, reads=[pTk, "oneb"] + vallk, writes=[pOk, pDk])
                    if d == 1:
                        c0 = gq * 512
                        P.add("act", lambda h: h.copy(out=na[:, c0:c0 + 512], in_=pO[:]), reads=[pOk], writes=[(nk, gq)])
                        P.add("act", lambda h: h.copy(out=da[:, c0:c0 + 512], in_=pD[:]), reads=[pDk], writes=[(dak, gq)])
                    else:
                        if d == 4:
                            c0 = gq * 512
                            nv = lambda a: a[:, c0:c0 + 512].rearrange("p (m r) -> p r m", r=4)
                        else:
                            nv = lambda a: a[:, :].rearrange("p (m r) -> p r m", r=16)[:, 4 * gq:4 * gq + 4, :]
                        pv_ = lambda a: a[:].rearrange("p (r m) -> p r m", r=4)
                        wk = [(nk, gq)] if d == 4 else [(nk, q) for q in range(4)]
                        wdk = [(dak, gq)] if d == 4 else [(dak, q) for q in range(4)]
                        P.add("dve", lambda h: h.tensor_tensor(out=nv(na), in0=pv_(pO), in1=nv(na), op=ALU.add), reads=[pOk] + wk, writes=wk)
                        P.add("dve", lambda h: h.tensor_tensor(out=nv(da), in0=pv_(pD), in1=nv(da), op=ALU.add), reads=[pDk] + wdk, writes=wdk)

                glist = [(pi, d, gq) for pi, d in enumerate((1, 4, 16)) for gq in range(4)]
                pend = stageS(*glist[0])
                yield
                for gi_ in range(len(glist)):
                    nxt = stageS(*glist[gi_ + 1]) if gi_ + 1 < len(glist) else None
                    stageV(pend)
                    pend = nxt
                yield
                allk = [(nk, q) for q in range(4)]
                alldk = [(dak, q) for q in range(4)]
                P.add("act", lambda h, da=da: h.activation(out=da[:], in_=da[:], func=AF.Ln), reads=alldk, writes=alldk)
                P.add("act", lambda h, da=da: h.activation(out=da[:], in_=da[:], func=AF.Exp, scale=-1.0), reads=alldk, writes=alldk)
                P.add("dve", lambda h, na=na, da=da: h.tensor_tensor(out=na[:], in0=na[:], in1=da[:], op=ALU.mult), reads=allk + alldk, writes=allk)
                P.add("pool", lambda h, na=na, za=za, oa=oa: h.tensor_tensor(out=oa[:], in0=na[:], in1=za[:], op=ALU.mult), reads=allk + [zk], writes=[ok])
                P.add("pool", lambda h, oa=oa, hd=hd, w0=w0: h.dma_start(out=OaT[hd, :, w0:w0 + 2048], in_=oa[:]), reads=[ok], writes=[("OaT", st_, hd)], dma=True)


            gens = []
            for st_ in range(2):
                w0 = st_ * 2048
                tiles_q = [8 + st_ * 4 + i for i in range(4)]
                tiles_k = [4 + st_ * 4 + i for i in range(8)]
                for hd in range(16):
                    gens.append(do_head(st_, hd, w0, tiles_q, tiles_k))
            next(gens[0])
            for gi2, gen in enumerate(gens):
                next(gen)
                if gi2 + 1 < len(gens):
                    next(gens[gi2 + 1])
                for _ in gen:
                    pass
                for _ in range(4):
                    cv2_tick()
            cv2_flush()

        if stop == "p2":
            P.emit(nc, es, {})
            return nc
        P.barrier()
        final_ops = []
        with ExitStack() as e3:
            s3 = lambda name, shape, dt=F32: e3.enter_context(nc.sbuf_tensor(name, shape, dt))
            p3 = lambda name, shape, dt=F32: e3.enter_context(nc.psum_tensor(name, shape, dt))
            wring = Ring([(s3("w3_%d" % i, [128, 16, 512], BF16), "w3_%d" % i) for i in range(3)])
            hT = s3("dhT", [128, 16, 512], BF16)
            oT = s3("doT", [128, 16, 512], BF16)
            yT = s3("dyT", [128, 32, 512], BF16)
            ynb = Ring([(s3("dyn%d" % i, [128, 4096], BF16), "dyn%d" % i) for i in range(2)])
            mT = s3("dmT", [128, 16, 512], BF16)
            xr4 = Ring([(s3("dxr%d" % i, [128, 2048]), "dxr%d" % i) for i in range(2)])
            sga = s3("dsga", [128, 4, 512], BF16)
            sgs = s3("dsgs", [128, 4, 512], BF16)
            m1 = s3("dm1", [128, 4, 512])
            m2r = Ring([(s3("dm2%d" % i, [128, 512]), "dm2%d" % i) for i in range(1)])
            junk3 = s3("djunk", [128, 2048], BF16)
            stat = Ring([(s3("dst%d" % i, [128, 4]), "dst%d" % i) for i in range(2)])
            psb = Ring([(p3("dps%d" % i, [128, 512]), "dps%d" % i) for i in range(6)])
            ptr = Ring([(p3("dpt%d" % i, [128, 1024], BF16), "dpt%d" % i) for i in range(2)])

            for mt_ in range(8):
                tm0 = mt_ * 512
                ti = 8 + mt_
                st_ = mt_ // 4
                P.add("sp", lambda h, tm0=tm0: h.dma_start(out=hT[:], in_=HnT[:, :, tm0:tm0 + T].rearrange("k p t -> p k t")), reads=[("HnT", ti)], writes=["dhT"], dma=True)
                P.add("sp", lambda h, tm0=tm0: h.dma_start(out=oT[:], in_=OaT[:, :, tm0:tm0 + T].rearrange("k p t -> p k t")), reads=[("OaT", st_, hd) for hd in range(16)], writes=["doT"], dma=True)
                for blk in range(4):
                    yb, ybk = ynb.next()
                    r0 = tm0 + blk * 128
                    P.add("sp", lambda h, yb=yb, r0=r0: h.dma_start(out=yb[:], in_=Yn[r0:r0 + 128, :]), reads=[("Yn", 32 + mt_ * 4 + blk)], writes=[ybk], dma=True)
                    for c4 in range(8):
                        pt, pk = ptr.next()
                        def tr(h, pt=pt, yb=yb, c4=c4):
                            for j in range(4):
                                cc = c4 * 4 + j
                                ins = h.transpose(out=pt[:, j * 128:(j + 1) * 128], in_=yb[:, cc * 128:(cc + 1) * 128], identity=idb[:])
                            return ins
                        P.add("pe", tr, reads=[ybk, "idb"], writes=[pk])
                        P.add("dve", lambda h, pt=pt, c4=c4, blk=blk: h.tensor_tensor(
                            out=yT[:, c4 * 4:(c4 + 1) * 4, blk * 128:(blk + 1) * 128], in0=pt[:, 0:512].rearrange("p (j t) -> p j t", j=4),
                            in1=par[:, P_SNW + c4 * 4:P_SNW + (c4 + 1) * 4].unsqueeze(2).to_broadcast([128, 4, 128]), op=ALU.mult),
                            reads=[pk, "par"], writes=[("dyT", blk, c4)])
                yT_keys = [("dyT", b, c) for b in range(4) for c in range(8)]

                def fm(glist, rhs_of, rkeys, consume):
                    wts = [load_w(wring, g) for g in glist]
                    nk = 16 * len(wts)
                    for cb in range(4):
                        ps, pk = psb.next()
                        def mm(h, wts=wts, ps=ps, cb=cb):
                            n = 0
                            for wi, (wt, _) in enumerate(wts):
                                for kc in range(16):
                                    ins = h.matmul(ps[:], wt[:, kc, cb * 128:(cb + 1) * 128], rhs_of(wi * 16 + kc), start=(n == 0), stop=(n == nk - 1))
                                    n += 1
                            return ins
                        P.add("pe", mm, reads=[k for _, ks in wts for k in ks] + rkeys, writes=[pk])
                        consume(cb, ps, pk)

                for dg in range(4):
                    def c_ga(cb, ps, pk):
                        P.add("act", lambda h: h.activation(out=sga[:, cb, :], in_=ps[:], func=AF.Sigmoid), reads=[pk], writes=[("dsga", cb)])
                    def c_gs(cb, ps, pk):
                        P.add("act", lambda h: h.activation(out=sgs[:, cb, :], in_=ps[:], func=AF.Sigmoid), reads=[pk], writes=[("dsgs", cb)])
                    def c_a(cb, ps, pk):
                        P.add("dve", lambda h: h.tensor_tensor(out=m1[:, cb, :], in0=ps[:], in1=sga[:, cb, :], op=ALU.mult), reads=[pk, ("dsga", cb)], writes=[("dm1", cb)])
                    def c_b(cb, ps, pk, dg=dg):
                        m2, m2k = m2r.next()
                        P.add("dve", lambda h: h.tensor_tensor(out=m2[:], in0=ps[:], in1=sgs[:, cb, :], op=ALU.mult), reads=[pk, ("dsgs", cb)], writes=[m2k])
                        P.add("pool", lambda h: h.tensor_tensor(out=mT[:, dg * 4 + cb, :], in0=m2[:], in1=m1[:, cb, :], op=ALU.add), reads=[m2k, ("dm1", cb)], writes=[("dmT", dg * 4 + cb)])
                    fm([G_GA[dg]], lambda k: hT[:, k, :], ["dhT"], c_ga)
                    fm([G_GS[dg]], lambda k: hT[:, k, :], ["dhT"], c_gs)
                    fm([G_WA[dg]], lambda k: oT[:, k, :], ["doT"], c_a)
                    fm(G_WS[dg], lambda k: yT[:, k, :], yT_keys, c_b)
                mT_keys = [("dmT", k) for k in range(16)]
                for blk in range(4):
                    xb, xbk = xr4.next()
                    r0 = tm0 + blk * 128
                    P.add("sp", lambda h, xb=xb, r0=r0: h.dma_start(out=xb[:], in_=xe[4096 + r0:4096 + r0 + 128, :]), writes=[xbk], dma=True)
                    for cg in range(4):
                        wt, wkeys = load_w(wring, G_WO[cg])
                        ps, pk = psb.next()
                        def mm(h, wt=wt, ps=ps, blk=blk):
                            for kc in range(16):
                                ins = h.matmul(ps[:], mT[:, kc, blk * 128:(blk + 1) * 128], wt[:, kc, :], start=(kc == 0), stop=(kc == 15))
                            return ins
                        P.add("pe", mm, reads=wkeys + mT_keys, writes=[pk])
                        P.add("dve", lambda h, ps=ps, xb=xb, cg=cg: h.tensor_tensor(out=xb[:, cg * 512:(cg + 1) * 512], in0=ps[:], in1=xb[:, cg * 512:(cg + 1) * 512], op=ALU.add),
                              reads=[pk, xbk], writes=[xbk])
                    stt_, sk = stat.next()
                    P.add("act", lambda h, stt_=stt_, xb=xb: h.activation(out=junk3[:], in_=xb[:], func=AF.Square, accum_out=stt_[:, 0:1]),
                          reads=[xbk], writes=["djunk", (sk, 0)])
                    P.add("dve", lambda h, stt_=stt_: h.tensor_scalar(out=stt_[:, 1:2], in0=stt_[:, 0:1], scalar1=1.0 / D, scalar2=EPS, op0=ALU.mult, op1=ALU.add), reads=[(sk, 0)], writes=[(sk, 1)])
                    P.add("act", lambda h, stt_=stt_: h.activation(out=stt_[:, 2:3], in_=stt_[:, 1:2], func=AF.Sqrt), reads=[(sk, 1)], writes=[(sk, 2)])
                    P.add("dve", lambda h, stt_=stt_: h.reciprocal(out=stt_[:, 3:4], in_=stt_[:, 2:3]), reads=[(sk, 2)], writes=[(sk, 3)])
                    P.add("dve", lambda h, stt_=stt_, xb=xb: h.scalar_tensor_tensor(out=xb[:], in0=xb[:], scalar=stt_[:, 3:4], in1=par[:, P_FNW:P_FNW + 2048], op0=ALU.mult, op1=ALU.mult),
                          reads=[xbk, (sk, 3), "par"], writes=[xbk])
                    final_ops.append(P.add("pool", lambda h, xb=xb, r0=r0: h.dma_start(out=out[r0:r0 + 128, :], in_=xb[:]), reads=[xbk], dma=True))

        P.emit(nc, es, {"pool": final_ops})
    return nc


_CACHE = {}


def _alibi_tables():
    tabs = np.zeros((16, 128, 768), np.float32)
    k = np.arange(128)[:, None]
    q = np.arange(128)[None, :]
    for hd in range(16):
        slope = np.float32(2.0 ** (-8.0 * (hd + 1) / 16))
        for pi, d in enumerate((1, 4, 16)):
            dist_p = (q - k + 128).astype(np.float32)
            prev = np.where(k >= q, -slope * dist_p * d, NEG)
            dist_c = (q - k).astype(np.float32)
            cur = np.where(k <= q, -slope * dist_c * d, NEG)
            tabs[hd, :, pi * 256:pi * 256 + 128] = prev
            tabs[hd, :, pi * 256 + 128:pi * 256 + 256] = cur
    return np.exp(tabs.astype(np.float64)).astype(np.float32)


def kernel(x, norm_w, w_in, conv_w, conv_b, dt_bias, a_log, d_skip, ssm_norm_w,
           w_attn_branch, w_ssm_branch, w_out, final_norm_w):
    f = lambda a: np.ascontiguousarray(np.asarray(a, dtype=np.float32))
    x = f(x)
    if "nc" not in _CACHE:
        _CACHE["nc"] = build_program()
    nc = _CACHE["nc"]
    par = np.zeros((128, NPAR), np.float32)
    par[:, P_NORMW:P_NORMW + 2048] = f(norm_w)[0][None, :]
    par[:, P_FNW:P_FNW + 2048] = f(final_norm_w)[None, :]
    cw = f(conv_w)[0]
    par[:, P_CW:P_CW + 192] = cw.reshape(4, 48, 128).transpose(2, 1, 0).reshape(128, 192)
    par[:, P_CB:P_CB + 48] = f(conv_b)[0].reshape(48, 128).T
    par[:, P_SNW:P_SNW + 32] = f(ssm_norm_w)[0].reshape(32, 128).T
    par[:, P_DTB:P_DTB + 64] = f(dt_bias)[0][None, :]
    par[:, P_ALOG:P_ALOG + 64] = f(a_log)[0][None, :]
    par[:, P_DSK:P_DSK + 64] = f(d_skip)[0][None, :]
    par[:, P_ID:P_ID + 128] = np.eye(128, dtype=np.float32)
    ki = np.arange(128)
    par[:, P_TM:P_TM + 128] = (ki[:, None] <= ki[None, :]).astype(np.float32)
    par[:, P_UM:P_UM + 128] = (ki[:, None] > ki[None, :]).astype(np.float32)
    par[:, P_ONE:P_ONE + 128] = 1.0
    tabs = _alibi_tables()
    tabs0 = tabs.copy()
    for pi in range(3):
        tabs0[:, :, pi * 256:pi * 256 + 128] = 0.0
    wi, wa, wss, wo = f(w_in)[0], f(w_attn_branch)[0], f(w_ssm_branch)[0], f(w_out)[0]
    in_maps = []
    for c in range(NCORE):
        b, hf = c // 2, c % 2
        if hf == 0:
            xe = np.concatenate([np.zeros((4096, D), np.float32), x[b, :4096]], axis=0)
        else:
            xe = x[b]
        p = par.copy()
        p[:, P_PV] = float(hf)
        in_maps.append({"xe": np.ascontiguousarray(xe), "w_in": wi, "w_attn": wa, "w_ssm": wss, "w_out": wo,
                        "params": p, "biasA": tabs, "bias0": tabs if hf == 1 else tabs0})
    if _CACHE.get("return_maps"):
        return in_maps
    res = run_bass_kernel_spmd(nc, in_maps, core_ids=list(range(NCORE)))
    outp = np.empty((4, SEQ, D), np.float32)
    for c in range(NCORE):
        b, hf = c // 2, c % 2
        outp[b, hf * 4096:(hf + 1) * 4096] = res.results[c]["out"]
    return outp
```

```python
import numpy as np
import ml_dtypes
from contextlib import ExitStack
import concourse.bass as bass
import concourse.mybir as mybir
from concourse.bass_utils import run_bass_kernel_spmd

F32 = mybir.dt.float32
BF16 = mybir.dt.bfloat16
AF = mybir.ActivationFunctionType
ALU = mybir.AluOpType

D = 2048
NIN = 22592
SEQ = 8192
NCORE = 8
TOKM = 4096
EXT = 8192
T = 512
NT = EXT // T
HALO0 = 2048
EPS = 1e-6
NEG = -30000.0
DBG_CHUNKS = None
DBG_GROUPS = 8
DBG_SU = 8
DBG_LIMIT = None
DBG_F = None
DBG_SKIP_P0 = False

C_Q, C_K, C_V, C_ZA, C_ZS, C_X, C_B, C_C, C_DT, C_GA, C_GS = 0, 2048, 4096, 6144, 8192, 12288, 16384, 17408, 18432, 18496, 20544

WG = []
def _add(name, src, row0, col0, width):
    WG.append((name, src, row0, col0, width))
    return len(WG) - 1
G_Q = [_add("q%d" % i, "w_in", 0, C_Q + 512 * i, 512) for i in range(4)]
G_K = [_add("k%d" % i, "w_in", 0, C_K + 512 * i, 512) for i in range(4)]
G_V = [_add("v%d" % i, "w_in", 0, C_V + 512 * i, 512) for i in range(4)]
G_ZA = [_add("za%d" % i, "w_in", 0, C_ZA + 512 * i, 512) for i in range(4)]
G_ZS = [_add("zs%d" % i, "w_in", 0, C_ZS + 512 * i, 512) for i in range(8)]
G_XBC = [_add("xbc%d" % i, "w_in", 0, C_X + 512 * i, 512) for i in range(12)]
G_DT = _add("dt", "w_in", 0, C_DT, 64)
G_GA = [_add("ga%d" % i, "w_in", 0, C_GA + 512 * i, 512) for i in range(4)]
G_GS = [_add("gs%d" % i, "w_in", 0, C_GS + 512 * i, 512) for i in range(4)]
G_WA = [_add("wa%d" % i, "w_attn", 0, 512 * i, 512) for i in range(4)]
G_WS = [[_add("ws%d_%d" % (i, kh), "w_ssm", 2048 * kh, 512 * i, 512) for kh in range(2)] for i in range(4)]
G_WO = [_add("wo%d" % i, "w_out", 0, 512 * i, 512) for i in range(4)]
NG = len(WG)
P0_GROUPS = G_XBC + [G_DT]
P1A_JOBS = [(g, q) for g in (G_K + G_V + G_Q + G_ZA + G_ZS + G_GA + G_GS + G_WA + [x for pr in G_WS for x in pr] + G_WO) for q in range(4)]
P2_JOBS = []

P_NORMW = 0
P_FNW = 2048
P_CW = 4096
P_CB = P_CW + 192
P_SNW = P_CB + 48
P_DTB = P_SNW + 32
P_ALOG = P_DTB + 64
P_DSK = P_ALOG + 64
P_PV = P_DSK + 64
P_ID = P_PV + 1
P_TM = P_ID + 128
P_UM = P_TM + 128
P_ONE = P_UM + 128
NPAR = P_ONE + 128


class Op:
    __slots__ = ("eng", "fn", "deps", "dma", "sem", "val", "has_dep", "cnt", "semi")

    def __init__(self, eng, fn, dma):
        self.eng = eng; self.fn = fn; self.dma = dma; self.deps = []
        self.sem = None; self.val = 0; self.has_dep = False; self.cnt = 0; self.semi = 0


class Prog:
    ENGS = ("pe", "act", "dve", "pool", "sp")
    NDS = 14
    CHUNK = 20000

    def __init__(self):
        self.ops = {e: [] for e in self.ENGS}
        self.res = {}
        self.ndma = {e: 0 for e in self.ENGS}
        self.pend = {}

    def barrier(self):
        deps = []
        for e in self.ENGS:
            last_c = None
            dm = {}
            for o in self.ops[e]:
                if o.dma:
                    dm[o.semi] = o
                else:
                    last_c = o
            if last_c is not None:
                deps.append(last_c)
            deps.extend(dm.values())
        self.pend = {e: list(deps) for e in self.ENGS}
        print("[barrier] ops so far:", getattr(self, "nadd", 0))

    def add(self, eng, fn, reads=(), writes=(), dma=False):
        op = Op(eng, fn, dma)
        self.nadd = getattr(self, "nadd", 0) + 1
        if DBG_LIMIT is not None and self.nadd > DBG_LIMIT:
            return op
        deps = list(self.pend.pop(eng, []))
        for k in reads:
            st = self.res.get(k)
            if st is not None and st[0] is not None:
                deps.append(st[0])
        for k in writes:
            st = self.res.get(k)
            if st is not None:
                if st[0] is not None:
                    deps.append(st[0])
                deps.extend(st[1].values())
                deps.extend(st[2])
        seen = set()
        for d in deps:
            if d is op or id(d) in seen:
                continue
            seen.add(id(d))
            if (not d.dma) and d.eng == eng and eng == "pe":
                continue
            op.deps.append(d)
            d.has_dep = True
        for k in reads:
            st = self.res.setdefault(k, [None, {}, []])
            if dma:
                st[2].append(op)
            else:
                st[1][eng] = op
        for k in writes:
            self.res[k] = [op, {}, []]
        if dma:
            n = self.ndma[eng]
            self.ndma[eng] = n + 1
            op.semi = n % self.NDS
            op.val = 16 * (n // self.NDS + 1)
        self.ops[eng].append(op)
        return op

    def emit(self, nc, es, final_waits):
        csem = {}
        for e in ("pe", "act", "dve", "pool"):
            nd = sum(1 for o in self.ops[e] if (not o.dma) and o.has_dep)
            csem[e] = [es.enter_context(nc.semaphore("c_%s_%d" % (e, i))) for i in range(nd // self.CHUNK + 1)]
            c = 0
            for o in self.ops[e]:
                if (not o.dma) and o.has_dep:
                    o.semi = c // self.CHUNK
                    o.cnt = c % self.CHUNK + 1
                    c += 1
        dsem = {}
        for e in self.ENGS:
            if self.ndma[e]:
                dsem[e] = [es.enter_context(nc.semaphore("d_%s_%d" % (e, i))) for i in range(self.NDS)]
        block = es.enter_context(nc.Block())
        prog = self

        wlog = self.wlog = []
        def run(e, h):
            seen = {}
            def wait(sem, val):
                key = id(sem)
                if seen.get(key, 0) >= val:
                    return
                seen[key] = val
                if DBG_LIMIT is not None:
                    wlog.append((e, str(getattr(sem, "name", sem)), val))
                h.wait_ge(sem, val)
            for o in prog.ops[e]:
                for d in o.deps:
                    if d.dma:
                        wait(dsem[d.eng][d.semi], d.val)
                    else:
                        wait(csem[d.eng][d.semi], d.cnt)
                if o.dma:
                    if o.val > 16:
                        wait(dsem[e][o.semi], o.val - 16)
                    ins = o.fn(h)
                    ins.then_inc(dsem[e][o.semi], 16)
                else:
                    ins = o.fn(h)
                    if o.has_dep:
                        ins.then_inc(csem[e][o.semi], 1)
            if e in final_waits:
                for d in final_waits[e]:
                    wait(dsem[d.eng][d.semi], d.val)

        @block.tensor
        def _(h):
            run("pe", h)

        @block.scalar
        def _(h):
            run("act", h)

        @block.vector
        def _(h):
            run("dve", h)

        @block.gpsimd
        def _(h):
            run("pool", h)

        @block.sync
        def _(h):
            run("sp", h)


class Ring:
    def __init__(self, items):
        self.items = items; self.i = 0

    def next(self):
        it = self.items[self.i % len(self.items)]
        self.i += 1
        return it


def build_program(stop=None, dbg=False, tiles=None):
    nc = bass.Bass("TRN2", target_bir_lowering=False)
    P = Prog()

    def din(name, shape, dt=F32):
        return nc.dram_tensor(name, shape, dt, kind="ExternalInput").ap()

    def dscr(name, shape, dt=BF16):
        return nc.dram_tensor(name, shape, dt, kind="ExternalOutput" if dbg else "Internal").ap()

    xe = din("xe", [EXT, D])
    wsrc = {"w_in": din("w_in", [D, NIN]), "w_attn": din("w_attn", [2048, D]),
            "w_ssm": din("w_ssm", [4096, D]), "w_out": din("w_out", [D, D])}
    params = din("params", [128, NPAR])
    biasA = din("biasA", [16, 128, 768])
    bias0 = din("bias0", [16, 128, 768])
    out = nc.dram_tensor("out", [TOKM, D], F32, kind="ExternalOutput").ap()

    wb = dscr("wb", [NG, 128, 16 * 512])
    Qs = dscr("Qs", [16, 128, TOKM])
    Ks = dscr("Ks", [16, 128, 6144])
    Vs = dscr("Vs", [6144, 2048])
    SZa = dscr("SZa", [16, 128, TOKM])
    SZs = dscr("SZs", [TOKM, 4096])
    HnT = dscr("HnT", [16, 128, TOKM])
    Xtok = dscr("Xtok", [EXT, 5120])
    BTs = dscr("BTs", [8, 128, TOKM])
    CTs = dscr("CTs", [8, 128, TOKM])
    DTs = dscr("DTs", [EXT, 64], F32)
    DAs = dscr("DAs", [EXT, 64], F32)
    Yn = dscr("Yn", [TOKM, 4096])
    OaT = dscr("OaT", [16, 128, TOKM])

    with ExitStack() as es:
        sb = lambda name, shape, dt=F32: es.enter_context(nc.sbuf_tensor(name, shape, dt))
        par = sb("par", [128, NPAR])
        idb = sb("idb", [128, 128], BF16)
        oneb = sb("oneb", [128, 128], BF16)
        anegt = sb("anegt", [128, 64])
        P.add("sp", lambda h: h.dma_start(out=par[:], in_=params[:, :]), writes=["par"], dma=True)
        P.add("dve", lambda h: h.tensor_copy(out=idb[:], in_=par[:, P_ID:P_ID + 128]), reads=["par"], writes=["idb"])
        P.add("dve", lambda h: h.tensor_copy(out=oneb[:], in_=par[:, P_ONE:P_ONE + 128]), reads=["par"], writes=["oneb"])
        P.add("act", lambda h: h.activation(out=anegt[:], in_=par[:, P_ALOG:P_ALOG + 64], func=AF.Exp), reads=["par"], writes=["aneg0"])
        P.add("dve", lambda h: h.tensor_scalar(out=anegt[:], in0=anegt[:], scalar1=-1.0, scalar2=None, op0=ALU.mult), reads=["aneg0"], writes=["aneg"])
        Tm = par[:, P_TM:P_TM + 128]
        Um = par[:, P_UM:P_UM + 128]
        ones32 = par[:, P_ONE:P_ONE + 128]

        with ExitStack() as e0:
            s0 = lambda name, shape, dt=F32: e0.enter_context(nc.sbuf_tensor(name, shape, dt))
            wf = [s0("wf%d" % i, [128, 16, 512]) for i in range(2)]
            wc = [s0("wc%d" % i, [128, 16, 512], BF16) for i in range(2)]
            cast_engs = ["act", "dve"]
            for g in ([] if DBG_SKIP_P0 else P0_GROUPS):
                name, src, row0, col0, width = WG[g]
                sl = g % 2
                srcap = wsrc[src][row0:row0 + 2048, col0:col0 + width].rearrange("(kc p) c -> p kc c", p=128)
                for hlf in range(2):
                    P.add("sp", lambda h, sl=sl, srcap=srcap, hlf=hlf, width=width: h.dma_start(
                        out=wf[sl][:, hlf * 8:(hlf + 1) * 8, 0:width], in_=srcap[:, hlf * 8:(hlf + 1) * 8, :]),
                        writes=[("wf", sl, hlf)], dma=True)
                for hlf in range(2):
                    ce = cast_engs[(g + hlf) % 2]
                    if ce == "act":
                        fn = lambda h, sl=sl, hlf=hlf, width=width: h.copy(out=wc[sl][:, hlf * 8:(hlf + 1) * 8, 0:width], in_=wf[sl][:, hlf * 8:(hlf + 1) * 8, 0:width])
                    else:
                        fn = lambda h, sl=sl, hlf=hlf, width=width: h.tensor_copy(out=wc[sl][:, hlf * 8:(hlf + 1) * 8, 0:width], in_=wf[sl][:, hlf * 8:(hlf + 1) * 8, 0:width])
                    P.add(ce, fn, reads=[("wf", sl, hlf)], writes=[("wc", sl, hlf)])
                dst = wb[g].rearrange("p (kc c) -> p kc c", c=512)
                P.add("pool", lambda h, sl=sl, dst=dst, width=width: h.dma_start(out=dst[:, :, 0:width], in_=wc[sl][:, :, 0:width]),
                      reads=[("wc", sl, 0), ("wc", sl, 1)], writes=[("wb", g)], dma=True)

        if stop == "p0":
            P.emit(nc, es, {})
            return nc
        P.barrier()
        def load_w(wring, g):
            wt, key = wring.next()
            width = WG[g][4]
            src = wb[g].rearrange("p (kc c) -> p kc c", c=512)
            for hlf in range(2):
                P.add("sp", lambda h, wt=wt, src=src, hlf=hlf, width=width: h.dma_start(
                    out=wt[:, hlf * 8:(hlf + 1) * 8, 0:width], in_=src[:, hlf * 8:(hlf + 1) * 8, 0:width]),
                    reads=[("wb", g)] + [("wb", g, q) for q in range(4)], writes=[(key, hlf)], dma=True)
            return wt, [(key, 0), (key, 1)]


        def make_converter(sfn, jobs, tag):
            wfq = Ring([(sfn("cvf%s%d" % (tag, i), [128, 4, 512]), "cvf%s%d" % (tag, i)) for i in range(3)])
            wcq = Ring([(sfn("cvc%s%d" % (tag, i), [128, 4, 512], BF16), "cvc%s%d" % (tag, i)) for i in range(2)])
            st = {"ld": 0, "loaded": []}

            def issue_load():
                if st["ld"] >= len(jobs):
                    return
                g, q = jobs[st["ld"]]
                st["ld"] += 1
                name, src, row0, col0, width = WG[g]
                wf, wfk = wfq.next()
                srcap = wsrc[src][row0 + q * 512:row0 + (q + 1) * 512, col0:col0 + width].rearrange("(kc p) c -> p kc c", p=128)
                P.add("act", lambda h: h.dma_start(out=wf[:, :, 0:width], in_=srcap), writes=[wfk], dma=True)
                st["loaded"].append((g, q, wf, wfk, width))

            def tick():
                issue_load()
                if not st["loaded"]:
                    return False
                if len(st["loaded"]) <= 2 and st["ld"] < len(jobs):
                    return True
                g, q, wf, wfk, width = st["loaded"].pop(0)
                wc, wck = wcq.next()
                P.add("act", lambda h: h.copy(out=wc[:, :, 0:width], in_=wf[:, :, 0:width]), reads=[wfk], writes=[wck])
                dst = wb[g].rearrange("p (kc c) -> p kc c", c=512)
                P.add("act", lambda h: h.dma_start(out=dst[:, 4 * q:4 * q + 4, 0:width], in_=wc[:, :, 0:width]), reads=[wck], writes=[("wb", g, q)], dma=True)
                return True

            def flush():
                while tick():
                    pass
            return tick, flush

        evac_rr = [0]

        def evac_eng():
            evac_rr[0] += 1
            return "act" if evac_rr[0] % 2 else "dve"

        def copy_op(eng, out_ap, in_ap, reads, writes):
            if eng == "act":
                return P.add("act", lambda h: h.copy(out=out_ap, in_=in_ap), reads=reads, writes=writes)
            return P.add(eng, lambda h: h.tensor_copy(out=out_ap, in_=in_ap), reads=reads, writes=writes)

        with ExitStack() as e1:
            s1 = lambda name, shape, dt=F32: e1.enter_context(nc.sbuf_tensor(name, shape, dt))
            p1 = lambda name, shape, dt=F32: e1.enter_context(nc.psum_tensor(name, shape, dt))
            wring = Ring([(s1("w1_%d" % i, [128, 16, 512], BF16), "w1_%d" % i) for i in range(3)])
            xbuf = Ring([(s1("xb%d" % i, [128, 2048]), "xb%d" % i) for i in range(2)])
            hnb = Ring([(s1("hnb%d" % i, [128, 2048], BF16), "hnb%d" % i) for i in range(2)])
            junk = s1("junk", [128, 2048], BF16)
            hnT = Ring([(s1("hnT%d" % i, [128, 16, 512], BF16), "hnT%d" % i) for i in range(2)])
            stat = Ring([(s1("stat%d" % i, [128, 4]), "stat%d" % i) for i in range(2)])
            stage = Ring([(s1("stg%d" % i, [128, 512], BF16), "stg%d" % i) for i in range(4)])
            XR = Ring([(s1("XR%d" % i, [128, 515]), "XR%d" % i) for i in range(2)])
            cacc = Ring([(s1("cacc%d" % i, [128, 512]), "cacc%d" % i) for i in range(2)])
            tail = s1("tail", [128, 48, 3])
            xcT = Ring([(s1("xcT%d" % i, [128, 512], BF16), "xcT%d" % i) for i in range(10)])
            xtk = Ring([(s1("xtk%d" % i, [128, 4, 512], BF16), "xtk%d" % i) for i in range(2)])
            dtw = Ring([(s1("dtw%d" % i, [128, 6, 256]), "dtw%d" % i) for i in range(2)])
            psb = Ring([(p1("ps1_%d" % i, [128, 512]), "ps1_%d" % i) for i in range(5)])
            ptr = Ring([(p1("pt1_%d" % i, [128, 1024], BF16), "pt1_%d" % i) for i in range(3)])

            P.add("dve", lambda h: h.memset(tail[:], 0.0), writes=[("tail", c) for c in range(48)])
            cv_tick, cv_flush = make_converter(s1, [] if DBG_SKIP_P0 else P1A_JOBS, "a")

            def emit_hn(ti):
                t0 = ti * T
                tm0 = t0 - 4096
                hT, hTk = hnT.next()
                for blk in range(4):
                    xb, xk = xbuf.next()
                    hb, hk = hnb.next()
                    stt, sk = stat.next()
                    r0 = t0 + blk * 128
                    P.add("sp", lambda h, xb=xb, r0=r0: h.dma_start(out=xb[:], in_=xe[r0:r0 + 128, :]), writes=[xk], dma=True)
                    P.add("act", lambda h, xb=xb, stt=stt: h.activation(out=junk[:], in_=xb[:], func=AF.Square, accum_out=stt[:, 0:1]),
                          reads=[xk], writes=["junk", (sk, 0)])
                    P.add("dve", lambda h, stt=stt: h.tensor_scalar(out=stt[:, 1:2], in0=stt[:, 0:1], scalar1=1.0 / D, scalar2=EPS, op0=ALU.mult, op1=ALU.add),
                          reads=[(sk, 0)], writes=[(sk, 1)])
                    P.add("act", lambda h, stt=stt: h.activation(out=stt[:, 2:3], in_=stt[:, 1:2], func=AF.Sqrt), reads=[(sk, 1)], writes=[(sk, 2)])
                    P.add("dve", lambda h, stt=stt: h.reciprocal(out=stt[:, 3:4], in_=stt[:, 2:3]), reads=[(sk, 2)], writes=[(sk, 3)])
                    P.add("dve", lambda h, xb=xb, hb=hb, stt=stt: h.scalar_tensor_tensor(
                        out=hb[:], in0=xb[:], scalar=stt[:, 3:4], in1=par[:, P_NORMW:P_NORMW + 2048], op0=ALU.mult, op1=ALU.mult),
                        reads=[xk, (sk, 3), "par"], writes=[hk])
                    for q4 in range(4):
                        pt, pk = ptr.next()
                        def tr(h, pt=pt, hb=hb, q4=q4):
                            for j in range(4):
                                kc = q4 * 4 + j
                                ins = h.transpose(out=pt[:, j * 128:(j + 1) * 128], in_=hb[:, kc * 128:(kc + 1) * 128], identity=idb[:])
                            return ins
                        P.add("pe", tr, reads=[hk, "idb"], writes=[pk])
                        copy_op(evac_eng(), hT[:, q4 * 4:(q4 + 1) * 4, blk * 128:(blk + 1) * 128],
                                pt[:, 0:512].rearrange("p (j t) -> p j t", j=4), [pk], [(hTk, blk, q4)])
                hT_keys = [(hTk, b, q) for b in range(4) for q in range(4)]
                if ti >= 8:
                    P.add("pool", lambda h, hT=hT, tm0=tm0: h.dma_start(out=HnT[:, :, tm0:tm0 + T].rearrange("k p t -> p k t"), in_=hT[:]),
                          reads=hT_keys, writes=[("HnT", ti)], dma=True)
                return hT, hT_keys

            tile_list = list(tiles if tiles is not None else range(NT))
            hn_ready = {}
            if tile_list:
                hn_ready[tile_list[0]] = emit_hn(tile_list[0])
            for tidx, ti in enumerate(tile_list):
                main = ti >= 8
                halo = ti >= 4
                t0 = ti * T
                tm0 = t0 - 4096
                tk0 = t0 - HALO0
                hT, hT_keys = hn_ready.pop(ti)
                def fm_group(g, consume):
                    cv_tick()
                    wt, wkeys = load_w(wring, g)
                    for cb in range(4):
                        ps, pk = psb.next()
                        def mm(h, wt=wt, ps=ps, cb=cb, hT=hT):
                            for kc in range(16):
                                ins = h.matmul(ps[:], wt[:, kc, cb * 128:(cb + 1) * 128], hT[:, kc, :], start=(kc == 0), stop=(kc == 15))
                            return ins
                        P.add("pe", mm, reads=wkeys + hT_keys, writes=[pk])
                        consume(cb, ps, pk)

                def tm_group(g, consume, width=512):
                    cv_tick()
                    wt, wkeys = load_w(wring, g)
                    for blk in range(4):
                        ps, pk = psb.next()
                        def mm(h, wt=wt, ps=ps, blk=blk, hT=hT):
                            for kc in range(16):
                                ins = h.matmul(ps[:, 0:width], hT[:, kc, blk * 128:(blk + 1) * 128], wt[:, kc, 0:width], start=(kc == 0), stop=(kc == 15))
                            return ins
                        P.add("pe", mm, reads=wkeys + hT_keys, writes=[pk])
                        consume(blk, ps, pk)

                def store_fm(dst3, hidx_base, toff, func=None, nm=None):
                    def consume(cb, ps, pk):
                        sg, sgk = stage.next()
                        if func is None:
                            copy_op(evac_eng(), sg[:], ps[:], [pk], [sgk])
                        else:
                            P.add("act", lambda h: h.activation(out=sg[:], in_=ps[:], func=func), reads=[pk], writes=[sgk])
                        hidx = hidx_base + cb
                        P.add("pool", lambda h: h.dma_start(out=dst3[hidx, :, toff:toff + T], in_=sg[:]), reads=[sgk], writes=[(nm, ti, hidx)], dma=True)
                    return consume

                def store_tm(dst2, col0, roff, func=None, nm=None):
                    def consume(blk, ps, pk):
                        sg, sgk = stage.next()
                        if func is None:
                            copy_op(evac_eng(), sg[:], ps[:], [pk], [sgk])
                        else:
                            P.add("act", lambda h: h.activation(out=sg[:], in_=ps[:], func=func), reads=[pk], writes=[sgk])
                        r0 = roff + blk * 128
                        P.add("pool", lambda h: h.dma_start(out=dst2[r0:r0 + 128, col0:col0 + 512], in_=sg[:]), reads=[sgk], writes=[(nm, ti, col0 // 512, blk)], dma=True)
                    return consume

                if main:
                    for i in range(4):
                        fm_group(G_Q[i], store_fm(Qs, 4 * i, tm0, nm="Qs"))
                if halo:
                    for i in range(4):
                        fm_group(G_K[i], store_fm(Ks, 4 * i, tk0, nm="Ks"))
                    for i in range(4):
                        tm_group(G_V[i], store_tm(Vs, 512 * i, tk0, nm="Vs"))
                if main:
                    for i in range(4):
                        fm_group(G_ZA[i], store_fm(SZa, 4 * i, tm0, AF.Silu, nm="SZa"))
                    for i in range(8):
                        tm_group(G_ZS[i], store_tm(SZs, 512 * i, tm0, AF.Silu, nm="SZs"))

                pend_tr = None
                pend_silu = []
                for i in range(12):
                    xcs = []
                    def consume(cb, ps, pk, i=i, xcs=xcs):
                        cbi = 4 * i + cb
                        xr, xrk = XR.next()
                        ca, cak = cacc.next()
                        xc, xck = xcT.next()
                        P.add("act", lambda h: h.copy(out=xr[:, 3:515], in_=ps[:]), reads=[pk], writes=[(xrk, 1)])
                        while pend_silu:
                            pend_silu.pop(0)()
                        P.add("dve", lambda h: h.tensor_copy(out=xr[:, 0:3], in_=tail[:, cbi, :]), reads=[("tail", cbi)], writes=[(xrk, 0)])
                        P.add("dve", lambda h: h.tensor_copy(out=tail[:, cbi, :], in_=xr[:, 512:515]), reads=[(xrk, 1), (xrk, 0)], writes=[("tail", cbi)])
                        cw = lambda k: par[:, P_CW + cbi * 4 + k:P_CW + cbi * 4 + k + 1]
                        P.add("dve", lambda h: h.tensor_scalar(out=ca[:], in0=xr[:, 3:515], scalar1=cw(3), scalar2=par[:, P_CB + cbi:P_CB + cbi + 1],
                                                               op0=ALU.mult, op1=ALU.add), reads=[(xrk, 1), (xrk, 0), "par"], writes=[cak])
                        for k in (2, 1, 0):
                            P.add("dve", lambda h, k=k: h.scalar_tensor_tensor(out=ca[:], in0=xr[:, k:k + 512], scalar=cw(k), in1=ca[:],
                                                                                op0=ALU.mult, op1=ALU.add), reads=[cak, (xrk, 1), (xrk, 0)], writes=[cak])
                        def do_silu(xc=xc, ca=ca, cak=cak, xck=xck):
                            P.add("act", lambda h: h.activation(out=xc[:], in_=ca[:], func=AF.Silu), reads=[cak], writes=[xck])
                        pend_silu.append(do_silu)
                        xcs.append((xc, xck))
                        if i >= 8 and main:
                            gi = cbi - 32 if i < 10 else cbi - 40
                            dst = BTs if i < 10 else CTs
                            def do_store(xc=xc, xck=xck, dst=dst, gi=gi, tm0=tm0, ti=ti, cbi=cbi):
                                P.add("pool", lambda h: h.dma_start(out=dst[gi, :, tm0:tm0 + T], in_=xc[:]), reads=[xck], writes=[("BC", ti, cbi)], dma=True)
                            pend_silu.append(do_store)
                    fm_group(G_XBC[i], consume)
                    if pend_tr is not None:
                        pend_tr()
                        pend_tr = None
                    if i < 10:
                        def do_tr(i=i, xcs=xcs, t0=t0):
                            xt_, xtkk = xtk.next()
                            for ch in range(4):
                                pt, pk = ptr.next()
                                def tr(h, pt=pt, ch=ch, xcs=xcs):
                                    for j in range(4):
                                        ins = h.transpose(out=pt[:, j * 128:(j + 1) * 128], in_=xcs[j][0][:, ch * 128:(ch + 1) * 128], identity=idb[:])
                                    return ins
                                P.add("pe", tr, reads=[k for _, k in xcs] + ["idb"], writes=[pk])
                                copy_op(evac_eng(), xt_[:, ch, :], pt[:, 0:512], [pk], [(xtkk, ch)])
                            P.add("pool", lambda h, xt_=xt_, i=i, t0=t0: h.dma_start(
                                out=Xtok[t0:t0 + T, 512 * i:512 * (i + 1)].rearrange("(c p) f -> p c f", p=128), in_=xt_[:]),
                                reads=[(xtkk, c) for c in range(4)], writes=[("Xtok", ti, i)], dma=True)
                        pend_tr = do_tr
                    if i == 5 and tidx + 1 < len(tile_list):
                        hn_ready[tile_list[tidx + 1]] = emit_hn(tile_list[tidx + 1])
                    pass
                while pend_silu:
                    pend_silu.pop(0)()
                if pend_tr is not None:
                    pend_tr()
                    pend_tr = None

                dw, dwk = dtw.next()
                wt, wkeys = load_w(wring, G_DT)
                ps, pk = psb.next()
                def mmdt(h, wt=wt, ps=ps, hT=hT):
                    for blk in range(4):
                        for kc in range(16):
                            ins = h.matmul(ps[:, blk * 64:(blk + 1) * 64], hT[:, kc, blk * 128:(blk + 1) * 128], wt[:, kc, 0:64], start=(kc == 0), stop=(kc == 15))
                    return ins
                P.add("pe", mmdt, reads=wkeys + hT_keys, writes=[pk])
                dtb_bc = par[:, P_DTB:P_DTB + 64].unsqueeze(1).to_broadcast([128, 4, 64])
                an_bc = anegt[:, :].unsqueeze(1).to_broadcast([128, 4, 64])
                v3 = lambda a: a.rearrange("p (b j) -> p b j", b=4)
                z, nz, mn, ex, dtv, dav = (dw[:, i, :] for i in range(6))
                P.add("dve", lambda h, ps=ps, z=z: h.tensor_tensor(out=v3(z), in0=v3(ps[:, 0:256]), in1=dtb_bc, op=ALU.add), reads=[pk, "par"], writes=[(dwk, 0)])
                P.add("dve", lambda h, z=z, nz=nz: h.tensor_scalar(out=nz, in0=z, scalar1=-1.0, scalar2=None, op0=ALU.mult), reads=[(dwk, 0)], writes=[(dwk, 1)])
                P.add("dve", lambda h, z=z, nz=nz, mn=mn: h.tensor_tensor(out=mn, in0=z, in1=nz, op=ALU.min), reads=[(dwk, 0), (dwk, 1)], writes=[(dwk, 2)])
                P.add("act", lambda h, mn=mn, ex=ex: h.activation(out=ex, in_=mn, func=AF.Exp), reads=[(dwk, 2)], writes=[(dwk, 3)])
                P.add("act", lambda h, ex=ex: h.activation(out=ex, in_=ex, func=AF.Ln, bias=1.0), reads=[(dwk, 3)], writes=[(dwk, 3)])
                P.add("dve", lambda h, z=z, nz=nz: h.tensor_scalar(out=nz, in0=z, scalar1=0.0, scalar2=None, op0=ALU.max), reads=[(dwk, 0), (dwk, 2)], writes=[(dwk, 1)])
                P.add("dve", lambda h, nz=nz, ex=ex, dtv=dtv: h.tensor_tensor(out=dtv, in0=nz, in1=ex, op=ALU.add), reads=[(dwk, 1), (dwk, 3)], writes=[(dwk, 4)])
                P.add("dve", lambda h, dtv=dtv, dav=dav: h.tensor_tensor(out=v3(dav), in0=v3(dtv), in1=an_bc, op=ALU.mult), reads=[(dwk, 4), "aneg"], writes=[(dwk, 5)])
                P.add("pool", lambda h, dtv=dtv, t0=t0: h.dma_start(out=DTs[t0:t0 + T, :].rearrange("(b p) j -> p b j", p=128), in_=v3(dtv)), reads=[(dwk, 4)], writes=[("DT", ti)], dma=True)
                P.add("pool", lambda h, dav=dav, t0=t0: h.dma_start(out=DAs[t0:t0 + T, :].rearrange("(b p) j -> p b j", p=128), in_=v3(dav)), reads=[(dwk, 5)], writes=[("DA", ti)], dma=True)
            cv_flush()


        if stop == "p1a":
            P.emit(nc, es, {})
            return nc
        P.barrier()
        INV = 1.0 / float(np.sqrt(128.0))
        with ExitStack() as e1:
            s1 = lambda name, shape, dt=F32: e1.enter_context(nc.sbuf_tensor(name, shape, dt))
            p1 = lambda name, shape, dt=F32: e1.enter_context(nc.psum_tensor(name, shape, dt))
            xtr = Ring([(s1("bxt%d" % i, [128, 5120], BF16), "bxt%d" % i) for i in range(3)])
            ddr = Ring([(s1("bdd%d" % i, [128, 2, 64]), "bdd%d" % i) for i in range(3)])
            bcr = Ring([(s1("bbc%d" % i, [128, 2, 8, 128], BF16), "bbc%d" % i) for i in range(3)])
            szr = Ring([(s1("bsz%d" % i, [128, 4096], BF16), "bsz%d" % i) for i in range(2)])
            smr = Ring([(s1("bsm%d" % i, [128, 5, 64]), "bsm%d" % i) for i in range(3)])
            xdr = Ring([(s1("bxd%d" % i, [128, 4096], BF16), "bxd%d" % i) for i in range(2)])
            xwr = Ring([(s1("bxw%d" % i, [128, 4096], BF16), "bxw%d" % i) for i in range(2)])
            cbr = Ring([(s1("bcb%d" % i, [128, 8, 128], BF16), "bcb%d" % i) for i in range(2)])
            dur = Ring([(s1("bdu%d" % i, [128, 8, 128]), "bdu%d" % i) for i in range(2)])
            der = Ring([(s1("bde%d" % i, [128, 1024], BF16), "bde%d" % i) for i in range(2)])
            mtr = Ring([(s1("bmt%d" % i, [128, 8, 128], BF16), "bmt%d" % i) for i in range(2)])
            t1r = Ring([(s1("bt1%d" % i, [128, 512]), "bt1%d" % i) for i in range(2)])
            t2r = Ring([(s1("bt2%d" % i, [128, 512]), "bt2%d" % i) for i in range(2)])
            t3r = Ring([(s1("bt3%d" % i, [128, 512]), "bt3%d" % i) for i in range(2)])
            ygf = s1("bygf", [128, 4096])
            ynr = Ring([(s1("byn%d" % i, [128, 4096], BF16), "byn%d" % i) for i in range(1)])
            ssr = Ring([(s1("bss%d" % i, [128, 12]), "bss%d" % i) for i in range(2)])
            jk = s1("bjunk", [128, 512], BF16)
            stt = s1("bst", [128, 8, 512])
            stb = s1("bstb", [128, 8, 512], BF16)
            pa = p1("bpa", [128, 512])
            pcbr = Ring([(p1("bpcb%d" % i, [128, 512]), "bpcb%d" % i) for i in range(2)])
            pseg = p1("bpseg", [128, 1024])
            py = p1("bpy", [128, 512])
            po = p1("bpo", [128, 512])
            pst = p1("bpst", [128, 512])

            P.add("dve", lambda h: h.memset(stt[:], 0.0), writes=[("st", g) for g in range(8)])
            P.add("pool", lambda h: h.memset(stb[:], 0.0), writes=[("stb", g) for g in range(8)])
            dsk = par[:, P_DSK:P_DSK + 64]
            v8 = lambda a: a.rearrange("p (j q) -> p j q", j=8)

            b64 = lambda a: a.unsqueeze(2).to_broadcast([128, 64, 64])
            v64 = lambda a: a.rearrange("p (j q) -> p j q", j=64)
            b8 = lambda a: a.unsqueeze(2).to_broadcast([128, 8, 64])

            def header(ci):
                c = dict(ci=ci, main=ci >= 32, ti=ci // 4, r0=ci * 128, rm0=ci * 128 - 4096)
                ti, r0, rm0 = c["ti"], c["r0"], c["rm0"]
                xt, xk = xtr.next(); dd, dk = ddr.next(); sm, smk = smr.next(); xd, xdk = xdr.next(); xw, xwk = xwr.next()
                c.update(xt=xt, xk=xk, dd=dd, dk=dk, sm=sm, smk=smk, xd=xd, xdk=xdk, xw=xw, xwk=xwk)
                P.add("sp", lambda h: h.dma_start(out=xt[:], in_=Xtok[r0:r0 + 128, :]), reads=[("Xtok", ti, i) for i in range(10)], writes=[xk], dma=True)
                P.add("sp", lambda h: h.dma_start(out=dd[:, 0, :], in_=DTs[r0:r0 + 128, :]), reads=[("DT", ti)], writes=[(dk, 0)], dma=True)
                P.add("sp", lambda h: h.dma_start(out=dd[:, 1, :], in_=DAs[r0:r0 + 128, :]), reads=[("DA", ti)], writes=[(dk, 1)], dma=True)
                dtt = dd[:, 0, :]; dat = dd[:, 1, :]
                c.update(dtt=dtt, dat=dat)
                if c["main"]:
                    bc, bck = bcr.next(); sz, szk = szr.next()
                    c.update(bc=bc, bck=bck, sz=sz, szk=szk)
                    P.add("sp", lambda h: h.dma_start(out=bc[:, 0, :, :], in_=BTs[:, :, rm0:rm0 + 128].rearrange("g n t -> n g t")),
                          reads=[("BC", ti, cc) for cc in range(32, 40)], writes=[(bck, 0)], dma=True)
                    P.add("sp", lambda h: h.dma_start(out=bc[:, 1, :, :], in_=CTs[:, :, rm0:rm0 + 128].rearrange("g n t -> n g t")),
                          reads=[("BC", ti, cc) for cc in range(40, 48)], writes=[(bck, 1)], dma=True)
                    P.add("sp", lambda h: h.dma_start(out=sz[:], in_=SZs[rm0:rm0 + 128, :]),
                          reads=[("SZs", ti, i, ci % 4) for i in range(8)], writes=[szk], dma=True)
                def mm_ac(h):
                    h.matmul(pa[:, 0:64], Tm, dat, start=True, stop=True)
                    return h.matmul(pa[:, 64:128], ones32, dat, start=True, stop=True)
                P.add("pe", mm_ac, reads=[(dk, 1), "par"], writes=["pa"])
                acs, eac, ela, dws, wsv = (sm[:, i, :] for i in range(5))
                c.update(eac=eac, ela=ela)
                P.add("dve", lambda h: h.tensor_copy(out=acs, in_=pa[:, 0:64]), reads=["pa"], writes=[(smk, 0), "pa"])
                P.add("act", lambda h: h.activation(out=eac, in_=acs, func=AF.Exp), reads=[(smk, 0)], writes=[(smk, 1)])
                P.add("act", lambda h: h.activation(out=ela, in_=pa[:, 64:128], func=AF.Exp), reads=["pa"], writes=[(smk, 2), "pa"])
                P.add("dve", lambda h: h.tensor_tensor(out=dws, in0=pa[:, 64:128], in1=acs, op=ALU.subtract), reads=["pa", (smk, 0)], writes=[(smk, 3), "pa"])
                P.add("act", lambda h: h.activation(out=wsv, in_=dws, func=AF.Exp), reads=[(smk, 3)], writes=[(smk, 4)])
                if c["main"]:
                    P.add("dve", lambda h: h.tensor_tensor(out=v64(xd[:]), in0=v64(xt[:, 0:4096]), in1=b64(dtt), op=ALU.mult), reads=[xk, (dk, 0)], writes=[xdk])
                P.add("dve", lambda h: h.tensor_tensor(out=wsv, in0=wsv, in1=dtt, op=ALU.mult), reads=[(smk, 4), (dk, 0)], writes=[(smk, 4)])
                P.add("dve" if c["main"] else "pool", lambda h: h.tensor_tensor(out=v64(xw[:]), in0=v64(xt[:, 0:4096]), in1=b64(wsv), op=ALU.mult), reads=[xk, (smk, 4)], writes=[xwk])
                if c["main"]:
                    c["ss"], c["ssk"] = ssr.next()
                    cb8, cb8k = cbr.next()
                    c.update(cb8=cb8, cb8k=cb8k)
                    for half in range(2):
                        pcb, pcbk = pcbr.next()
                        def mm_cb(h, pcb=pcb, half=half):
                            for gg in range(4):
                                g = half * 4 + gg
                                ins = h.matmul(pcb[:, gg * 128:(gg + 1) * 128], bc[:, 0, g, :], bc[:, 1, g, :], start=True, stop=True)
                            return ins
                        P.add("pe", mm_cb, reads=[(bck, 0), (bck, 1)], writes=[pcbk])
                        P.add("dve", lambda h, pcb=pcb, half=half: h.tensor_tensor(
                            out=cb8[:, half * 4:(half + 1) * 4, :], in0=pcb[:].rearrange("p (g l) -> p g l", g=4),
                            in1=Tm.unsqueeze(1).to_broadcast([128, 4, 128]), op=ALU.mult), reads=[pcbk, "par"], writes=[(cb8k, half)])
                return c

            def stageA(c, g):
                bc, bck, dat, dk = c["bc"], c["bck"], c["dat"], c["dk"]
                du, duk = dur.next(); de, dek = der.next()
                cbm = c["cb8"][:, g, :]; cbk = (c["cb8k"], g // 4)
                P.add("dve", lambda h: h.tensor_tensor(out=du[:], in0=Um.unsqueeze(1).to_broadcast([128, 8, 128]),
                                                       in1=dat[:, g * 8:(g + 1) * 8].unsqueeze(2).to_broadcast([128, 8, 128]), op=ALU.mult),
                      reads=[(dk, 1), "par"], writes=[duk])
                def mm_seg(h):
                    for j in range(8):
                        ins = h.matmul(pseg[:, j * 128:(j + 1) * 128], du[:, j, :], Tm, start=True, stop=True)
                    return ins
                P.add("pe", mm_seg, reads=[duk, "par"], writes=["pseg"])
                P.add("act", lambda h: h.activation(out=de[:], in_=pseg[:], func=AF.Exp), reads=["pseg"], writes=[dek])
                return dict(cbm=cbm, cbk=cbk, de=de, dek=dek)

            def stageB1(c, gc, g):
                bc, bck, xd, xdk = c["bc"], c["bck"], c["xd"], c["xdk"]
                cbm, cbk, de, dek = gc["cbm"], gc["cbk"], gc["de"], gc["dek"]
                mt, mtk = mtr.next()
                P.add("dve", lambda h: h.tensor_tensor(out=mt[:], in0=de[:].rearrange("p (j l) -> p j l", j=8), in1=cbm.unsqueeze(1).to_broadcast([128, 8, 128]), op=ALU.mult),
                      reads=[dek, cbk], writes=[mtk])
                def mm_y(h):
                    for j in range(8):
                        c0 = (g * 8 + j) * 64
                        ins = h.matmul(py[:, j * 64:(j + 1) * 64], mt[:, j, :], xd[:, c0:c0 + 64], start=True, stop=True)
                    return ins
                P.add("pe", lambda h: h.matmul(po[:], bc[:, 1, g, :], stb[:, g, :], start=True, stop=True), reads=[(bck, 1), ("stb", g)], writes=["po"])
                P.add("pe", mm_y, reads=[mtk, xdk], writes=["py"])

            def stageB2(c, g):
                xt, xk, sz, szk = c["xt"], c["xk"], c["sz"], c["szk"]
                eac, smk, ss, ssk = c["eac"], c["smk"], c["ss"], c["ssk"]
                t1, t1k = t1r.next(); t2, t2k = t2r.next(); t3, t3k = t3r.next()
                gs = slice(g * 512, (g + 1) * 512)
                P.add("dve", lambda h: h.tensor_tensor(out=v8(t1[:]), in0=v8(po[:]), in1=b8(eac[:, g * 8:(g + 1) * 8]), op=ALU.mult), reads=["po", (smk, 1)], writes=[t1k])
                P.add("pool", lambda h: h.tensor_tensor(out=v8(t3[:]), in0=v8(xt[:, gs]), in1=b8(dsk[:, g * 8:(g + 1) * 8]), op=ALU.mult), reads=[xk, "par"], writes=[t3k])
                P.add("dve", lambda h: h.tensor_tensor(out=t2[:], in0=py[:], in1=t1[:], op=ALU.add), reads=["py", t1k], writes=[t2k])
                P.add("pool", lambda h: h.tensor_tensor(out=t3[:], in0=t2[:], in1=t3[:], op=ALU.add), reads=[t2k, t3k], writes=[t3k])
                P.add("pool", lambda h: h.tensor_tensor(out=ygf[:, gs], in0=t3[:], in1=sz[:, gs], op=ALU.mult), reads=[t3k, szk], writes=[("ygf", g)])
                P.add("act", lambda h: h.activation(out=jk[:], in_=ygf[:, gs], func=AF.Square, accum_out=ss[:, g:g + 1]), reads=[("ygf", g)], writes=["bjunk", (ssk, g)])

            def finish(c):
                ss, ssk, rm0, ci = c["ss"], c["ssk"], c["rm0"], c["ci"]
                yn, ynk = ynr.next()
                P.add("dve", lambda h: h.tensor_reduce(out=ss[:, 8:9], in_=ss[:, 0:8], axis=mybir.AxisListType.X, op=ALU.add), reads=[(ssk, g) for g in range(8)], writes=[(ssk, 8)])
                P.add("dve", lambda h: h.tensor_scalar(out=ss[:, 9:10], in0=ss[:, 8:9], scalar1=1.0 / 4096, scalar2=EPS, op0=ALU.mult, op1=ALU.add), reads=[(ssk, 8)], writes=[(ssk, 9)])
                P.add("act", lambda h: h.activation(out=ss[:, 10:11], in_=ss[:, 9:10], func=AF.Sqrt), reads=[(ssk, 9)], writes=[(ssk, 10)])
                P.add("dve", lambda h: h.reciprocal(out=ss[:, 11:12], in_=ss[:, 10:11]), reads=[(ssk, 10)], writes=[(ssk, 11)])
                P.add("dve", lambda h: h.tensor_scalar(out=yn[:], in0=ygf[:], scalar1=ss[:, 11:12], scalar2=None, op0=ALU.mult), reads=[("ygf", g) for g in range(8)] + [(ssk, 11)], writes=[ynk])
                P.add("pool", lambda h: h.dma_start(out=Yn[rm0:rm0 + 128, :], in_=yn[:]), reads=[ynk], writes=[("Yn", ci)], dma=True)

            def state_update(c):
                xt, xk, xw, xwk, ela, smk, ci = c["xt"], c["xk"], c["xw"], c["xwk"], c["ela"], c["smk"], c["ci"]
                for g in range(DBG_SU):
                    gs = slice(g * 512, (g + 1) * 512)
                    P.add("pe", lambda h, g=g, gs=gs: h.matmul(pst[:], xt[:, 4096 + g * 128:4096 + (g + 1) * 128], xw[:, gs], start=True, stop=True), reads=[xk, xwk], writes=["pst"])
                    P.add("dve", lambda h, g=g: h.tensor_tensor(out=v8(stt[:, g, :]), in0=v8(stt[:, g, :]), in1=b8(ela[:, g * 8:(g + 1) * 8]), op=ALU.mult), reads=[("st", g), (smk, 2)], writes=[("st", g)])
                    P.add("dve", lambda h, g=g: h.tensor_tensor(out=stt[:, g, :], in0=pst[:], in1=stt[:, g, :], op=ALU.add), reads=["pst", ("st", g)], writes=[("st", g)])
                    if ci == 31:
                        P.add("dve", lambda h, g=g: h.tensor_scalar(out=stt[:, g, :], in0=stt[:, g, :], scalar1=par[:, P_PV:P_PV + 1], scalar2=None, op0=ALU.mult), reads=[("st", g), "par"], writes=[("st", g)])
                    if ci >= 31 and ci < 63:
                        P.add("act", lambda h, g=g: h.copy(out=stb[:, g, :], in_=stt[:, g, :]), reads=[("st", g)], writes=[("stb", g)])

            chunks = list(DBG_CHUNKS if DBG_CHUNKS is not None else range(64))
            ctxs = {}
            if chunks:
                ctxs[0] = header(chunks[0])
            for idx, ci in enumerate(chunks):
                c = ctxs.pop(idx)
                if idx + 1 < len(chunks):
                    ctxs[idx + 1] = header(chunks[idx + 1])
                if c["main"]:
                    gcs = {0: stageA(c, 0)}
                    for g in range(8):
                        if g + 1 < 8:
                            gcs[g + 1] = stageA(c, g + 1)
                        if g >= 1:
                            stageB2(c, g - 1)
                        stageB1(c, gcs.pop(g), g)
                    stageB2(c, 7)
                    state_update(c)
                    finish(c)
                else:
                    state_update(c)

        if stop == "p1b":
            P.emit(nc, es, {})
            return nc
        P.barrier()
        with ExitStack() as e2:
            s2 = lambda name, shape, dt=F32: e2.enter_context(nc.sbuf_tensor(name, shape, dt))
            p2 = lambda name, shape, dt=F32: e2.enter_context(nc.psum_tensor(name, shape, dt))
            qtr = Ring([(s2("cq%d" % i, [128, 2048], BF16), "cq%d" % i) for i in range(2)])
            ktr = Ring([(s2("ck%d" % i, [128, 4096], BF16), "ck%d" % i) for i in range(2)])
            zar = Ring([(s2("cz%d" % i, [128, 2048], BF16), "cz%d" % i) for i in range(2)])
            vpr = Ring([(s2("cv%d" % i, [128, 3, 32, 128], BF16), "cv%d" % i) for i in range(2)])
            bir = Ring([(s2("cb%d" % i, [128, 2, 768]), "cb%d" % i) for i in range(1)])
            nar = Ring([(s2("cn%d" % i, [128, 2048]), "cn%d" % i) for i in range(2)])
            dar = Ring([(s2("cd%d" % i, [128, 2048]), "cd%d" % i) for i in range(2)])
            ssb = Ring([(s2("cs%d" % i, [128, 1024], BF16), "cs%d" % i) for i in range(2)])
            bbr = Ring([(s2("cbb%d" % i, [128, 2, 768], BF16), "cbb%d" % i) for i in range(2)])
            ptb = Ring([(s2("cp%d" % i, [128, 1024], BF16), "cp%d" % i) for i in range(2)])
            oar = Ring([(s2("co%d" % i, [128, 2048], BF16), "co%d" % i) for i in range(2)])
            pSr = Ring([(p2("cpS%d" % i, [128, 1024]), "cpS%d" % i) for i in range(2)])
            pOr = Ring([(p2("cpO%d" % i, [128, 512]), "cpO%d" % i) for i in range(2)])
            pDr = Ring([(p2("cpD%d" % i, [128, 512]), "cpD%d" % i) for i in range(2)])

            cv2_tick, cv2_flush = make_converter(s2, [] if DBG_SKIP_P0 else P2_JOBS, "b")

            def strided(ap, start, step, n=128):
                return ap[:, start:start + step * (n - 1) + 1:step]

            def do_head(st_, hd, w0, tiles_q, tiles_k):
                qt, qk = qtr.next(); kt, kk = ktr.next(); za, zk = zar.next(); vp, vk = vpr.next(); bi, bk = bir.next()
                na, nk = nar.next(); da, dak = dar.next(); oa, ok = oar.next()
                P.add("sp", lambda h, qt=qt, hd=hd, w0=w0: h.dma_start(out=qt[:], in_=Qs[hd, :, w0:w0 + 2048]), reads=[("Qs", t, hd) for t in tiles_q], writes=[qk], dma=True)
                P.add("sp", lambda h, kt=kt, hd=hd, w0=w0: h.dma_start(out=kt[:], in_=Ks[hd, :, w0:w0 + 4096]), reads=[("Ks", t, hd) for t in tiles_k], writes=[kk], dma=True)
                P.add("sp", lambda h, za=za, hd=hd, w0=w0: h.dma_start(out=za[:], in_=SZa[hd, :, w0:w0 + 2048]), reads=[("SZa", t, hd) for t in tiles_q], writes=[zk], dma=True)
                vkeys = [("Vs", t, hd // 4, b) for t in tiles_k for b in range(4)]
                vsrc = Vs[w0:w0 + 4096, hd * 128:(hd + 1) * 128]
                P.add("sp", lambda h, vp=vp, vsrc=vsrc: h.dma_start(out=vp[:, 0, :, :], in_=vsrc.rearrange("(b p) e -> p b e", p=128)), reads=vkeys, writes=[(vk, 0)], dma=True)
                for r in range(4):
                    P.add("sp", lambda h, vp=vp, vsrc=vsrc, r=r: h.dma_start(
                        out=vp[:, 1, r * 8:(r + 1) * 8, :], in_=vsrc.rearrange("(i p r) e -> p r i e", p=128, r=4)[:, r, :, :]), reads=vkeys, writes=[(vk, 1, r)], dma=True)
                for r in range(16):
                    P.add("sp", lambda h, vp=vp, vsrc=vsrc, r=r: h.dma_start(
                        out=vp[:, 2, r * 2:(r + 1) * 2, :], in_=vsrc.rearrange("(i p r) e -> p r i e", p=128, r=16)[:, r, :, :]), reads=vkeys, writes=[(vk, 2, r)], dma=True)
                vallk = [(vk, 0)] + [(vk, 1, r) for r in range(4)] + [(vk, 2, r) for r in range(16)]
                P.add("sp", lambda h, bi=bi, hd=hd: h.dma_start(out=bi[:, 0, :], in_=biasA[hd]), writes=[(bk, 0)], dma=True)
                P.add("sp", lambda h, bi=bi, hd=hd: h.dma_start(out=bi[:, 1, :], in_=bias0[hd]), writes=[(bk, 1)], dma=True)
                bib, bbk = bbr.next()
                P.add("act", lambda h, bi=bi, bib=bib: h.copy(out=bib[:], in_=bi[:]), reads=[(bk, 0), (bk, 1)], writes=[bbk])
                def stageS(pi, d, gq):
                    units = []
                    for u in range(4):
                        if d == 1:
                            qb = 16 + 4 * gq + u
                            kp = kt[:, (qb - 1) * 128:qb * 128]; kc_ = kt[:, qb * 128:(qb + 1) * 128]
                            qa = qt[:, (qb - 16) * 128:(qb - 15) * 128]
                            units.append((kp, kc_, qa, qb - 1, qb, qb == 16))
                        elif d == 4:
                            i = 4 + gq; r = u
                            kp = strided(kt, (i - 1) * 512 + r, 4); kc_ = strided(kt, i * 512 + r, 4)
                            qa = strided(qt, (i - 4) * 512 + r, 4)
                            units.append((kp, kc_, qa, r * 8 + i - 1, r * 8 + i, i == 4))
                        else:
                            r = 4 * gq + u
                            kp = strided(kt, r, 16); kc_ = strided(kt, 2048 + r, 16)
                            qa = strided(qt, r, 16)
                            units.append((kp, kc_, qa, r * 2, r * 2 + 1, True))
                    pS, pSk = pSr.next(); sS, sSk = ssb.next(); pT, pTk = ptb.next()
                    def mm_s(h):
                        for u, (kp, kc_, qa, _, _, _) in enumerate(units):
                            h.matmul(pS[:, u * 256:u * 256 + 128], kp, qa, start=True, stop=True)
                            ins = h.matmul(pS[:, u * 256 + 128:u * 256 + 256], kc_, qa, start=True, stop=True)
                        return ins
                    P.add("pe", mm_s, reads=[qk, kk], writes=[pSk])
                    bsl = slice(pi * 256, (pi + 1) * 256)
                    halo_flags = [st_ == 0 and un[5] for un in units]
                    P.add("act", lambda h: h.activation(out=sS[:], in_=pS[:], func=AF.Exp, scale=INV), reads=[pSk], writes=[sSk])
                    if all(halo_flags) or not any(halo_flags):
                        wh = 1 if halo_flags[0] else 0
                        P.add("dve", lambda h: h.tensor_tensor(
                            out=pT[:].rearrange("p (u c) -> p u c", u=4), in0=sS[:].rearrange("p (u c) -> p u c", u=4),
                            in1=bib[:, wh, bsl].unsqueeze(1).to_broadcast([128, 4, 256]), op=ALU.mult),
                            reads=[sSk, bbk], writes=[pTk])
                    else:
                        P.add("dve", lambda h: h.tensor_tensor(out=pT[:, 0:256], in0=sS[:, 0:256], in1=bib[:, 1, bsl], op=ALU.mult),
                              reads=[sSk, bbk], writes=[(pTk, "a")])
                        P.add("dve", lambda h: h.tensor_tensor(
                            out=pT[:, 256:1024].rearrange("p (u c) -> p u c", u=3), in0=sS[:, 256:1024].rearrange("p (u c) -> p u c", u=3),
                            in1=bib[:, 0, bsl].unsqueeze(1).to_broadcast([128, 3, 256]), op=ALU.mult),
                            reads=[sSk, bbk, (pTk, "a")], writes=[pTk])
                    return dict(units=units, pT=pT, pTk=pTk, pi=pi, d=d, gq=gq)

                def stageV(sc):
                    units, pT, pTk, pi, d, gq = sc["units"], sc["pT"], sc["pTk"], sc["pi"], sc["d"], sc["gq"]
                    pO, pOk = pOr.next(); pD, pDk = pDr.next()
                    def mm_pv(h):
                        for u, (_, _, _, vb0, vb1, _) in enumerate(units):
                            h.matmul(pO[:, u * 128:(u + 1) * 128], vp[:, pi, vb0, :], pT[:, u * 256:u * 256 + 128], start=True, stop=False)
                            h.matmul(pO[:, u * 128:(u + 1) * 128], vp[:, pi, vb1, :], pT[:, u * 256 + 128:u * 256 + 256], start=False, stop=True)
                        for u in range(4):
                            h.matmul(pD[:, u * 128:(u + 1) * 128], oneb[:], pT[:, u * 256:u * 256 + 128], start=True, stop=False)
                            ins = h.matmul(pD[:, u * 128:(u + 1) * 128], oneb[:], pT[:, u * 256 + 128:u * 256 + 256], start=False, stop=True)
                        return ins
                    P.add("pe", mm_pv, reads=[pTk, "oneb"] + vallk, writes=[pOk, pDk])
                    if d == 1:
                        c0 = gq * 512
                        P.add("act", lambda h: h.copy(out=na[:, c0:c0 + 512], in_=pO[:]), reads=[pOk], writes=[(nk, gq)])
                        P.add("act", lambda h: h.copy(out=da[:, c0:c0 + 512], in_=pD[:]), reads=[pDk], writes=[(dak, gq)])
                    else:
                        if d == 4:
                            c0 = gq * 512
                            nv = lambda a: a[:, c0:c0 + 512].rearrange("p (m r) -> p r m", r=4)
                        else:
                            nv = lambda a: a[:, :].rearrange("p (m r) -> p r m", r=16)[:, 4 * gq:4 * gq + 4, :]
                        pv_ = lambda a: a[:].rearrange("p (r m) -> p r m", r=4)
                        wk = [(nk, gq)] if d == 4 else [(nk, q) for q in range(4)]
                        wdk = [(dak, gq)] if d == 4 else [(dak, q) for q in range(4)]
                        P.add("dve", lambda h: h.tensor_tensor(out=nv(na), in0=pv_(pO), in1=nv(na), op=ALU.add), reads=[pOk] + wk, writes=wk)
                        P.add("dve", lambda h: h.tensor_tensor(out=nv(da), in0=pv_(pD), in1=nv(da), op=ALU.add), reads=[pDk] + wdk, writes=wdk)

                glist = [(pi, d, gq) for pi, d in enumerate((1, 4, 16)) for gq in range(4)]
                pend = stageS(*glist[0])
                yield
                for gi_ in range(len(glist)):
                    nxt = stageS(*glist[gi_ + 1]) if gi_ + 1 < len(glist) else None
                    stageV(pend)
                    pend = nxt
                yield
                allk = [(nk, q) for q in range(4)]
                alldk = [(dak, q) for q in range(4)]
                P.add("act", lambda h, da=da: h.activation(out=da[:], in_=da[:], func=AF.Ln), reads=alldk, writes=alldk)
                P.add("act", lambda h, da=da: h.activation(out=da[:], in_=da[:], func=AF.Exp, scale=-1.0), reads=alldk, writes=alldk)
                P.add("dve", lambda h, na=na, da=da: h.tensor_tensor(out=na[:], in0=na[:], in1=da[:], op=ALU.mult), reads=allk + alldk, writes=allk)
                P.add("pool", lambda h, na=na, za=za, oa=oa: h.tensor_tensor(out=oa[:], in0=na[:], in1=za[:], op=ALU.mult), reads=allk + [zk], writes=[ok])
                P.add("pool", lambda h, oa=oa, hd=hd, w0=w0: h.dma_start(out=OaT[hd, :, w0:w0 + 2048], in_=oa[:]), reads=[ok], writes=[("OaT", st_, hd)], dma=True)


            gens = []
            for st_ in range(2):
                w0 = st_ * 2048
                tiles_q = [8 + st_ * 4 + i for i in range(4)]
                tiles_k = [4 + st_ * 4 + i for i in range(8)]
                for hd in range(16):
                    gens.append(do_head(st_, hd, w0, tiles_q, tiles_k))
            next(gens[0])
            for gi2, gen in enumerate(gens):
                next(gen)
                if gi2 + 1 < len(gens):
                    next(gens[gi2 + 1])
                for _ in gen:
                    pass
                for _ in range(4):
                    cv2_tick()
            cv2_flush()

        if stop == "p2":
            P.emit(nc, es, {})
            return nc
        P.barrier()
        final_ops = []
        with ExitStack() as e3:
            s3 = lambda name, shape, dt=F32: e3.enter_context(nc.sbuf_tensor(name, shape, dt))
            p3 = lambda name, shape, dt=F32: e3.enter_context(nc.psum_tensor(name, shape, dt))
            wring = Ring([(s3("w3_%d" % i, [128, 16, 512], BF16), "w3_%d" % i) for i in range(3)])
            hT = s3("dhT", [128, 16, 512], BF16)
            oT = s3("doT", [128, 16, 512], BF16)
            yT = s3("dyT", [128, 32, 512], BF16)
            ynb = Ring([(s3("dyn%d" % i, [128, 4096], BF16), "dyn%d" % i) for i in range(2)])
            mT = s3("dmT", [128, 16, 512], BF16)
            xr4 = Ring([(s3("dxr%d" % i, [128, 2048]), "dxr%d" % i) for i in range(2)])
            sga = s3("dsga", [128, 4, 512], BF16)
            sgs = s3("dsgs", [128, 4, 512], BF16)
            m1 = s3("dm1", [128, 4, 512])
            m2r = Ring([(s3("dm2%d" % i, [128, 512]), "dm2%d" % i) for i in range(1)])
            junk3 = s3("djunk", [128, 2048], BF16)
            stat = Ring([(s3("dst%d" % i, [128, 4]), "dst%d" % i) for i in range(2)])
            psb = Ring([(p3("dps%d" % i, [128, 512]), "dps%d" % i) for i in range(6)])
            ptr = Ring([(p3("dpt%d" % i, [128, 1024], BF16), "dpt%d" % i) for i in range(2)])

            for mt_ in range(8):
                tm0 = mt_ * 512
                ti = 8 + mt_
                st_ = mt_ // 4
                P.add("sp", lambda h, tm0=tm0: h.dma_start(out=hT[:], in_=HnT[:, :, tm0:tm0 + T].rearrange("k p t -> p k t")), reads=[("HnT", ti)], writes=["dhT"], dma=True)
                P.add("sp", lambda h, tm0=tm0: h.dma_start(out=oT[:], in_=OaT[:, :, tm0:tm0 + T].rearrange("k p t -> p k t")), reads=[("OaT", st_, hd) for hd in range(16)], writes=["doT"], dma=True)
                for blk in range(4):
                    yb, ybk = ynb.next()
                    r0 = tm0 + blk * 128
                    P.add("sp", lambda h, yb=yb, r0=r0: h.dma_start(out=yb[:], in_=Yn[r0:r0 + 128, :]), reads=[("Yn", 32 + mt_ * 4 + blk)], writes=[ybk], dma=True)
                    for c4 in range(8):
                        pt, pk = ptr.next()
                        def tr(h, pt=pt, yb=yb, c4=c4):
                            for j in range(4):
                                cc = c4 * 4 + j
                                ins = h.transpose(out=pt[:, j * 128:(j + 1) * 128], in_=yb[:, cc * 128:(cc + 1) * 128], identity=idb[:])
                            return ins
                        P.add("pe", tr, reads=[ybk, "idb"], writes=[pk])
                        P.add("dve", lambda h, pt=pt, c4=c4, blk=blk: h.tensor_tensor(
                            out=yT[:, c4 * 4:(c4 + 1) * 4, blk * 128:(blk + 1) * 128], in0=pt[:, 0:512].rearrange("p (j t) -> p j t", j=4),
                            in1=par[:, P_SNW + c4 * 4:P_SNW + (c4 + 1) * 4].unsqueeze(2).to_broadcast([128, 4, 128]), op=ALU.mult),
                            reads=[pk, "par"], writes=[("dyT", blk, c4)])
                yT_keys = [("dyT", b, c) for b in range(4) for c in range(8)]

                def fm(glist, rhs_of, rkeys, consume):
                    wts = [load_w(wring, g) for g in glist]
                    nk = 16 * len(wts)
                    for cb in range(4):
                        ps, pk = psb.next()
                        def mm(h, wts=wts, ps=ps, cb=cb):
                            n = 0
                            for wi, (wt, _) in enumerate(wts):
                                for kc in range(16):
                                    ins = h.matmul(ps[:], wt[:, kc, cb * 128:(cb + 1) * 128], rhs_of(wi * 16 + kc), start=(n == 0), stop=(n == nk - 1))
                                    n += 1
                            return ins
                        P.add("pe", mm, reads=[k for _, ks in wts for k in ks] + rkeys, writes=[pk])
                        consume(cb, ps, pk)

                for dg in range(4):
                    def c_ga(cb, ps, pk):
                        P.add("act", lambda h: h.activation(out=sga[:, cb, :], in_=ps[:], func=AF.Sigmoid), reads=[pk], writes=[("dsga", cb)])
                    def c_gs(cb, ps, pk):
                        P.add("act", lambda h: h.activation(out=sgs[:, cb, :], in_=ps[:], func=AF.Sigmoid), reads=[pk], writes=[("dsgs", cb)])
                    def c_a(cb, ps, pk):
                        P.add("dve", lambda h: h.tensor_tensor(out=m1[:, cb, :], in0=ps[:], in1=sga[:, cb, :], op=ALU.mult), reads=[pk, ("dsga", cb)], writes=[("dm1", cb)])
                    def c_b(cb, ps, pk, dg=dg):
                        m2, m2k = m2r.next()
                        P.add("dve", lambda h: h.tensor_tensor(out=m2[:], in0=ps[:], in1=sgs[:, cb, :], op=ALU.mult), reads=[pk, ("dsgs", cb)], writes=[m2k])
                        P.add("pool", lambda h: h.tensor_tensor(out=mT[:, dg * 4 + cb, :], in0=m2[:], in1=m1[:, cb, :], op=ALU.add), reads=[m2k, ("dm1", cb)], writes=[("dmT", dg * 4 + cb)])
                    fm([G_GA[dg]], lambda k: hT[:, k, :], ["dhT"], c_ga)
                    fm([G_GS[dg]], lambda k: hT[:, k, :], ["dhT"], c_gs)
                    fm([G_WA[dg]], lambda k: oT[:, k, :], ["doT"], c_a)
                    fm(G_WS[dg], lambda k: yT[:, k, :], yT_keys, c_b)
                mT_keys = [("dmT", k) for k in range(16)]
                for blk in range(4):
                    xb, xbk = xr4.next()
                    r0 = tm0 + blk * 128
                    P.add("sp", lambda h, xb=xb, r0=r0: h.dma_start(out=xb[:], in_=xe[4096 + r0:4096 + r0 + 128, :]), writes=[xbk], dma=True)
                    for cg in range(4):
                        wt, wkeys = load_w(wring, G_WO[cg])
                        ps, pk = psb.next()
                        def mm(h, wt=wt, ps=ps, blk=blk):
                            for kc in range(16):
                                ins = h.matmul(ps[:], mT[:, kc, blk * 128:(blk + 1) * 128], wt[:, kc, :], start=(kc == 0), stop=(kc == 15))
                            return ins
                        P.add("pe", mm, reads=wkeys + mT_keys, writes=[pk])
                        P.add("dve", lambda h, ps=ps, xb=xb, cg=cg: h.tensor_tensor(out=xb[:, cg * 512:(cg + 1) * 512], in0=ps[:], in1=xb[:, cg * 512:(cg + 1) * 512], op=ALU.add),
                              reads=[pk, xbk], writes=[xbk])
                    stt_, sk = stat.next()
                    P.add("act", lambda h, stt_=stt_, xb=xb: h.activation(out=junk3[:], in_=xb[:], func=AF.Square, accum_out=stt_[:, 0:1]),
                          reads=[xbk], writes=["djunk", (sk, 0)])
                    P.add("dve", lambda h, stt_=stt_: h.tensor_scalar(out=stt_[:, 1:2], in0=stt_[:, 0:1], scalar1=1.0 / D, scalar2=EPS, op0=ALU.mult, op1=ALU.add), reads=[(sk, 0)], writes=[(sk, 1)])
                    P.add("act", lambda h, stt_=stt_: h.activation(out=stt_[:, 2:3], in_=stt_[:, 1:2], func=AF.Sqrt), reads=[(sk, 1)], writes=[(sk, 2)])
                    P.add("dve", lambda h, stt_=stt_: h.reciprocal(out=stt_[:, 3:4], in_=stt_[:, 2:3]), reads=[(sk, 2)], writes=[(sk, 3)])
                    P.add("dve", lambda h, stt_=stt_, xb=xb: h.scalar_tensor_tensor(out=xb[:], in0=xb[:], scalar=stt_[:, 3:4], in1=par[:, P_FNW:P_FNW + 2048], op0=ALU.mult, op1=ALU.mult),
                          reads=[xbk, (sk, 3), "par"], writes=[xbk])
                    final_ops.append(P.add("pool", lambda h, xb=xb, r0=r0: h.dma_start(out=out[r0:r0 + 128, :], in_=xb[:]), reads=[xbk], dma=True))

        P.emit(nc, es, {"pool": final_ops})
    return nc


_CACHE = {}


def _alibi_tables():
    tabs = np.zeros((16, 128, 768), np.float32)
    k = np.arange(128)[:, None]
    q = np.arange(128)[None, :]
    for hd in range(16):
        slope = np.float32(2.0 ** (-8.0 * (hd + 1) / 16))
        for pi, d in enumerate((1, 4, 16)):
            dist_p = (q - k + 128).astype(np.float32)
            prev = np.where(k >= q, -slope * dist_p * d, NEG)
            dist_c = (q - k).astype(np.float32)
            cur = np.where(k <= q, -slope * dist_c * d, NEG)
            tabs[hd, :, pi * 256:pi * 256 + 128] = prev
            tabs[hd, :, pi * 256 + 128:pi * 256 + 256] = cur
    return np.exp(tabs.astype(np.float64)).astype(np.float32)


def kernel(x, norm_w, w_in, conv_w, conv_b, dt_bias, a_log, d_skip, ssm_norm_w,
           w_attn_branch, w_ssm_branch, w_out, final_norm_w):
    f = lambda a: np.ascontiguousarray(np.asarray(a, dtype=np.float32))
    x = f(x)
    if "nc" not in _CACHE:
        _CACHE["nc"] = build_program()
    nc = _CACHE["nc"]
    par = np.zeros((128, NPAR), np.float32)
    par[:, P_NORMW:P_NORMW + 2048] = f(norm_w)[0][None, :]
    par[:, P_FNW:P_FNW + 2048] = f(final_norm_w)[None, :]
    cw = f(conv_w)[0]
    par[:, P_CW:P_CW + 192] = cw.reshape(4, 48, 128).transpose(2, 1, 0).reshape(128, 192)
    par[:, P_CB:P_CB + 48] = f(conv_b)[0].reshape(48, 128).T
    par[:, P_SNW:P_SNW + 32] = f(ssm_norm_w)[0].reshape(32, 128).T
    par[:, P_DTB:P_DTB + 64] = f(dt_bias)[0][None, :]
    par[:, P_ALOG:P_ALOG + 64] = f(a_log)[0][None, :]
    par[:, P_DSK:P_DSK + 64] = f(d_skip)[0][None, :]
    par[:, P_ID:P_ID + 128] = np.eye(128, dtype=np.float32)
    ki = np.arange(128)
    par[:, P_TM:P_TM + 128] = (ki[:, None] <= ki[None, :]).astype(np.float32)
    par[:, P_UM:P_UM + 128] = (ki[:, None] > ki[None, :]).astype(np.float32)
    par[:, P_ONE:P_ONE + 128] = 1.0
    tabs = _alibi_tables()
    tabs0 = tabs.copy()
    for pi in range(3):
        tabs0[:, :, pi * 256:pi * 256 + 128] = 0.0
    wi, wa, wss, wo = f(w_in)[0], f(w_attn_branch)[0], f(w_ssm_branch)[0], f(w_out)[0]
    in_maps = []
    for c in range(NCORE):
        b, hf = c // 2, c % 2
        if hf == 0:
            xe = np.concatenate([np.zeros((4096, D), np.float32), x[b, :4096]], axis=0)
        else:
            xe = x[b]
        p = par.copy()
        p[:, P_PV] = float(hf)
        in_maps.append({"xe": np.ascontiguousarray(xe), "w_in": wi, "w_attn": wa, "w_ssm": wss, "w_out": wo,
                        "params": p, "biasA": tabs, "bias0": tabs if hf == 1 else tabs0})
    if _CACHE.get("return_maps"):
        return in_maps
    res = run_bass_kernel_spmd(nc, in_maps, core_ids=list(range(NCORE)))
    outp = np.empty((4, SEQ, D), np.float32)
    for c in range(NCORE):
        b, hf = c // 2, c % 2
        outp[b, hf * 4096:(hf + 1) * 4096] = res.results[c]["out"]
    return outp
```

```python
import numpy as np
import ml_dtypes
from contextlib import ExitStack
import concourse.bass as bass
import concourse.mybir as mybir
from concourse.bass_utils import run_bass_kernel_spmd

F32 = mybir.dt.float32
BF16 = mybir.dt.bfloat16
AF = mybir.ActivationFunctionType
ALU = mybir.AluOpType

D = 2048
NIN = 22592
SEQ = 8192
NCORE = 8
TOKM = 4096
EXT = 8192
T = 512
NT = EXT // T
HALO0 = 2048
EPS = 1e-6
NEG = -30000.0
DBG_CHUNKS = None
DBG_GROUPS = 8
DBG_SU = 8
DBG_LIMIT = None
DBG_F = None
DBG_SKIP_P0 = False

C_Q, C_K, C_V, C_ZA, C_ZS, C_X, C_B, C_C, C_DT, C_GA, C_GS = 0, 2048, 4096, 6144, 8192, 12288, 16384, 17408, 18432, 18496, 20544

WG = []
def _add(name, src, row0, col0, width):
    WG.append((name, src, row0, col0, width))
    return len(WG) - 1
G_Q = [_add("q%d" % i, "w_in", 0, C_Q + 512 * i, 512) for i in range(4)]
G_K = [_add("k%d" % i, "w_in", 0, C_K + 512 * i, 512) for i in range(4)]
G_V = [_add("v%d" % i, "w_in", 0, C_V + 512 * i, 512) for i in range(4)]
G_ZA = [_add("za%d" % i, "w_in", 0, C_ZA + 512 * i, 512) for i in range(4)]
G_ZS = [_add("zs%d" % i, "w_in", 0, C_ZS + 512 * i, 512) for i in range(8)]
G_XBC = [_add("xbc%d" % i, "w_in", 0, C_X + 512 * i, 512) for i in range(12)]
G_DT = _add("dt", "w_in", 0, C_DT, 64)
G_GA = [_add("ga%d" % i, "w_in", 0, C_GA + 512 * i, 512) for i in range(4)]
G_GS = [_add("gs%d" % i, "w_in", 0, C_GS + 512 * i, 512) for i in range(4)]
G_WA = [_add("wa%d" % i, "w_attn", 0, 512 * i, 512) for i in range(4)]
G_WS = [[_add("ws%d_%d" % (i, kh), "w_ssm", 2048 * kh, 512 * i, 512) for kh in range(2)] for i in range(4)]
G_WO = [_add("wo%d" % i, "w_out", 0, 512 * i, 512) for i in range(4)]
NG = len(WG)
P0_GROUPS = G_XBC + [G_DT]
P1A_JOBS = [(g, q) for g in (G_K + G_V + G_Q + G_ZA + G_ZS + G_GA + G_GS + G_WA + [x for pr in G_WS for x in pr] + G_WO) for q in range(4)]
P2_JOBS = []

P_NORMW = 0
P_FNW = 2048
P_CW = 4096
P_CB = P_CW + 192
P_SNW = P_CB + 48
P_DTB = P_SNW + 32
P_ALOG = P_DTB + 64
P_DSK = P_ALOG + 64
P_PV = P_DSK + 64
P_ID = P_PV + 1
P_TM = P_ID + 128
P_UM = P_TM + 128
P_ONE = P_UM + 128
NPAR = P_ONE + 128


class Op:
    __slots__ = ("eng", "fn", "deps", "dma", "sem", "val", "has_dep", "cnt", "semi")

    def __init__(self, eng, fn, dma):
        self.eng = eng; self.fn = fn; self.dma = dma; self.deps = []
        self.sem = None; self.val = 0; self.has_dep = False; self.cnt = 0; self.semi = 0


class Prog:
    ENGS = ("pe", "act", "dve", "pool", "sp")
    NDS = 14
    CHUNK = 20000

    def __init__(self):
        self.ops = {e: [] for e in self.ENGS}
        self.res = {}
        self.ndma = {e: 0 for e in self.ENGS}
        self.pend = {}

    def barrier(self):
        deps = []
        for e in self.ENGS:
            last_c = None
            dm = {}
            for o in self.ops[e]:
                if o.dma:
                    dm[o.semi] = o
                else:
                    last_c = o
            if last_c is not None:
                deps.append(last_c)
            deps.extend(dm.values())
        self.pend = {e: list(deps) for e in self.ENGS}
        print("[barrier] ops so far:", getattr(self, "nadd", 0))

    def add(self, eng, fn, reads=(), writes=(), dma=False):
        op = Op(eng, fn, dma)
        self.nadd = getattr(self, "nadd", 0) + 1
        if DBG_LIMIT is not None and self.nadd > DBG_LIMIT:
            return op
        deps = list(self.pend.pop(eng, []))
        for k in reads:
            st = self.res.get(k)
            if st is not None and st[0] is not None:
                deps.append(st[0])
        for k in writes:
            st = self.res.get(k)
            if st is not None:
                if st[0] is not None:
                    deps.append(st[0])
                deps.extend(st[1].values())
                deps.extend(st[2])
        seen = set()
        for d in deps:
            if d is op or id(d) in seen:
                continue
            seen.add(id(d))
            if (not d.dma) and d.eng == eng and eng == "pe":
                continue
            op.deps.append(d)
            d.has_dep = True
        for k in reads:
            st = self.res.setdefault(k, [None, {}, []])
            if dma:
                st[2].append(op)
            else:
                st[1][eng] = op
        for k in writes:
            self.res[k] = [op, {}, []]
        if dma:
            n = self.ndma[eng]
            self.ndma[eng] = n + 1
            op.semi = n % self.NDS
            op.val = 16 * (n // self.NDS + 1)
        self.ops[eng].append(op)
        return op

    def emit(self, nc, es, final_waits):
        csem = {}
        for e in ("pe", "act", "dve", "pool"):
            nd = sum(1 for o in self.ops[e] if (not o.dma) and o.has_dep)
            csem[e] = [es.enter_context(nc.semaphore("c_%s_%d" % (e, i))) for i in range(nd // self.CHUNK + 1)]
            c = 0
            for o in self.ops[e]:
                if (not o.dma) and o.has_dep:
                    o.semi = c // self.CHUNK
                    o.cnt = c % self.CHUNK + 1
                    c += 1
        dsem = {}
        for e in self.ENGS:
            if self.ndma[e]:
                dsem[e] = [es.enter_context(nc.semaphore("d_%s_%d" % (e, i))) for i in range(self.NDS)]
        block = es.enter_context(nc.Block())
        prog = self

        wlog = self.wlog = []
        def run(e, h):
            seen = {}
            def wait(sem, val):
                key = id(sem)
                if seen.get(key, 0) >= val:
                    return
                seen[key] = val
                if DBG_LIMIT is not None:
                    wlog.append((e, str(getattr(sem, "name", sem)), val))
                h.wait_ge(sem, val)
            for o in prog.ops[e]:
                for d in o.deps:
                    if d.dma:
                        wait(dsem[d.eng][d.semi], d.val)
                    else:
                        wait(csem[d.eng][d.semi], d.cnt)
                if o.dma:
                    if o.val > 16:
                        wait(dsem[e][o.semi], o.val - 16)
                    ins = o.fn(h)
                    ins.then_inc(dsem[e][o.semi], 16)
                else:
                    ins = o.fn(h)
                    if o.has_dep:
                        ins.then_inc(csem[e][o.semi], 1)
            if e in final_waits:
                for d in final_waits[e]:
                    wait(dsem[d.eng][d.semi], d.val)

        @block.tensor
        def _(h):
            run("pe", h)

        @block.scalar
        def _(h):
            run("act", h)

        @block.vector
        def _(h):
            run("dve", h)

        @block.gpsimd
        def _(h):
            run("pool", h)

        @block.sync
        def _(h):
            run("sp", h)


class Ring:
    def __init__(self, items):
        self.items = items; self.i = 0

    def next(self):
        it = self.items[self.i % len(self.items)]
        self.i += 1
        return it


def build_program(stop=None, dbg=False, tiles=None):
    nc = bass.Bass("TRN2", target_bir_lowering=False)
    P = Prog()

    def din(name, shape, dt=F32):
        return nc.dram_tensor(name, shape, dt, kind="ExternalInput").ap()

    def dscr(name, shape, dt=BF16):
        return nc.dram_tensor(name, shape, dt, kind="ExternalOutput" if dbg else "Internal").ap()

    xe = din("xe", [EXT, D])
    wsrc = {"w_in": din("w_in", [D, NIN]), "w_attn": din("w_attn", [2048, D]),
            "w_ssm": din("w_ssm", [4096, D]), "w_out": din("w_out", [D, D])}
    params = din("params", [128, NPAR])
    biasA = din("biasA", [16, 128, 768])
    bias0 = din("bias0", [16, 128, 768])
    out = nc.dram_tensor("out", [TOKM, D], F32, kind="ExternalOutput").ap()

    wb = dscr("wb", [NG, 128, 16 * 512])
    Qs = dscr("Qs", [16, 128, TOKM])
    Ks = dscr("Ks", [16, 128, 6144])
    Vs = dscr("Vs", [6144, 2048])
    SZa = dscr("SZa", [16, 128, TOKM])
    SZs = dscr("SZs", [TOKM, 4096])
    HnT = dscr("HnT", [16, 128, TOKM])
    Xtok = dscr("Xtok", [EXT, 5120])
    BTs = dscr("BTs", [8, 128, TOKM])
    CTs = dscr("CTs", [8, 128, TOKM])
    DTs = dscr("DTs", [EXT, 64], F32)
    DAs = dscr("DAs", [EXT, 64], F32)
    Yn = dscr("Yn", [TOKM, 4096])
    OaT = dscr("OaT", [16, 128, TOKM])

    with ExitStack() as es:
        sb = lambda name, shape, dt=F32: es.enter_context(nc.sbuf_tensor(name, shape, dt))
        par = sb("par", [128, NPAR])
        idb = sb("idb", [128, 128], BF16)
        oneb = sb("oneb", [128, 128], BF16)
        anegt = sb("anegt", [128, 64])
        P.add("sp", lambda h: h.dma_start(out=par[:], in_=params[:, :]), writes=["par"], dma=True)
        P.add("dve", lambda h: h.tensor_copy(out=idb[:], in_=par[:, P_ID:P_ID + 128]), reads=["par"], writes=["idb"])
        P.add("dve", lambda h: h.tensor_copy(out=oneb[:], in_=par[:, P_ONE:P_ONE + 128]), reads=["par"], writes=["oneb"])
        P.add("act", lambda h: h.activation(out=anegt[:], in_=par[:, P_ALOG:P_ALOG + 64], func=AF.Exp), reads=["par"], writes=["aneg0"])
        P.add("dve", lambda h: h.tensor_scalar(out=anegt[:], in0=anegt[:], scalar1=-1.0, scalar2=None, op0=ALU.mult), reads=["aneg0"], writes=["aneg"])
        Tm = par[:, P_TM:P_TM + 128]
        Um = par[:, P_UM:P_UM + 128]
        ones32 = par[:, P_ONE:P_ONE + 128]

        with ExitStack() as e0:
            s0 = lambda name, shape, dt=F32: e0.enter_context(nc.sbuf_tensor(name, shape, dt))
            wf = [s0("wf%d" % i, [128, 16, 512]) for i in range(2)]
            wc = [s0("wc%d" % i, [128, 16, 512], BF16) for i in range(2)]
            cast_engs = ["act", "dve"]
            for g in ([] if DBG_SKIP_P0 else P0_GROUPS):
                name, src, row0, col0, width = WG[g]
                sl = g % 2
                srcap = wsrc[src][row0:row0 + 2048, col0:col0 + width].rearrange("(kc p) c -> p kc c", p=128)
                for hlf in range(2):
                    P.add("sp", lambda h, sl=sl, srcap=srcap, hlf=hlf, width=width: h.dma_start(
                        out=wf[sl][:, hlf * 8:(hlf + 1) * 8, 0:width], in_=srcap[:, hlf * 8:(hlf + 1) * 8, :]),
                        writes=[("wf", sl, hlf)], dma=True)
                for hlf in range(2):
                    ce = cast_engs[(g + hlf) % 2]
                    if ce == "act":
                        fn = lambda h, sl=sl, hlf=hlf, width=width: h.copy(out=wc[sl][:, hlf * 8:(hlf + 1) * 8, 0:width], in_=wf[sl][:, hlf * 8:(hlf + 1) * 8, 0:width])
                    else:
                        fn = lambda h, sl=sl, hlf=hlf, width=width: h.tensor_copy(out=wc[sl][:, hlf * 8:(hlf + 1) * 8, 0:width], in_=wf[sl][:, hlf * 8:(hlf + 1) * 8, 0:width])
                    P.add(ce, fn, reads=[("wf", sl, hlf)], writes=[("wc", sl, hlf)])
                dst = wb[g].rearrange("p (kc c) -> p kc c", c=512)
                P.add("pool", lambda h, sl=sl, dst=dst, width=width: h.dma_start(out=dst[:, :, 0:width], in_=wc[sl][:, :, 0:width]),
                      reads=[("wc", sl, 0), ("wc", sl, 1)], writes=[("wb", g)], dma=True)

        if stop == "p0":
            P.emit(nc, es, {})
            return nc
        P.barrier()
        def load_w(wring, g):
            wt, key = wring.next()
            width = WG[g][4]
            src = wb[g].rearrange("p (kc c) -> p kc c", c=512)
            for hlf in range(2):
                P.add("sp", lambda h, wt=wt, src=src, hlf=hlf, width=width: h.dma_start(
                    out=wt[:, hlf * 8:(hlf + 1) * 8, 0:width], in_=src[:, hlf * 8:(hlf + 1) * 8, 0:width]),
                    reads=[("wb", g)] + [("wb", g, q) for q in range(4)], writes=[(key, hlf)], dma=True)
            return wt, [(key, 0), (key, 1)]


        def make_converter(sfn, jobs, tag):
            wfq = Ring([(sfn("cvf%s%d" % (tag, i), [128, 4, 512]), "cvf%s%d" % (tag, i)) for i in range(3)])
            wcq = Ring([(sfn("cvc%s%d" % (tag, i), [128, 4, 512], BF16), "cvc%s%d" % (tag, i)) for i in range(2)])
            st = {"ld": 0, "loaded": []}

            def issue_load():
                if st["ld"] >= len(jobs):
                    return
                g, q = jobs[st["ld"]]
                st["ld"] += 1
                name, src, row0, col0, width = WG[g]
                wf, wfk = wfq.next()
                srcap = wsrc[src][row0 + q * 512:row0 + (q + 1) * 512, col0:col0 + width].rearrange("(kc p) c -> p kc c", p=128)
                P.add("act", lambda h: h.dma_start(out=wf[:, :, 0:width], in_=srcap), writes=[wfk], dma=True)
                st["loaded"].append((g, q, wf, wfk, width))

            def tick():
                issue_load()
                if not st["loaded"]:
                    return False
                if len(st["loaded"]) <= 2 and st["ld"] < len(jobs):
                    return True
                g, q, wf, wfk, width = st["loaded"].pop(0)
                wc, wck = wcq.next()
                P.add("act", lambda h: h.copy(out=wc[:, :, 0:width], in_=wf[:, :, 0:width]), reads=[wfk], writes=[wck])
                dst = wb[g].rearrange("p (kc c) -> p kc c", c=512)
                P.add("act", lambda h: h.dma_start(out=dst[:, 4 * q:4 * q + 4, 0:width], in_=wc[:, :, 0:width]), reads=[wck], writes=[("wb", g, q)], dma=True)
                return True

            def flush():
                while tick():
                    pass
            return tick, flush

        evac_rr = [0]

        def evac_eng():
            evac_rr[0] += 1
            return "act" if evac_rr[0] % 2 else "dve"

        def copy_op(eng, out_ap, in_ap, reads, writes):
            if eng == "act":
                return P.add("act", lambda h: h.copy(out=out_ap, in_=in_ap), reads=reads, writes=writes)
            return P.add(eng, lambda h: h.tensor_copy(out=out_ap, in_=in_ap), reads=reads, writes=writes)

        with ExitStack() as e1:
            s1 = lambda name, shape, dt=F32: e1.enter_context(nc.sbuf_tensor(name, shape, dt))
            p1 = lambda name, shape, dt=F32: e1.enter_context(nc.psum_tensor(name, shape, dt))
            wring = Ring([(s1("w1_%d" % i, [128, 16, 512], BF16), "w1_%d" % i) for i in range(3)])
            xbuf = Ring([(s1("xb%d" % i, [128, 2048]), "xb%d" % i) for i in range(2)])
            hnb = Ring([(s1("hnb%d" % i, [128, 2048], BF16), "hnb%d" % i) for i in range(2)])
            junk = s1("junk", [128, 2048], BF16)
            hnT = Ring([(s1("hnT%d" % i, [128, 16, 512], BF16), "hnT%d" % i) for i in range(2)])
            stat = Ring([(s1("stat%d" % i, [128, 4]), "stat%d" % i) for i in range(2)])
            stage = Ring([(s1("stg%d" % i, [128, 512], BF16), "stg%d" % i) for i in range(4)])
            XR = Ring([(s1("XR%d" % i, [128, 515]), "XR%d" % i) for i in range(2)])
            cacc = Ring([(s1("cacc%d" % i, [128, 512]), "cacc%d" % i) for i in range(2)])
            tail = s1("tail", [128, 48, 3])
            xcT = Ring([(s1("xcT%d" % i, [128, 512], BF16), "xcT%d" % i) for i in range(10)])
            xtk = Ring([(s1("xtk%d" % i, [128, 4, 512], BF16), "xtk%d" % i) for i in range(2)])
            dtw = Ring([(s1("dtw%d" % i, [128, 6, 256]), "dtw%d" % i) for i in range(2)])
            psb = Ring([(p1("ps1_%d" % i, [128, 512]), "ps1_%d" % i) for i in range(5)])
            ptr = Ring([(p1("pt1_%d" % i, [128, 1024], BF16), "pt1_%d" % i) for i in range(3)])

            P.add("dve", lambda h: h.memset(tail[:], 0.0), writes=[("tail", c) for c in range(48)])
            cv_tick, cv_flush = make_converter(s1, [] if DBG_SKIP_P0 else P1A_JOBS, "a")

            def emit_hn(ti):
                t0 = ti * T
                tm0 = t0 - 4096
                hT, hTk = hnT.next()
                for blk in range(4):
                    xb, xk = xbuf.next()
                    hb, hk = hnb.next()
                    stt, sk = stat.next()
                    r0 = t0 + blk * 128
                    P.add("sp", lambda h, xb=xb, r0=r0: h.dma_start(out=xb[:], in_=xe[r0:r0 + 128, :]), writes=[xk], dma=True)
                    P.add("act", lambda h, xb=xb, stt=stt: h.activation(out=junk[:], in_=xb[:], func=AF.Square, accum_out=stt[:, 0:1]),
                          reads=[xk], writes=["junk", (sk, 0)])
                    P.add("dve", lambda h, stt=stt: h.tensor_scalar(out=stt[:, 1:2], in0=stt[:, 0:1], scalar1=1.0 / D, scalar2=EPS, op0=ALU.mult, op1=ALU.add),
                          reads=[(sk, 0)], writes=[(sk, 1)])
                    P.add("act", lambda h, stt=stt: h.activation(out=stt[:, 2:3], in_=stt[:, 1:2], func=AF.Sqrt), reads=[(sk, 1)], writes=[(sk, 2)])
                    P.add("dve", lambda h, stt=stt: h.reciprocal(out=stt[:, 3:4], in_=stt[:, 2:3]), reads=[(sk, 2)], writes=[(sk, 3)])
                    P.add("dve", lambda h, xb=xb, hb=hb, stt=stt: h.scalar_tensor_tensor(
                        out=hb[:], in0=xb[:], scalar=stt[:, 3:4], in1=par[:, P_NORMW:P_NORMW + 2048], op0=ALU.mult, op1=ALU.mult),
                        reads=[xk, (sk, 3), "par"], writes=[hk])
                    for q4 in range(4):
                        pt, pk = ptr.next()
                        def tr(h, pt=pt, hb=hb, q4=q4):
                            for j in range(4):
                                kc = q4 * 4 + j
                                ins = h.transpose(out=pt[:, j * 128:(j + 1) * 128], in_=hb[:, kc * 128:(kc + 1) * 128], identity=idb[:])
                            return ins
                        P.add("pe", tr, reads=[hk, "idb"], writes=[pk])
                        copy_op(evac_eng(), hT[:, q4 * 4:(q4 + 1) * 4, blk * 128:(blk + 1) * 128],
                                pt[:, 0:512].rearrange("p (j t) -> p j t", j=4), [pk], [(hTk, blk, q4)])
                hT_keys = [(hTk, b, q) for b in range(4) for q in range(4)]
                if ti >= 8:
                    P.add("pool", lambda h, hT=hT, tm0=tm0: h.dma_start(out=HnT[:, :, tm0:tm0 + T].rearrange("k p t -> p k t"), in_=hT[:]),
                          reads=hT_keys, writes=[("HnT", ti)], dma=True)
                return hT, hT_keys

            tile_list = list(tiles if tiles is not None else range(NT))
            hn_ready = {}
            if tile_list:
                hn_ready[tile_list[0]] = emit_hn(tile_list[0])
            for tidx, ti in enumerate(tile_list):
                main = ti >= 8
                halo = ti >= 4
                t0 = ti * T
                tm0 = t0 - 4096
                tk0 = t0 - HALO0
                hT, hT_keys = hn_ready.pop(ti)
                def fm_group(g, consume):
                    cv_tick()
                    wt, wkeys = load_w(wring, g)
                    for cb in range(4):
                        ps, pk = psb.next()
                        def mm(h, wt=wt, ps=ps, cb=cb, hT=hT):
                            for kc in range(16):
                                ins = h.matmul(ps[:], wt[:, kc, cb * 128:(cb + 1) * 128], hT[:, kc, :], start=(kc == 0), stop=(kc == 15))
                            return ins
                        P.add("pe", mm, reads=wkeys + hT_keys, writes=[pk])
                        consume(cb, ps, pk)

                def tm_group(g, consume, width=512):
                    cv_tick()
                    wt, wkeys = load_w(wring, g)
                    for blk in range(4):
                        ps, pk = psb.next()
                        def mm(h, wt=wt, ps=ps, blk=blk, hT=hT):
                            for kc in range(16):
                                ins = h.matmul(ps[:, 0:width], hT[:, kc, blk * 128:(blk + 1) * 128], wt[:, kc, 0:width], start=(kc == 0), stop=(kc == 15))
                            return ins
                        P.add("pe", mm, reads=wkeys + hT_keys, writes=[pk])
                        consume(blk, ps, pk)

                def store_fm(dst3, hidx_base, toff, func=None, nm=None):
                    def consume(cb, ps, pk):
                        sg, sgk = stage.next()
                        if func is None:
                            copy_op(evac_eng(), sg[:], ps[:], [pk], [sgk])
                        else:
                            P.add("act", lambda h: h.activation(out=sg[:], in_=ps[:], func=func), reads=[pk], writes=[sgk])
                        hidx = hidx_base + cb
                        P.add("pool", lambda h: h.dma_start(out=dst3[hidx, :, toff:toff + T], in_=sg[:]), reads=[sgk], writes=[(nm, ti, hidx)], dma=True)
                    return consume

                def store_tm(dst2, col0, roff, func=None, nm=None):
                    def consume(blk, ps, pk):
                        sg, sgk = stage.next()
                        if func is None:
                            copy_op(evac_eng(), sg[:], ps[:], [pk], [sgk])
                        else:
                            P.add("act", lambda h: h.activation(out=sg[:], in_=ps[:], func=func), reads=[pk], writes=[sgk])
                        r0 = roff + blk * 128
                        P.add("pool", lambda h: h.dma_start(out=dst2[r0:r0 + 128, col0:col0 + 512], in_=sg[:]), reads=[sgk], writes=[(nm, ti, col0 // 512, blk)], dma=True)
                    return consume

                if main:
                    for i in range(4):
                        fm_group(G_Q[i], store_fm(Qs, 4 * i, tm0, nm="Qs"))
                if halo:
                    for i in range(4):
                        fm_group(G_K[i], store_fm(Ks, 4 * i, tk0, nm="Ks"))
                    for i in range(4):
                        tm_group(G_V[i], store_tm(Vs, 512 * i, tk0, nm="Vs"))
                if main:
                    for i in range(4):
                        fm_group(G_ZA[i], store_fm(SZa, 4 * i, tm0, AF.Silu, nm="SZa"))
                    for i in range(8):
                        tm_group(G_ZS[i], store_tm(SZs, 512 * i, tm0, AF.Silu, nm="SZs"))

                pend_tr = None
                pend_silu = []
                for i in range(12):
                    xcs = []
                    def consume(cb, ps, pk, i=i, xcs=xcs):
                        cbi = 4 * i + cb
                        xr, xrk = XR.next()
                        ca, cak = cacc.next()
                        xc, xck = xcT.next()
                        P.add("act", lambda h: h.copy(out=xr[:, 3:515], in_=ps[:]), reads=[pk], writes=[(xrk, 1)])
                        while pend_silu:
                            pend_silu.pop(0)()
                        P.add("dve", lambda h: h.tensor_copy(out=xr[:, 0:3], in_=tail[:, cbi, :]), reads=[("tail", cbi)], writes=[(xrk, 0)])
                        P.add("dve", lambda h: h.tensor_copy(out=tail[:, cbi, :], in_=xr[:, 512:515]), reads=[(xrk, 1), (xrk, 0)], writes=[("tail", cbi)])
                        cw = lambda k: par[:, P_CW + cbi * 4 + k:P_CW + cbi * 4 + k + 1]
                        P.add("dve", lambda h: h.tensor_scalar(out=ca[:], in0=xr[:, 3:515], scalar1=cw(3), scalar2=par[:, P_CB + cbi:P_CB + cbi + 1],
                                                               op0=ALU.mult, op1=ALU.add), reads=[(xrk, 1), (xrk, 0), "par"], writes=[cak])
                        for k in (2, 1, 0):
                            P.add("dve", lambda h, k=k: h.scalar_tensor_tensor(out=ca[:], in0=xr[:, k:k + 512], scalar=cw(k), in1=ca[:],
                                                                                op0=ALU.mult, op1=ALU.add), reads=[cak, (xrk, 1), (xrk, 0)], writes=[cak])
                        def do_silu(xc=xc, ca=ca, cak=cak, xck=xck):
                            P.add("act", lambda h: h.activation(out=xc[:], in_=ca[:], func=AF.Silu), reads=[cak], writes=[xck])
                        pend_silu.append(do_silu)
                        xcs.append((xc, xck))
                        if i >= 8 and main:
                            gi = cbi - 32 if i < 10 else cbi - 40
                            dst = BTs if i < 10 else CTs
                            def do_store(xc=xc, xck=xck, dst=dst, gi=gi, tm0=tm0, ti=ti, cbi=cbi):
                                P.add("pool", lambda h: h.dma_start(out=dst[gi, :, tm0:tm0 + T], in_=xc[:]), reads=[xck], writes=[("BC", ti, cbi)], dma=True)
                            pend_silu.append(do_store)
                    fm_group(G_XBC[i], consume)
                    if pend_tr is not None:
                        pend_tr()
                        pend_tr = None
                    if i < 10:
                        def do_tr(i=i, xcs=xcs, t0=t0):
                            xt_, xtkk = xtk.next()
                            for ch in range(4):
                                pt, pk = ptr.next()
                                def tr(h, pt=pt, ch=ch, xcs=xcs):
                                    for j in range(4):
                                        ins = h.transpose(out=pt[:, j * 128:(j + 1) * 128], in_=xcs[j][0][:, ch * 128:(ch + 1) * 128], identity=idb[:])
                                    return ins
                                P.add("pe", tr, reads=[k for _, k in xcs] + ["idb"], writes=[pk])
                                copy_op(evac_eng(), xt_[:, ch, :], pt[:, 0:512], [pk], [(xtkk, ch)])
                            P.add("pool", lambda h, xt_=xt_, i=i, t0=t0: h.dma_start(
                                out=Xtok[t0:t0 + T, 512 * i:512 * (i + 1)].rearrange("(c p) f -> p c f", p=128), in_=xt_[:]),
                                reads=[(xtkk, c) for c in range(4)], writes=[("Xtok", ti, i)], dma=True)
                        pend_tr = do_tr
                    if i == 5 and tidx + 1 < len(tile_list):
                        hn_ready[tile_list[tidx + 1]] = emit_hn(tile_list[tidx + 1])
                    pass
                while pend_silu:
                    pend_silu.pop(0)()
                if pend_tr is not None:
                    pend_tr()
                    pend_tr = None

                dw, dwk = dtw.next()
                wt, wkeys = load_w(wring, G_DT)
                ps, pk = psb.next()
                def mmdt(h, wt=wt, ps=ps, hT=hT):
                    for blk in range(4):
                        for kc in range(16):
                            ins = h.matmul(ps[:, blk * 64:(blk + 1) * 64], hT[:, kc, blk * 128:(blk + 1) * 128], wt[:, kc, 0:64], start=(kc == 0), stop=(kc == 15))
                    return ins
                P.add("pe", mmdt, reads=wkeys + hT_keys, writes=[pk])
                dtb_bc = par[:, P_DTB:P_DTB + 64].unsqueeze(1).to_broadcast([128, 4, 64])
                an_bc = anegt[:, :].unsqueeze(1).to_broadcast([128, 4, 64])
                v3 = lambda a: a.rearrange("p (b j) -> p b j", b=4)
                z, nz, mn, ex, dtv, dav = (dw[:, i, :] for i in range(6))
                P.add("dve", lambda h, ps=ps, z=z: h.tensor_tensor(out=v3(z), in0=v3(ps[:, 0:256]), in1=dtb_bc, op=ALU.add), reads=[pk, "par"], writes=[(dwk, 0)])
                P.add("dve", lambda h, z=z, nz=nz: h.tensor_scalar(out=nz, in0=z, scalar1=-1.0, scalar2=None, op0=ALU.mult), reads=[(dwk, 0)], writes=[(dwk, 1)])
                P.add("dve", lambda h, z=z, nz=nz, mn=mn: h.tensor_tensor(out=mn, in0=z, in1=nz, op=ALU.min), reads=[(dwk, 0), (dwk, 1)], writes=[(dwk, 2)])
                P.add("act", lambda h, mn=mn, ex=ex: h.activation(out=ex, in_=mn, func=AF.Exp), reads=[(dwk, 2)], writes=[(dwk, 3)])
                P.add("act", lambda h, ex=ex: h.activation(out=ex, in_=ex, func=AF.Ln, bias=1.0), reads=[(dwk, 3)], writes=[(dwk, 3)])
                P.add("dve", lambda h, z=z, nz=nz: h.tensor_scalar(out=nz, in0=z, scalar1=0.0, scalar2=None, op0=ALU.max), reads=[(dwk, 0), (dwk, 2)], writes=[(dwk, 1)])
                P.add("dve", lambda h, nz=nz, ex=ex, dtv=dtv: h.tensor_tensor(out=dtv, in0=nz, in1=ex, op=ALU.add), reads=[(dwk, 1), (dwk, 3)], writes=[(dwk, 4)])
                P.add("dve", lambda h, dtv=dtv, dav=dav: h.tensor_tensor(out=v3(dav), in0=v3(dtv), in1=an_bc, op=ALU.mult), reads=[(dwk, 4), "aneg"], writes=[(dwk, 5)])
                P.add("pool", lambda h, dtv=dtv, t0=t0: h.dma_start(out=DTs[t0:t0 + T, :].rearrange("(b p) j -> p b j", p=128), in_=v3(dtv)), reads=[(dwk, 4)], writes=[("DT", ti)], dma=True)
                P.add("pool", lambda h, dav=dav, t0=t0: h.dma_start(out=DAs[t0:t0 + T, :].rearrange("(b p) j -> p b j", p=128), in_=v3(dav)), reads=[(dwk, 5)], writes=[("DA", ti)], dma=True)
            cv_flush()


        if stop == "p1a":
            P.emit(nc, es, {})
            return nc
        P.barrier()
        INV = 1.0 / float(np.sqrt(128.0))
        with ExitStack() as e1:
            s1 = lambda name, shape, dt=F32: e1.enter_context(nc.sbuf_tensor(name, shape, dt))
            p1 = lambda name, shape, dt=F32: e1.enter_context(nc.psum_tensor(name, shape, dt))
            xtr = Ring([(s1("bxt%d" % i, [128, 5120], BF16), "bxt%d" % i) for i in range(3)])
            ddr = Ring([(s1("bdd%d" % i, [128, 2, 64]), "bdd%d" % i) for i in range(3)])
            bcr = Ring([(s1("bbc%d" % i, [128, 2, 8, 128], BF16), "bbc%d" % i) for i in range(3)])
            szr = Ring([(s1("bsz%d" % i, [128, 4096], BF16), "bsz%d" % i) for i in range(2)])
            smr = Ring([(s1("bsm%d" % i, [128, 5, 64]), "bsm%d" % i) for i in range(3)])
            xdr = Ring([(s1("bxd%d" % i, [128, 4096], BF16), "bxd%d" % i) for i in range(2)])
            xwr = Ring([(s1("bxw%d" % i, [128, 4096], BF16), "bxw%d" % i) for i in range(2)])
            cbr = Ring([(s1("bcb%d" % i, [128, 8, 128], BF16), "bcb%d" % i) for i in range(2)])
            dur = Ring([(s1("bdu%d" % i, [128, 8, 128]), "bdu%d" % i) for i in range(2)])
            der = Ring([(s1("bde%d" % i, [128, 1024], BF16), "bde%d" % i) for i in range(2)])
            mtr = Ring([(s1("bmt%d" % i, [128, 8, 128], BF16), "bmt%d" % i) for i in range(2)])
            t1r = Ring([(s1("bt1%d" % i, [128, 512]), "bt1%d" % i) for i in range(2)])
            t2r = Ring([(s1("bt2%d" % i, [128, 512]), "bt2%d" % i) for i in range(2)])
            t3r = Ring([(s1("bt3%d" % i, [128, 512]), "bt3%d" % i) for i in range(2)])
            ygf = s1("bygf", [128, 4096])
            ynr = Ring([(s1("byn%d" % i, [128, 4096], BF16), "byn%d" % i) for i in range(1)])
            ssr = Ring([(s1("bss%d" % i, [128, 12]), "bss%d" % i) for i in range(2)])
            jk = s1("bjunk", [128, 512], BF16)
            stt = s1("bst", [128, 8, 512])
            stb = s1("bstb", [128, 8, 512], BF16)
            pa = p1("bpa", [128, 512])
            pcbr = Ring([(p1("bpcb%d" % i, [128, 512]), "bpcb%d" % i) for i in range(2)])
            pseg = p1("bpseg", [128, 1024])
            py = p1("bpy", [128, 512])
            po = p1("bpo", [128, 512])
            pst = p1("bpst", [128, 512])

            P.add("dve", lambda h: h.memset(stt[:], 0.0), writes=[("st", g) for g in range(8)])
            P.add("pool", lambda h: h.memset(stb[:], 0.0), writes=[("stb", g) for g in range(8)])
            dsk = par[:, P_DSK:P_DSK + 64]
            v8 = lambda a: a.rearrange("p (j q) -> p j q", j=8)

            b64 = lambda a: a.unsqueeze(2).to_broadcast([128, 64, 64])
            v64 = lambda a: a.rearrange("p (j q) -> p j q", j=64)
            b8 = lambda a: a.unsqueeze(2).to_broadcast([128, 8, 64])

            def header(ci):
                c = dict(ci=ci, main=ci >= 32, ti=ci // 4, r0=ci * 128, rm0=ci * 128 - 4096)
                ti, r0, rm0 = c["ti"], c["r0"], c["rm0"]
                xt, xk = xtr.next(); dd, dk = ddr.next(); sm, smk = smr.next(); xd, xdk = xdr.next(); xw, xwk = xwr.next()
                c.update(xt=xt, xk=xk, dd=dd, dk=dk, sm=sm, smk=smk, xd=xd, xdk=xdk, xw=xw, xwk=xwk)
                P.add("sp", lambda h: h.dma_start(out=xt[:], in_=Xtok[r0:r0 + 128, :]), reads=[("Xtok", ti, i) for i in range(10)], writes=[xk], dma=True)
                P.add("sp", lambda h: h.dma_start(out=dd[:, 0, :], in_=DTs[r0:r0 + 128, :]), reads=[("DT", ti)], writes=[(dk, 0)], dma=True)
                P.add("sp", lambda h: h.dma_start(out=dd[:, 1, :], in_=DAs[r0:r0 + 128, :]), reads=[("DA", ti)], writes=[(dk, 1)], dma=True)
                dtt = dd[:, 0, :]; dat = dd[:, 1, :]
                c.update(dtt=dtt, dat=dat)
                if c["main"]:
                    bc, bck = bcr.next(); sz, szk = szr.next()
                    c.update(bc=bc, bck=bck, sz=sz, szk=szk)
                    P.add("sp", lambda h: h.dma_start(out=bc[:, 0, :, :], in_=BTs[:, :, rm0:rm0 + 128].rearrange("g n t -> n g t")),
                          reads=[("BC", ti, cc) for cc in range(32, 40)], writes=[(bck, 0)], dma=True)
                    P.add("sp", lambda h: h.dma_start(out=bc[:, 1, :, :], in_=CTs[:, :, rm0:rm0 + 128].rearrange("g n t -> n g t")),
                          reads=[("BC", ti, cc) for cc in range(40, 48)], writes=[(bck, 1)], dma=True)
                    P.add("sp", lambda h: h.dma_start(out=sz[:], in_=SZs[rm0:rm0 + 128, :]),
                          reads=[("SZs", ti, i, ci % 4) for i in range(8)], writes=[szk], dma=True)
                def mm_ac(h):
                    h.matmul(pa[:, 0:64], Tm, dat, start=True, stop=True)
                    return h.matmul(pa[:, 64:128], ones32, dat, start=True, stop=True)
                P.add("pe", mm_ac, reads=[(dk, 1), "par"], writes=["pa"])
                acs, eac, ela, dws, wsv = (sm[:, i, :] for i in range(5))
                c.update(eac=eac, ela=ela)
                P.add("dve", lambda h: h.tensor_copy(out=acs, in_=pa[:, 0:64]), reads=["pa"], writes=[(smk, 0), "pa"])
                P.add("act", lambda h: h.activation(out=eac, in_=acs, func=AF.Exp), reads=[(smk, 0)], writes=[(smk, 1)])
                P.add("act", lambda h: h.activation(out=ela, in_=pa[:, 64:128], func=AF.Exp), reads=["pa"], writes=[(smk, 2), "pa"])
                P.add("dve", lambda h: h.tensor_tensor(out=dws, in0=pa[:, 64:128], in1=acs, op=ALU.subtract), reads=["pa", (smk, 0)], writes=[(smk, 3), "pa"])
                P.add("act", lambda h: h.activation(out=wsv, in_=dws, func=AF.Exp), reads=[(smk, 3)], writes=[(smk, 4)])
                if c["main"]:
                    P.add("dve", lambda h: h.tensor_tensor(out=v64(xd[:]), in0=v64(xt[:, 0:4096]), in1=b64(dtt), op=ALU.mult), reads=[xk, (dk, 0)], writes=[xdk])
                P.add("dve", lambda h: h.tensor_tensor(out=wsv, in0=wsv, in1=dtt, op=ALU.mult), reads=[(smk, 4), (dk, 0)], writes=[(smk, 4)])
                P.add("dve", lambda h: h.tensor_tensor(out=v64(xw[:]), in0=v64(xt[:, 0:4096]), in1=b64(wsv), op=ALU.mult), reads=[xk, (smk, 4)], writes=[xwk])
                if c["main"]:
                    c["ss"], c["ssk"] = ssr.next()
                    cb8, cb8k = cbr.next()
                    c.update(cb8=cb8, cb8k=cb8k)
                    for half in range(2):
                        pcb, pcbk = pcbr.next()
                        def mm_cb(h, pcb=pcb, half=half):
                            for gg in range(4):
                                g = half * 4 + gg
                                ins = h.matmul(pcb[:, gg * 128:(gg + 1) * 128], bc[:, 0, g, :], bc[:, 1, g, :], start=True, stop=True)
                            return ins
                        P.add("pe", mm_cb, reads=[(bck, 0), (bck, 1)], writes=[pcbk])
                        P.add("dve", lambda h, pcb=pcb, half=half: h.tensor_tensor(
                            out=cb8[:, half * 4:(half + 1) * 4, :], in0=pcb[:].rearrange("p (g l) -> p g l", g=4),
                            in1=Tm.unsqueeze(1).to_broadcast([128, 4, 128]), op=ALU.mult), reads=[pcbk, "par"], writes=[(cb8k, half)])
                return c

            def stageA(c, g):
                bc, bck, dat, dk = c["bc"], c["bck"], c["dat"], c["dk"]
                du, duk = dur.next(); de, dek = der.next()
                cbm = c["cb8"][:, g, :]; cbk = (c["cb8k"], g // 4)
                P.add("dve", lambda h: h.tensor_tensor(out=du[:], in0=Um.unsqueeze(1).to_broadcast([128, 8, 128]),
                                                       in1=dat[:, g * 8:(g + 1) * 8].unsqueeze(2).to_broadcast([128, 8, 128]), op=ALU.mult),
                      reads=[(dk, 1), "par"], writes=[duk])
                def mm_seg(h):
                    for j in range(8):
                        ins = h.matmul(pseg[:, j * 128:(j + 1) * 128], du[:, j, :], Tm, start=True, stop=True)
                    return ins
                P.add("pe", mm_seg, reads=[duk, "par"], writes=["pseg"])
                P.add("act", lambda h: h.activation(out=de[:], in_=pseg[:], func=AF.Exp), reads=["pseg"], writes=[dek])
                return dict(cbm=cbm, cbk=cbk, de=de, dek=dek)

            def stageB1(c, gc, g):
                bc, bck, xd, xdk = c["bc"], c["bck"], c["xd"], c["xdk"]
                cbm, cbk, de, dek = gc["cbm"], gc["cbk"], gc["de"], gc["dek"]
                mt, mtk = mtr.next()
                P.add("dve", lambda h: h.tensor_tensor(out=mt[:], in0=de[:].rearrange("p (j l) -> p j l", j=8), in1=cbm.unsqueeze(1).to_broadcast([128, 8, 128]), op=ALU.mult),
                      reads=[dek, cbk], writes=[mtk])
                def mm_y(h):
                    for j in range(8):
                        c0 = (g * 8 + j) * 64
                        ins = h.matmul(py[:, j * 64:(j + 1) * 64], mt[:, j, :], xd[:, c0:c0 + 64], start=True, stop=True)
                    return ins
                P.add("pe", lambda h: h.matmul(po[:], bc[:, 1, g, :], stb[:, g, :], start=True, stop=True), reads=[(bck, 1), ("stb", g)], writes=["po"])
                P.add("pe", mm_y, reads=[mtk, xdk], writes=["py"])

            def stageB2(c, g):
                xt, xk, sz, szk = c["xt"], c["xk"], c["sz"], c["szk"]
                eac, smk, ss, ssk = c["eac"], c["smk"], c["ss"], c["ssk"]
                t1, t1k = t1r.next(); t2, t2k = t2r.next(); t3, t3k = t3r.next()
                gs = slice(g * 512, (g + 1) * 512)
                P.add("dve", lambda h: h.tensor_tensor(out=v8(t1[:]), in0=v8(po[:]), in1=b8(eac[:, g * 8:(g + 1) * 8]), op=ALU.mult), reads=["po", (smk, 1)], writes=[t1k])
                P.add("pool", lambda h: h.tensor_tensor(out=v8(t3[:]), in0=v8(xt[:, gs]), in1=b8(dsk[:, g * 8:(g + 1) * 8]), op=ALU.mult), reads=[xk, "par"], writes=[t3k])
                P.add("dve", lambda h: h.tensor_tensor(out=t2[:], in0=py[:], in1=t1[:], op=ALU.add), reads=["py", t1k], writes=[t2k])
                P.add("pool", lambda h: h.tensor_tensor(out=t3[:], in0=t2[:], in1=t3[:], op=ALU.add), reads=[t2k, t3k], writes=[t3k])
                P.add("pool", lambda h: h.tensor_tensor(out=ygf[:, gs], in0=t3[:], in1=sz[:, gs], op=ALU.mult), reads=[t3k, szk], writes=[("ygf", g)])
                P.add("act", lambda h: h.activation(out=jk[:], in_=ygf[:, gs], func=AF.Square, accum_out=ss[:, g:g + 1]), reads=[("ygf", g)], writes=["bjunk", (ssk, g)])

            def finish(c):
                ss, ssk, rm0, ci = c["ss"], c["ssk"], c["rm0"], c["ci"]
                yn, ynk = ynr.next()
                P.add("dve", lambda h: h.tensor_reduce(out=ss[:, 8:9], in_=ss[:, 0:8], axis=mybir.AxisListType.X, op=ALU.add), reads=[(ssk, g) for g in range(8)], writes=[(ssk, 8)])
                P.add("dve", lambda h: h.tensor_scalar(out=ss[:, 9:10], in0=ss[:, 8:9], scalar1=1.0 / 4096, scalar2=EPS, op0=ALU.mult, op1=ALU.add), reads=[(ssk, 8)], writes=[(ssk, 9)])
                P.add("act", lambda h: h.activation(out=ss[:, 10:11], in_=ss[:, 9:10], func=AF.Sqrt), reads=[(ssk, 9)], writes=[(ssk, 10)])
                P.add("dve", lambda h: h.reciprocal(out=ss[:, 11:12], in_=ss[:, 10:11]), reads=[(ssk, 10)], writes=[(ssk, 11)])
                P.add("dve", lambda h: h.tensor_scalar(out=yn[:], in0=ygf[:], scalar1=ss[:, 11:12], scalar2=None, op0=ALU.mult), reads=[("ygf", g) for g in range(8)] + [(ssk, 11)], writes=[ynk])
                P.add("pool", lambda h: h.dma_start(out=Yn[rm0:rm0 + 128, :], in_=yn[:]), reads=[ynk], writes=[("Yn", ci)], dma=True)

            def state_update(c):
                xt, xk, xw, xwk, ela, smk, ci = c["xt"], c["xk"], c["xw"], c["xwk"], c["ela"], c["smk"], c["ci"]
                for g in range(DBG_SU):
                    gs = slice(g * 512, (g + 1) * 512)
                    P.add("pe", lambda h, g=g, gs=gs: h.matmul(pst[:], xt[:, 4096 + g * 128:4096 + (g + 1) * 128], xw[:, gs], start=True, stop=True), reads=[xk, xwk], writes=["pst"])
                    P.add("dve", lambda h, g=g: h.tensor_tensor(out=v8(stt[:, g, :]), in0=v8(stt[:, g, :]), in1=b8(ela[:, g * 8:(g + 1) * 8]), op=ALU.mult), reads=[("st", g), (smk, 2)], writes=[("st", g)])
                    P.add("dve", lambda h, g=g: h.tensor_tensor(out=stt[:, g, :], in0=pst[:], in1=stt[:, g, :], op=ALU.add), reads=["pst", ("st", g)], writes=[("st", g)])
                    if ci == 31:
                        P.add("dve", lambda h, g=g: h.tensor_scalar(out=stt[:, g, :], in0=stt[:, g, :], scalar1=par[:, P_PV:P_PV + 1], scalar2=None, op0=ALU.mult), reads=[("st", g), "par"], writes=[("st", g)])
                    if ci >= 31 and ci < 63:
                        P.add("act", lambda h, g=g: h.copy(out=stb[:, g, :], in_=stt[:, g, :]), reads=[("st", g)], writes=[("stb", g)])

            chunks = list(DBG_CHUNKS if DBG_CHUNKS is not None else range(64))
            ctxs = {}
            if chunks:
                ctxs[0] = header(chunks[0])
            for idx, ci in enumerate(chunks):
                c = ctxs.pop(idx)
                if idx + 1 < len(chunks):
                    ctxs[idx + 1] = header(chunks[idx + 1])
                if c["main"]:
                    gcs = {0: stageA(c, 0)}
                    for g in range(8):
                        if g + 1 < 8:
                            gcs[g + 1] = stageA(c, g + 1)
                        if g >= 1:
                            stageB2(c, g - 1)
                        stageB1(c, gcs.pop(g), g)
                    stageB2(c, 7)
                    state_update(c)
                    finish(c)
                else:
                    state_update(c)

        if stop == "p1b":
            P.emit(nc, es, {})
            return nc
        P.barrier()
        with ExitStack() as e2:
            s2 = lambda name, shape, dt=F32: e2.enter_context(nc.sbuf_tensor(name, shape, dt))
            p2 = lambda name, shape, dt=F32: e2.enter_context(nc.psum_tensor(name, shape, dt))
            qtr = Ring([(s2("cq%d" % i, [128, 2048], BF16), "cq%d" % i) for i in range(2)])
            ktr = Ring([(s2("ck%d" % i, [128, 4096], BF16), "ck%d" % i) for i in range(2)])
            zar = Ring([(s2("cz%d" % i, [128, 2048], BF16), "cz%d" % i) for i in range(2)])
            vpr = Ring([(s2("cv%d" % i, [128, 3, 32, 128], BF16), "cv%d" % i) for i in range(2)])
            bir = Ring([(s2("cb%d" % i, [128, 2, 768]), "cb%d" % i) for i in range(1)])
            nar = Ring([(s2("cn%d" % i, [128, 2048]), "cn%d" % i) for i in range(2)])
            dar = Ring([(s2("cd%d" % i, [128, 2048]), "cd%d" % i) for i in range(2)])
            ssb = Ring([(s2("cs%d" % i, [128, 1024], BF16), "cs%d" % i) for i in range(2)])
            bbr = Ring([(s2("cbb%d" % i, [128, 2, 768], BF16), "cbb%d" % i) for i in range(2)])
            ptb = Ring([(s2("cp%d" % i, [128, 1024], BF16), "cp%d" % i) for i in range(2)])
            oar = Ring([(s2("co%d" % i, [128, 2048], BF16), "co%d" % i) for i in range(2)])
            pSr = Ring([(p2("cpS%d" % i, [128, 1024]), "cpS%d" % i) for i in range(2)])
            pOr = Ring([(p2("cpO%d" % i, [128, 512]), "cpO%d" % i) for i in range(2)])
            pDr = Ring([(p2("cpD%d" % i, [128, 512]), "cpD%d" % i) for i in range(2)])

            cv2_tick, cv2_flush = make_converter(s2, [] if DBG_SKIP_P0 else P2_JOBS, "b")

            def strided(ap, start, step, n=128):
                return ap[:, start:start + step * (n - 1) + 1:step]

            def do_head(st_, hd, w0, tiles_q, tiles_k):
                qt, qk = qtr.next(); kt, kk = ktr.next(); za, zk = zar.next(); vp, vk = vpr.next(); bi, bk = bir.next()
                na, nk = nar.next(); da, dak = dar.next(); oa, ok = oar.next()
                P.add("sp", lambda h, qt=qt, hd=hd, w0=w0: h.dma_start(out=qt[:], in_=Qs[hd, :, w0:w0 + 2048]), reads=[("Qs", t, hd) for t in tiles_q], writes=[qk], dma=True)
                P.add("sp", lambda h, kt=kt, hd=hd, w0=w0: h.dma_start(out=kt[:], in_=Ks[hd, :, w0:w0 + 4096]), reads=[("Ks", t, hd) for t in tiles_k], writes=[kk], dma=True)
                P.add("sp", lambda h, za=za, hd=hd, w0=w0: h.dma_start(out=za[:], in_=SZa[hd, :, w0:w0 + 2048]), reads=[("SZa", t, hd) for t in tiles_q], writes=[zk], dma=True)
                vkeys = [("Vs", t, hd // 4, b) for t in tiles_k for b in range(4)]
                vsrc = Vs[w0:w0 + 4096, hd * 128:(hd + 1) * 128]
                P.add("sp", lambda h, vp=vp, vsrc=vsrc: h.dma_start(out=vp[:, 0, :, :], in_=vsrc.rearrange("(b p) e -> p b e", p=128)), reads=vkeys, writes=[(vk, 0)], dma=True)
                for r in range(4):
                    P.add("sp", lambda h, vp=vp, vsrc=vsrc, r=r: h.dma_start(
                        out=vp[:, 1, r * 8:(r + 1) * 8, :], in_=vsrc.rearrange("(i p r) e -> p r i e", p=128, r=4)[:, r, :, :]), reads=vkeys, writes=[(vk, 1, r)], dma=True)
                for r in range(16):
                    P.add("sp", lambda h, vp=vp, vsrc=vsrc, r=r: h.dma_start(
                        out=vp[:, 2, r * 2:(r + 1) * 2, :], in_=vsrc.rearrange("(i p r) e -> p r i e", p=128, r=16)[:, r, :, :]), reads=vkeys, writes=[(vk, 2, r)], dma=True)
                vallk = [(vk, 0)] + [(vk, 1, r) for r in range(4)] + [(vk, 2, r) for r in range(16)]
                P.add("sp", lambda h, bi=bi, hd=hd: h.dma_start(out=bi[:, 0, :], in_=biasA[hd]), writes=[(bk, 0)], dma=True)
                P.add("sp", lambda h, bi=bi, hd=hd: h.dma_start(out=bi[:, 1, :], in_=bias0[hd]), writes=[(bk, 1)], dma=True)
                bib, bbk = bbr.next()
                P.add("act", lambda h, bi=bi, bib=bib: h.copy(out=bib[:], in_=bi[:]), reads=[(bk, 0), (bk, 1)], writes=[bbk])
                def stageS(pi, d, gq):
                    units = []
                    for u in range(4):
                        if d == 1:
                            qb = 16 + 4 * gq + u
                            kp = kt[:, (qb - 1) * 128:qb * 128]; kc_ = kt[:, qb * 128:(qb + 1) * 128]
                            qa = qt[:, (qb - 16) * 128:(qb - 15) * 128]
                            units.append((kp, kc_, qa, qb - 1, qb, qb == 16))
                        elif d == 4:
                            i = 4 + gq; r = u
                            kp = strided(kt, (i - 1) * 512 + r, 4); kc_ = strided(kt, i * 512 + r, 4)
                            qa = strided(qt, (i - 4) * 512 + r, 4)
                            units.append((kp, kc_, qa, r * 8 + i - 1, r * 8 + i, i == 4))
                        else:
                            r = 4 * gq + u
                            kp = strided(kt, r, 16); kc_ = strided(kt, 2048 + r, 16)
                            qa = strided(qt, r, 16)
                            units.append((kp, kc_, qa, r * 2, r * 2 + 1, True))
                    pS, pSk = pSr.next(); sS, sSk = ssb.next(); pT, pTk = ptb.next()
                    def mm_s(h):
                        for u, (kp, kc_, qa, _, _, _) in enumerate(units):
                            h.matmul(pS[:, u * 256:u * 256 + 128], kp, qa, start=True, stop=True)
                            ins = h.matmul(pS[:, u * 256 + 128:u * 256 + 256], kc_, qa, start=True, stop=True)
                        return ins
                    P.add("pe", mm_s, reads=[qk, kk], writes=[pSk])
                    bsl = slice(pi * 256, (pi + 1) * 256)
                    halo_flags = [st_ == 0 and un[5] for un in units]
                    P.add("act", lambda h: h.activation(out=sS[:], in_=pS[:], func=AF.Exp, scale=INV), reads=[pSk], writes=[sSk])
                    if all(halo_flags) or not any(halo_flags):
                        wh = 1 if halo_flags[0] else 0
                        P.add("dve", lambda h: h.tensor_tensor(
                            out=pT[:].rearrange("p (u c) -> p u c", u=4), in0=sS[:].rearrange("p (u c) -> p u c", u=4),
                            in1=bib[:, wh, bsl].unsqueeze(1).to_broadcast([128, 4, 256]), op=ALU.mult),
                            reads=[sSk, bbk], writes=[pTk])
                    else:
                        P.add("dve", lambda h: h.tensor_tensor(out=pT[:, 0:256], in0=sS[:, 0:256], in1=bib[:, 1, bsl], op=ALU.mult),
                              reads=[sSk, bbk], writes=[(pTk, "a")])
                        P.add("dve", lambda h: h.tensor_tensor(
                            out=pT[:, 256:1024].rearrange("p (u c) -> p u c", u=3), in0=sS[:, 256:1024].rearrange("p (u c) -> p u c", u=3),
                            in1=bib[:, 0, bsl].unsqueeze(1).to_broadcast([128, 3, 256]), op=ALU.mult),
                            reads=[sSk, bbk, (pTk, "a")], writes=[pTk])
                    return dict(units=units, pT=pT, pTk=pTk, pi=pi, d=d, gq=gq)

                def stageV(sc):
                    units, pT, pTk, pi, d, gq = sc["units"], sc["pT"], sc["pTk"], sc["pi"], sc["d"], sc["gq"]
                    pO, pOk = pOr.next(); pD, pDk = pDr.next()
                    def mm_pv(h):
                        for u, (_, _, _, vb0, vb1, _) in enumerate(units):
                            h.matmul(pO[:, u * 128:(u + 1) * 128], vp[:, pi, vb0, :], pT[:, u * 256:u * 256 + 128], start=True, stop=False)
                            h.matmul(pO[:, u * 128:(u + 1) * 128], vp[:, pi, vb1, :], pT[:, u * 256 + 128:u * 256 + 256], start=False, stop=True)
                        for u in range(4):
                            h.matmul(pD[:, u * 128:(u + 1) * 128], oneb[:], pT[:, u * 256:u * 256 + 128], start=True, stop=False)
                            ins = h.matmul(pD[:, u * 128:(u + 1) * 128], oneb[:], pT[:, u * 256 + 128:u * 256 + 256], start=False, stop=True)
                        return ins
                    P.add("pe", mm_pv, reads=[pTk, "oneb"] + vallk, writes=[pOk, pDk])
                    if d == 1:
                        c0 = gq * 512
                        P.add("act", lambda h: h.copy(out=na[:, c0:c0 + 512], in_=pO[:]), reads=[pOk], writes=[(nk, gq)])
                        P.add("act", lambda h: h.copy(out=da[:, c0:c0 + 512], in_=pD[:]), reads=[pDk], writes=[(dak, gq)])
                    else:
                        if d == 4:
                            c0 = gq * 512
                            nv = lambda a: a[:, c0:c0 + 512].rearrange("p (m r) -> p r m", r=4)
                        else:
                            nv = lambda a: a[:, :].rearrange("p (m r) -> p r m", r=16)[:, 4 * gq:4 * gq + 4, :]
                        pv_ = lambda a: a[:].rearrange("p (r m) -> p r m", r=4)
                        wk = [(nk, gq)] if d == 4 else [(nk, q) for q in range(4)]
                        wdk = [(dak, gq)] if d == 4 else [(dak, q) for q in range(4)]
                        P.add("dve", lambda h: h.tensor_tensor(out=nv(na), in0=pv_(pO), in1=nv(na), op=ALU.add), reads=[pOk] + wk, writes=wk)
                        P.add("dve", lambda h: h.tensor_tensor(out=nv(da), in0=pv_(pD), in1=nv(da), op=ALU.add), reads=[pDk] + wdk, writes=wdk)

                glist = [(pi, d, gq) for pi, d in enumerate((1, 4, 16)) for gq in range(4)]
                pend = stageS(*glist[0])
                yield
                for gi_ in range(len(glist)):
                    nxt = stageS(*glist[gi_ + 1]) if gi_ + 1 < len(glist) else None
                    stageV(pend)
                    pend = nxt
                yield
                allk = [(nk, q) for q in range(4)]
                alldk = [(dak, q) for q in range(4)]
                P.add("act", lambda h, da=da: h.activation(out=da[:], in_=da[:], func=AF.Ln), reads=alldk, writes=alldk)
                P.add("act", lambda h, da=da: h.activation(out=da[:], in_=da[:], func=AF.Exp, scale=-1.0), reads=alldk, writes=alldk)
                P.add("dve", lambda h, na=na, da=da: h.tensor_tensor(out=na[:], in0=na[:], in1=da[:], op=ALU.mult), reads=allk + alldk, writes=allk)
                P.add("pool", lambda h, na=na, za=za, oa=oa: h.tensor_tensor(out=oa[:], in0=na[:], in1=za[:], op=ALU.mult), reads=allk + [zk], writes=[ok])
                P.add("pool", lambda h, oa=oa, hd=hd, w0=w0: h.dma_start(out=OaT[hd, :, w0:w0 + 2048], in_=oa[:]), reads=[ok], writes=[("OaT", st_, hd)], dma=True)


            gens = []
            for st_ in range(2):
                w0 = st_ * 2048
                tiles_q = [8 + st_ * 4 + i for i in range(4)]
                tiles_k = [4 + st_ * 4 + i for i in range(8)]
                for hd in range(16):
                    gens.append(do_head(st_, hd, w0, tiles_q, tiles_k))
            next(gens[0])
            for gi2, gen in enumerate(gens):
                next(gen)
                if gi2 + 1 < len(gens):
                    next(gens[gi2 + 1])
                for _ in gen:
                    pass
                for _ in range(4):
                    cv2_tick()
            cv2_flush()

        if stop == "p2":
            P.emit(nc, es, {})
            return nc
        P.barrier()
        final_ops = []
        with ExitStack() as e3:
            s3 = lambda name, shape, dt=F32: e3.enter_context(nc.sbuf_tensor(name, shape, dt))
            p3 = lambda name, shape, dt=F32: e3.enter_context(nc.psum_tensor(name, shape, dt))
            wring = Ring([(s3("w3_%d" % i, [128, 16, 512], BF16), "w3_%d" % i) for i in range(3)])
            hT = s3("dhT", [128, 16, 512], BF16)
            oT = s3("doT", [128, 16, 512], BF16)
            yT = s3("dyT", [128, 32, 512], BF16)
            ynb = Ring([(s3("dyn%d" % i, [128, 4096], BF16), "dyn%d" % i) for i in range(2)])
            mT = s3("dmT", [128, 16, 512], BF16)
            xr4 = Ring([(s3("dxr%d" % i, [128, 2048]), "dxr%d" % i) for i in range(2)])
            sga = s3("dsga", [128, 4, 512], BF16)
            sgs = s3("dsgs", [128, 4, 512], BF16)
            m1 = s3("dm1", [128, 4, 512])
            m2r = Ring([(s3("dm2%d" % i, [128, 512]), "dm2%d" % i) for i in range(1)])
            junk3 = s3("djunk", [128, 2048], BF16)
            stat = Ring([(s3("dst%d" % i, [128, 4]), "dst%d" % i) for i in range(2)])
            psb = Ring([(p3("dps%d" % i, [128, 512]), "dps%d" % i) for i in range(6)])
            ptr = Ring([(p3("dpt%d" % i, [128, 1024], BF16), "dpt%d" % i) for i in range(2)])

            for mt_ in range(8):
                tm0 = mt_ * 512
                ti = 8 + mt_
                st_ = mt_ // 4
                P.add("sp", lambda h, tm0=tm0: h.dma_start(out=hT[:], in_=HnT[:, :, tm0:tm0 + T].rearrange("k p t -> p k t")), reads=[("HnT", ti)], writes=["dhT"], dma=True)
                P.add("sp", lambda h, tm0=tm0: h.dma_start(out=oT[:], in_=OaT[:, :, tm0:tm0 + T].rearrange("k p t -> p k t")), reads=[("OaT", st_, hd) for hd in range(16)], writes=["doT"], dma=True)
                for blk in range(4):
                    yb, ybk = ynb.next()
                    r0 = tm0 + blk * 128
                    P.add("sp", lambda h, yb=yb, r0=r0: h.dma_start(out=yb[:], in_=Yn[r0:r0 + 128, :]), reads=[("Yn", 32 + mt_ * 4 + blk)], writes=[ybk], dma=True)
                    for c4 in range(8):
                        pt, pk = ptr.next()
                        def tr(h, pt=pt, yb=yb, c4=c4):
                            for j in range(4):
                                cc = c4 * 4 + j
                                ins = h.transpose(out=pt[:, j * 128:(j + 1) * 128], in_=yb[:, cc * 128:(cc + 1) * 128], identity=idb[:])
                            return ins
                        P.add("pe", tr, reads=[ybk, "idb"], writes=[pk])
                        P.add("dve", lambda h, pt=pt, c4=c4, blk=blk: h.tensor_tensor(
                            out=yT[:, c4 * 4:(c4 + 1) * 4, blk * 128:(blk + 1) * 128], in0=pt[:, 0:512].rearrange("p (j t) -> p j t", j=4),
                            in1=par[:, P_SNW + c4 * 4:P_SNW + (c4 + 1) * 4].unsqueeze(2).to_broadcast([128, 4, 128]), op=ALU.mult),
                            reads=[pk, "par"], writes=[("dyT", blk, c4)])
                yT_keys = [("dyT", b, c) for b in range(4) for c in range(8)]

                def fm(glist, rhs_of, rkeys, consume):
                    wts = [load_w(wring, g) for g in glist]
                    nk = 16 * len(wts)
                    for cb in range(4):
                        ps, pk = psb.next()
                        def mm(h, wts=wts, ps=ps, cb=cb):
                            n = 0
                            for wi, (wt, _) in enumerate(wts):
                                for kc in range(16):
                                    ins = h.matmul(ps[:], wt[:, kc, cb * 128:(cb + 1) * 128], rhs_of(wi * 16 + kc), start=(n == 0), stop=(n == nk - 1))
                                    n += 1
                            return ins
                        P.add("pe", mm, reads=[k for _, ks in wts for k in ks] + rkeys, writes=[pk])
                        consume(cb, ps, pk)

                for dg in range(4):
                    def c_ga(cb, ps, pk):
                        P.add("act", lambda h: h.activation(out=sga[:, cb, :], in_=ps[:], func=AF.Sigmoid), reads=[pk], writes=[("dsga", cb)])
                    def c_gs(cb, ps, pk):
                        P.add("act", lambda h: h.activation(out=sgs[:, cb, :], in_=ps[:], func=AF.Sigmoid), reads=[pk], writes=[("dsgs", cb)])
                    def c_a(cb, ps, pk):
                        P.add("dve", lambda h: h.tensor_tensor(out=m1[:, cb, :], in0=ps[:], in1=sga[:, cb, :], op=ALU.mult), reads=[pk, ("dsga", cb)], writes=[("dm1", cb)])
                    def c_b(cb, ps, pk, dg=dg):
                        m2, m2k = m2r.next()
                        P.add("dve", lambda h: h.tensor_tensor(out=m2[:], in0=ps[:], in1=sgs[:, cb, :], op=ALU.mult), reads=[pk, ("dsgs", cb)], writes=[m2k])
                        P.add("pool", lambda h: h.tensor_tensor(out=mT[:, dg * 4 + cb, :], in0=m2[:], in1=m1[:, cb, :], op=ALU.add), reads=[m2k, ("dm1", cb)], writes=[("dmT", dg * 4 + cb)])
                    fm([G_GA[dg]], lambda k: hT[:, k, :], ["dhT"], c_ga)
                    fm([G_GS[dg]], lambda k: hT[:, k, :], ["dhT"], c_gs)
                    fm([G_WA[dg]], lambda k: oT[:, k, :], ["doT"], c_a)
                    fm(G_WS[dg], lambda k: yT[:, k, :], yT_keys, c_b)
                mT_keys = [("dmT", k) for k in range(16)]
                for blk in range(4):
                    xb, xbk = xr4.next()
                    r0 = tm0 + blk * 128
                    P.add("sp", lambda h, xb=xb, r0=r0: h.dma_start(out=xb[:], in_=xe[4096 + r0:4096 + r0 + 128, :]), writes=[xbk], dma=True)
                    for cg in range(4):
                        wt, wkeys = load_w(wring, G_WO[cg])
                        ps, pk = psb.next()
                        def mm(h, wt=wt, ps=ps, blk=blk):
                            for kc in range(16):
                                ins = h.matmul(ps[:], mT[:, kc, blk * 128:(blk + 1) * 128], wt[:, kc, :], start=(kc == 0), stop=(kc == 15))
                            return ins
                        P.add("pe", mm, reads=wkeys + mT_keys, writes=[pk])
                        P.add("dve", lambda h, ps=ps, xb=xb, cg=cg: h.tensor_tensor(out=xb[:, cg * 512:(cg + 1) * 512], in0=ps[:], in1=xb[:, cg * 512:(cg + 1) * 512], op=ALU.add),
                              reads=[pk, xbk], writes=[xbk])
                    stt_, sk = stat.next()
                    P.add("act", lambda h, stt_=stt_, xb=xb: h.activation(out=junk3[:], in_=xb[:], func=AF.Square, accum_out=stt_[:, 0:1]),
                          reads=[xbk], writes=["djunk", (sk, 0)])
                    P.add("dve", lambda h, stt_=stt_: h.tensor_scalar(out=stt_[:, 1:2], in0=stt_[:, 0:1], scalar1=1.0 / D, scalar2=EPS, op0=ALU.mult, op1=ALU.add), reads=[(sk, 0)], writes=[(sk, 1)])
                    P.add("act", lambda h, stt_=stt_: h.activation(out=stt_[:, 2:3], in_=stt_[:, 1:2], func=AF.Sqrt), reads=[(sk, 1)], writes=[(sk, 2)])
                    P.add("dve", lambda h, stt_=stt_: h.reciprocal(out=stt_[:, 3:4], in_=stt_[:, 2:3]), reads=[(sk, 2)], writes=[(sk, 3)])
                    P.add("dve", lambda h, stt_=stt_, xb=xb: h.scalar_tensor_tensor(out=xb[:], in0=xb[:], scalar=stt_[:, 3:4], in1=par[:, P_FNW:P_FNW + 2048], op0=ALU.mult, op1=ALU.mult),
                          reads=[xbk, (sk, 3), "par"], writes=[xbk])
                    final_ops.append(P.add("pool", lambda h, xb=xb, r0=r0: h.dma_start(out=out[r0:r0 + 128, :], in_=xb[:]), reads=[xbk], dma=True))

        P.emit(nc, es, {"pool": final_ops})
    return nc


_CACHE = {}


def _alibi_tables():
    tabs = np.zeros((16, 128, 768), np.float32)
    k = np.arange(128)[:, None]
    q = np.arange(128)[None, :]
    for hd in range(16):
        slope = np.float32(2.0 ** (-8.0 * (hd + 1) / 16))
        for pi, d in enumerate((1, 4, 16)):
            dist_p = (q - k + 128).astype(np.float32)
            prev = np.where(k >= q, -slope * dist_p * d, NEG)
            dist_c = (q - k).astype(np.float32)
            cur = np.where(k <= q, -slope * dist_c * d, NEG)
            tabs[hd, :, pi * 256:pi * 256 + 128] = prev
            tabs[hd, :, pi * 256 + 128:pi * 256 + 256] = cur
    return np.exp(tabs.astype(np.float64)).astype(np.float32)


def kernel(x, norm_w, w_in, conv_w, conv_b, dt_bias, a_log, d_skip, ssm_norm_w,
           w_attn_branch, w_ssm_branch, w_out, final_norm_w):
    f = lambda a: np.ascontiguousarray(np.asarray(a, dtype=np.float32))
    x = f(x)
    if "nc" not in _CACHE:
        _CACHE["nc"] = build_program()
    nc = _CACHE["nc"]
    par = np.zeros((128, NPAR), np.float32)
    par[:, P_NORMW:P_NORMW + 2048] = f(norm_w)[0][None, :]
    par[:, P_FNW:P_FNW + 2048] = f(final_norm_w)[None, :]
    cw = f(conv_w)[0]
    par[:, P_CW:P_CW + 192] = cw.reshape(4, 48, 128).transpose(2, 1, 0).reshape(128, 192)
    par[:, P_CB:P_CB + 48] = f(conv_b)[0].reshape(48, 128).T
    par[:, P_SNW:P_SNW + 32] = f(ssm_norm_w)[0].reshape(32, 128).T
    par[:, P_DTB:P_DTB + 64] = f(dt_bias)[0][None, :]
    par[:, P_ALOG:P_ALOG + 64] = f(a_log)[0][None, :]
    par[:, P_DSK:P_DSK + 64] = f(d_skip)[0][None, :]
    par[:, P_ID:P_ID + 128] = np.eye(128, dtype=np.float32)
    ki = np.arange(128)
    par[:, P_TM:P_TM + 128] = (ki[:, None] <= ki[None, :]).astype(np.float32)
    par[:, P_UM:P_UM + 128] = (ki[:, None] > ki[None, :]).astype(np.float32)
    par[:, P_ONE:P_ONE + 128] = 1.0
    tabs = _alibi_tables()
    tabs0 = tabs.copy()
    for pi in range(3):
        tabs0[:, :, pi * 256:pi * 256 + 128] = 0.0
    wi, wa, wss, wo = f(w_in)[0], f(w_attn_branch)[0], f(w_ssm_branch)[0], f(w_out)[0]
    in_maps = []
    for c in range(NCORE):
        b, hf = c // 2, c % 2
        if hf == 0:
            xe = np.concatenate([np.zeros((4096, D), np.float32), x[b, :4096]], axis=0)
        else:
            xe = x[b]
        p = par.copy()
        p[:, P_PV] = float(hf)
        in_maps.append({"xe": np.ascontiguousarray(xe), "w_in": wi, "w_attn": wa, "w_ssm": wss, "w_out": wo,
                        "params": p, "biasA": tabs, "bias0": tabs if hf == 1 else tabs0})
    if _CACHE.get("return_maps"):
        return in_maps
    res = run_bass_kernel_spmd(nc, in_maps, core_ids=list(range(NCORE)))
    outp = np.empty((4, SEQ, D), np.float32)
    for c in range(NCORE):
        b, hf = c // 2, c % 2
        outp[b, hf * 4096:(hf + 1) * 4096] = res.results[c]["out"]
    return outp
```

```python
import numpy as np
import ml_dtypes
from contextlib import ExitStack
import concourse.bass as bass
import concourse.mybir as mybir
from concourse.bass_utils import run_bass_kernel_spmd

F32 = mybir.dt.float32
BF16 = mybir.dt.bfloat16
AF = mybir.ActivationFunctionType
ALU = mybir.AluOpType

D = 2048
NIN = 22592
SEQ = 8192
NCORE = 8
TOKM = 4096
EXT = 8192
T = 512
NT = EXT // T
HALO0 = 2048
EPS = 1e-6
NEG = -30000.0
DBG_CHUNKS = None
DBG_GROUPS = 8
DBG_SU = 8
DBG_LIMIT = None
DBG_F = None
DBG_SKIP_P0 = False

C_Q, C_K, C_V, C_ZA, C_ZS, C_X, C_B, C_C, C_DT, C_GA, C_GS = 0, 2048, 4096, 6144, 8192, 12288, 16384, 17408, 18432, 18496, 20544

WG = []
def _add(name, src, row0, col0, width):
    WG.append((name, src, row0, col0, width))
    return len(WG) - 1
G_Q = [_add("q%d" % i, "w_in", 0, C_Q + 512 * i, 512) for i in range(4)]
G_K = [_add("k%d" % i, "w_in", 0, C_K + 512 * i, 512) for i in range(4)]
G_V = [_add("v%d" % i, "w_in", 0, C_V + 512 * i, 512) for i in range(4)]
G_ZA = [_add("za%d" % i, "w_in", 0, C_ZA + 512 * i, 512) for i in range(4)]
G_ZS = [_add("zs%d" % i, "w_in", 0, C_ZS + 512 * i, 512) for i in range(8)]
G_XBC = [_add("xbc%d" % i, "w_in", 0, C_X + 512 * i, 512) for i in range(12)]
G_DT = _add("dt", "w_in", 0, C_DT, 64)
G_GA = [_add("ga%d" % i, "w_in", 0, C_GA + 512 * i, 512) for i in range(4)]
G_GS = [_add("gs%d" % i, "w_in", 0, C_GS + 512 * i, 512) for i in range(4)]
G_WA = [_add("wa%d" % i, "w_attn", 0, 512 * i, 512) for i in range(4)]
G_WS = [[_add("ws%d_%d" % (i, kh), "w_ssm", 2048 * kh, 512 * i, 512) for kh in range(2)] for i in range(4)]
G_WO = [_add("wo%d" % i, "w_out", 0, 512 * i, 512) for i in range(4)]
NG = len(WG)
P0_GROUPS = G_XBC + [G_DT]
P1A_JOBS = [(g, q) for g in (G_K + G_V + G_Q + G_ZA + G_ZS + G_GA + G_GS + G_WA + [x for pr in G_WS for x in pr] + G_WO) for q in range(4)]
P2_JOBS = []

P_NORMW = 0
P_FNW = 2048
P_CW = 4096
P_CB = P_CW + 192
P_SNW = P_CB + 48
P_DTB = P_SNW + 32
P_ALOG = P_DTB + 64
P_DSK = P_ALOG + 64
P_PV = P_DSK + 64
P_ID = P_PV + 1
P_TM = P_ID + 128
P_UM = P_TM + 128
P_ONE = P_UM + 128
NPAR = P_ONE + 128


class Op:
    __slots__ = ("eng", "fn", "deps", "dma", "sem", "val", "has_dep", "cnt", "semi")

    def __init__(self, eng, fn, dma):
        self.eng = eng; self.fn = fn; self.dma = dma; self.deps = []
        self.sem = None; self.val = 0; self.has_dep = False; self.cnt = 0; self.semi = 0


class Prog:
    ENGS = ("pe", "act", "dve", "pool", "sp")
    NDS = 14
    CHUNK = 20000

    def __init__(self):
        self.ops = {e: [] for e in self.ENGS}
        self.res = {}
        self.ndma = {e: 0 for e in self.ENGS}
        self.pend = {}

    def barrier(self):
        deps = []
        for e in self.ENGS:
            last_c = None
            dm = {}
            for o in self.ops[e]:
                if o.dma:
                    dm[o.semi] = o
                else:
                    last_c = o
            if last_c is not None:
                deps.append(last_c)
            deps.extend(dm.values())
        self.pend = {e: list(deps) for e in self.ENGS}
        print("[barrier] ops so far:", getattr(self, "nadd", 0))

    def add(self, eng, fn, reads=(), writes=(), dma=False):
        op = Op(eng, fn, dma)
        self.nadd = getattr(self, "nadd", 0) + 1
        if DBG_LIMIT is not None and self.nadd > DBG_LIMIT:
            return op
        deps = list(self.pend.pop(eng, []))
        for k in reads:
            st = self.res.get(k)
            if st is not None and st[0] is not None:
                deps.append(st[0])
        for k in writes:
            st = self.res.get(k)
            if st is not None:
                if st[0] is not None:
                    deps.append(st[0])
                deps.extend(st[1].values())
                deps.extend(st[2])
        seen = set()
        for d in deps:
            if d is op or id(d) in seen:
                continue
            seen.add(id(d))
            if (not d.dma) and d.eng == eng and eng == "pe":
                continue
            op.deps.append(d)
            d.has_dep = True
        for k in reads:
            st = self.res.setdefault(k, [None, {}, []])
            if dma:
                st[2].append(op)
            else:
                st[1][eng] = op
        for k in writes:
            self.res[k] = [op, {}, []]
        if dma:
            n = self.ndma[eng]
            self.ndma[eng] = n + 1
            op.semi = n % self.NDS
            op.val = 16 * (n // self.NDS + 1)
        self.ops[eng].append(op)
        return op

    def emit(self, nc, es, final_waits):
        csem = {}
        for e in ("pe", "act", "dve", "pool"):
            nd = sum(1 for o in self.ops[e] if (not o.dma) and o.has_dep)
            csem[e] = [es.enter_context(nc.semaphore("c_%s_%d" % (e, i))) for i in range(nd // self.CHUNK + 1)]
            c = 0
            for o in self.ops[e]:
                if (not o.dma) and o.has_dep:
                    o.semi = c // self.CHUNK
                    o.cnt = c % self.CHUNK + 1
                    c += 1
        dsem = {}
        for e in self.ENGS:
            if self.ndma[e]:
                dsem[e] = [es.enter_context(nc.semaphore("d_%s_%d" % (e, i))) for i in range(self.NDS)]
        block = es.enter_context(nc.Block())
        prog = self

        wlog = self.wlog = []
        def run(e, h):
            seen = {}
            def wait(sem, val):
                key = id(sem)
                if seen.get(key, 0) >= val:
                    return
                seen[key] = val
                if DBG_LIMIT is not None:
                    wlog.append((e, str(getattr(sem, "name", sem)), val))
                h.wait_ge(sem, val)
            for o in prog.ops[e]:
                for d in o.deps:
                    if d.dma:
                        wait(dsem[d.eng][d.semi], d.val)
                    else:
                        wait(csem[d.eng][d.semi], d.cnt)
                if o.dma:
                    if o.val > 16:
                        wait(dsem[e][o.semi], o.val - 16)
                    ins = o.fn(h)
                    ins.then_inc(dsem[e][o.semi], 16)
                else:
                    ins = o.fn(h)
                    if o.has_dep:
                        ins.then_inc(csem[e][o.semi], 1)
            if e in final_waits:
                for d in final_waits[e]:
                    wait(dsem[d.eng][d.semi], d.val)

        @block.tensor
        def _(h):
            run("pe", h)

        @block.scalar
        def _(h):
            run("act", h)

        @block.vector
        def _(h):
            run("dve", h)

        @block.gpsimd
        def _(h):
            run("pool", h)

        @block.sync
        def _(h):
            run("sp", h)


class Ring:
    def __init__(self, items):
        self.items = items; self.i = 0

    def next(self):
        it = self.items[self.i % len(self.items)]
        self.i += 1
        return it


def build_program(stop=None, dbg=False, tiles=None):
    nc = bass.Bass("TRN2", target_bir_lowering=False)
    P = Prog()

    def din(name, shape, dt=F32):
        return nc.dram_tensor(name, shape, dt, kind="ExternalInput").ap()

    def dscr(name, shape, dt=BF16):
        return nc.dram_tensor(name, shape, dt, kind="ExternalOutput" if dbg else "Internal").ap()

    xe = din("xe", [EXT, D])
    wsrc = {"w_in": din("w_in", [D, NIN]), "w_attn": din("w_attn", [2048, D]),
            "w_ssm": din("w_ssm", [4096, D]), "w_out": din("w_out", [D, D])}
    params = din("params", [128, NPAR])
    biasA = din("biasA", [16, 128, 768])
    bias0 = din("bias0", [16, 128, 768])
    out = nc.dram_tensor("out", [TOKM, D], F32, kind="ExternalOutput").ap()

    wb = dscr("wb", [NG, 128, 16 * 512])
    Qs = dscr("Qs", [16, 128, TOKM])
    Ks = dscr("Ks", [16, 128, 6144])
    Vs = dscr("Vs", [6144, 2048])
    SZa = dscr("SZa", [16, 128, TOKM])
    SZs = dscr("SZs", [TOKM, 4096])
    HnT = dscr("HnT", [16, 128, TOKM])
    Xtok = dscr("Xtok", [EXT, 5120])
    BTs = dscr("BTs", [8, 128, TOKM])
    CTs = dscr("CTs", [8, 128, TOKM])
    DTs = dscr("DTs", [EXT, 64], F32)
    DAs = dscr("DAs", [EXT, 64], F32)
    Yn = dscr("Yn", [TOKM, 4096])
    OaT = dscr("OaT", [16, 128, TOKM])

    with ExitStack() as es:
        sb = lambda name, shape, dt=F32: es.enter_context(nc.sbuf_tensor(name, shape, dt))
        par = sb("par", [128, NPAR])
        idb = sb("idb", [128, 128], BF16)
        oneb = sb("oneb", [128, 128], BF16)
        anegt = sb("anegt", [128, 64])
        P.add("sp", lambda h: h.dma_start(out=par[:], in_=params[:, :]), writes=["par"], dma=True)
        P.add("dve", lambda h: h.tensor_copy(out=idb[:], in_=par[:, P_ID:P_ID + 128]), reads=["par"], writes=["idb"])
        P.add("dve", lambda h: h.tensor_copy(out=oneb[:], in_=par[:, P_ONE:P_ONE + 128]), reads=["par"], writes=["oneb"])
        P.add("act", lambda h: h.activation(out=anegt[:], in_=par[:, P_ALOG:P_ALOG + 64], func=AF.Exp), reads=["par"], writes=["aneg0"])
        P.add("dve", lambda h: h.tensor_scalar(out=anegt[:], in0=anegt[:], scalar1=-1.0, scalar2=None, op0=ALU.mult), reads=["aneg0"], writes=["aneg"])
        Tm = par[:, P_TM:P_TM + 128]
        Um = par[:, P_UM:P_UM + 128]
        ones32 = par[:, P_ONE:P_ONE + 128]

        with ExitStack() as e0:
            s0 = lambda name, shape, dt=F32: e0.enter_context(nc.sbuf_tensor(name, shape, dt))
            wf = [s0("wf%d" % i, [128, 16, 512]) for i in range(2)]
            wc = [s0("wc%d" % i, [128, 16, 512], BF16) for i in range(2)]
            cast_engs = ["act", "dve"]
            for g in ([] if DBG_SKIP_P0 else P0_GROUPS):
                name, src, row0, col0, width = WG[g]
                sl = g % 2
                srcap = wsrc[src][row0:row0 + 2048, col0:col0 + width].rearrange("(kc p) c -> p kc c", p=128)
                for hlf in range(2):
                    P.add("sp", lambda h, sl=sl, srcap=srcap, hlf=hlf, width=width: h.dma_start(
                        out=wf[sl][:, hlf * 8:(hlf + 1) * 8, 0:width], in_=srcap[:, hlf * 8:(hlf + 1) * 8, :]),
                        writes=[("wf", sl, hlf)], dma=True)
                for hlf in range(2):
                    ce = cast_engs[(g + hlf) % 2]
                    if ce == "act":
                        fn = lambda h, sl=sl, hlf=hlf, width=width: h.copy(out=wc[sl][:, hlf * 8:(hlf + 1) * 8, 0:width], in_=wf[sl][:, hlf * 8:(hlf + 1) * 8, 0:width])
                    else:
                        fn = lambda h, sl=sl, hlf=hlf, width=width: h.tensor_copy(out=wc[sl][:, hlf * 8:(hlf + 1) * 8, 0:width], in_=wf[sl][:, hlf * 8:(hlf + 1) * 8, 0:width])
                    P.add(ce, fn, reads=[("wf", sl, hlf)], writes=[("wc", sl, hlf)])
                dst = wb[g].rearrange("p (kc c) -> p kc c", c=512)
                P.add("pool", lambda h, sl=sl, dst=dst, width=width: h.dma_start(out=dst[:, :, 0:width], in_=wc[sl][:, :, 0:width]),
                      reads=[("wc", sl, 0), ("wc", sl, 1)], writes=[("wb", g)], dma=True)

        if stop == "p0":
            P.emit(nc, es, {})
            return nc
        P.barrier()
        def load_w(wring, g):
            wt, key = wring.next()
            width = WG[g][4]
            src = wb[g].rearrange("p (kc c) -> p kc c", c=512)
            for hlf in range(2):
                P.add("sp", lambda h, wt=wt, src=src, hlf=hlf, width=width: h.dma_start(
                    out=wt[:, hlf * 8:(hlf + 1) * 8, 0:width], in_=src[:, hlf * 8:(hlf + 1) * 8, 0:width]),
                    reads=[("wb", g)] + [("wb", g, q) for q in range(4)], writes=[(key, hlf)], dma=True)
            return wt, [(key, 0), (key, 1)]


        def make_converter(sfn, jobs, tag):
            wfq = Ring([(sfn("cvf%s%d" % (tag, i), [128, 4, 512]), "cvf%s%d" % (tag, i)) for i in range(3)])
            wcq = Ring([(sfn("cvc%s%d" % (tag, i), [128, 4, 512], BF16), "cvc%s%d" % (tag, i)) for i in range(2)])
            st = {"ld": 0, "loaded": []}

            def issue_load():
                if st["ld"] >= len(jobs):
                    return
                g, q = jobs[st["ld"]]
                st["ld"] += 1
                name, src, row0, col0, width = WG[g]
                wf, wfk = wfq.next()
                srcap = wsrc[src][row0 + q * 512:row0 + (q + 1) * 512, col0:col0 + width].rearrange("(kc p) c -> p kc c", p=128)
                P.add("act", lambda h: h.dma_start(out=wf[:, :, 0:width], in_=srcap), writes=[wfk], dma=True)
                st["loaded"].append((g, q, wf, wfk, width))

            def tick():
                issue_load()
                if not st["loaded"]:
                    return False
                if len(st["loaded"]) <= 2 and st["ld"] < len(jobs):
                    return True
                g, q, wf, wfk, width = st["loaded"].pop(0)
                wc, wck = wcq.next()
                P.add("act", lambda h: h.copy(out=wc[:, :, 0:width], in_=wf[:, :, 0:width]), reads=[wfk], writes=[wck])
                dst = wb[g].rearrange("p (kc c) -> p kc c", c=512)
                P.add("act", lambda h: h.dma_start(out=dst[:, 4 * q:4 * q + 4, 0:width], in_=wc[:, :, 0:width]), reads=[wck], writes=[("wb", g, q)], dma=True)
                return True

            def flush():
                while tick():
                    pass
            return tick, flush

        evac_rr = [0]

        def evac_eng():
            evac_rr[0] += 1
            return "act" if evac_rr[0] % 2 else "dve"

        def copy_op(eng, out_ap, in_ap, reads, writes):
            if eng == "act":
                return P.add("act", lambda h: h.copy(out=out_ap, in_=in_ap), reads=reads, writes=writes)
            return P.add(eng, lambda h: h.tensor_copy(out=out_ap, in_=in_ap), reads=reads, writes=writes)

        with ExitStack() as e1:
            s1 = lambda name, shape, dt=F32: e1.enter_context(nc.sbuf_tensor(name, shape, dt))
            p1 = lambda name, shape, dt=F32: e1.enter_context(nc.psum_tensor(name, shape, dt))
            wring = Ring([(s1("w1_%d" % i, [128, 16, 512], BF16), "w1_%d" % i) for i in range(3)])
            xbuf = Ring([(s1("xb%d" % i, [128, 2048]), "xb%d" % i) for i in range(2)])
            hnb = Ring([(s1("hnb%d" % i, [128, 2048], BF16), "hnb%d" % i) for i in range(2)])
            junk = s1("junk", [128, 2048], BF16)
            hnT = Ring([(s1("hnT%d" % i, [128, 16, 512], BF16), "hnT%d" % i) for i in range(2)])
            stat = Ring([(s1("stat%d" % i, [128, 4]), "stat%d" % i) for i in range(2)])
            stage = Ring([(s1("stg%d" % i, [128, 512], BF16), "stg%d" % i) for i in range(4)])
            XR = Ring([(s1("XR%d" % i, [128, 515]), "XR%d" % i) for i in range(2)])
            cacc = Ring([(s1("cacc%d" % i, [128, 512]), "cacc%d" % i) for i in range(2)])
            tail = s1("tail", [128, 48, 3])
            xcT = Ring([(s1("xcT%d" % i, [128, 512], BF16), "xcT%d" % i) for i in range(10)])
            xtk = Ring([(s1("xtk%d" % i, [128, 4, 512], BF16), "xtk%d" % i) for i in range(2)])
            dtw = Ring([(s1("dtw%d" % i, [128, 6, 256]), "dtw%d" % i) for i in range(2)])
            psb = Ring([(p1("ps1_%d" % i, [128, 512]), "ps1_%d" % i) for i in range(5)])
            ptr = Ring([(p1("pt1_%d" % i, [128, 1024], BF16), "pt1_%d" % i) for i in range(3)])

            P.add("dve", lambda h: h.memset(tail[:], 0.0), writes=[("tail", c) for c in range(48)])
            cv_tick, cv_flush = make_converter(s1, [] if DBG_SKIP_P0 else P1A_JOBS, "a")

            def emit_hn(ti):
                t0 = ti * T
                tm0 = t0 - 4096
                hT, hTk = hnT.next()
                for blk in range(4):
                    xb, xk = xbuf.next()
                    hb, hk = hnb.next()
                    stt, sk = stat.next()
                    r0 = t0 + blk * 128
                    P.add("sp", lambda h, xb=xb, r0=r0: h.dma_start(out=xb[:], in_=xe[r0:r0 + 128, :]), writes=[xk], dma=True)
                    P.add("act", lambda h, xb=xb, stt=stt: h.activation(out=junk[:], in_=xb[:], func=AF.Square, accum_out=stt[:, 0:1]),
                          reads=[xk], writes=["junk", (sk, 0)])
                    P.add("dve", lambda h, stt=stt: h.tensor_scalar(out=stt[:, 1:2], in0=stt[:, 0:1], scalar1=1.0 / D, scalar2=EPS, op0=ALU.mult, op1=ALU.add),
                          reads=[(sk, 0)], writes=[(sk, 1)])
                    P.add("act", lambda h, stt=stt: h.activation(out=stt[:, 2:3], in_=stt[:, 1:2], func=AF.Sqrt), reads=[(sk, 1)], writes=[(sk, 2)])
                    P.add("dve", lambda h, stt=stt: h.reciprocal(out=stt[:, 3:4], in_=stt[:, 2:3]), reads=[(sk, 2)], writes=[(sk, 3)])
                    P.add("dve", lambda h, xb=xb, hb=hb, stt=stt: h.scalar_tensor_tensor(
                        out=hb[:], in0=xb[:], scalar=stt[:, 3:4], in1=par[:, P_NORMW:P_NORMW + 2048], op0=ALU.mult, op1=ALU.mult),
                        reads=[xk, (sk, 3), "par"], writes=[hk])
                    for q4 in range(4):
                        pt, pk = ptr.next()
                        def tr(h, pt=pt, hb=hb, q4=q4):
                            for j in range(4):
                                kc = q4 * 4 + j
                                ins = h.transpose(out=pt[:, j * 128:(j + 1) * 128], in_=hb[:, kc * 128:(kc + 1) * 128], identity=idb[:])
                            return ins
                        P.add("pe", tr, reads=[hk, "idb"], writes=[pk])
                        copy_op(evac_eng(), hT[:, q4 * 4:(q4 + 1) * 4, blk * 128:(blk + 1) * 128],
                                pt[:, 0:512].rearrange("p (j t) -> p j t", j=4), [pk], [(hTk, blk, q4)])
                hT_keys = [(hTk, b, q) for b in range(4) for q in range(4)]
                if ti >= 8:
                    P.add("pool", lambda h, hT=hT, tm0=tm0: h.dma_start(out=HnT[:, :, tm0:tm0 + T].rearrange("k p t -> p k t"), in_=hT[:]),
                          reads=hT_keys, writes=[("HnT", ti)], dma=True)
                return hT, hT_keys

            tile_list = list(tiles if tiles is not None else range(NT))
            hn_ready = {}
            if tile_list:
                hn_ready[tile_list[0]] = emit_hn(tile_list[0])
            for tidx, ti in enumerate(tile_list):
                main = ti >= 8
                halo = ti >= 4
                t0 = ti * T
                tm0 = t0 - 4096
                tk0 = t0 - HALO0
                hT, hT_keys = hn_ready.pop(ti)
                def fm_group(g, consume):
                    cv_tick()
                    wt, wkeys = load_w(wring, g)
                    for cb in range(4):
                        ps, pk = psb.next()
                        def mm(h, wt=wt, ps=ps, cb=cb, hT=hT):
                            for kc in range(16):
                                ins = h.matmul(ps[:], wt[:, kc, cb * 128:(cb + 1) * 128], hT[:, kc, :], start=(kc == 0), stop=(kc == 15))
                            return ins
                        P.add("pe", mm, reads=wkeys + hT_keys, writes=[pk])
                        consume(cb, ps, pk)

                def tm_group(g, consume, width=512):
                    cv_tick()
                    wt, wkeys = load_w(wring, g)
                    for blk in range(4):
                        ps, pk = psb.next()
                        def mm(h, wt=wt, ps=ps, blk=blk, hT=hT):
                            for kc in range(16):
                                ins = h.matmul(ps[:, 0:width], hT[:, kc, blk * 128:(blk + 1) * 128], wt[:, kc, 0:width], start=(kc == 0), stop=(kc == 15))
                            return ins
                        P.add("pe", mm, reads=wkeys + hT_keys, writes=[pk])
                        consume(blk, ps, pk)

                def store_fm(dst3, hidx_base, toff, func=None, nm=None):
                    def consume(cb, ps, pk):
                        sg, sgk = stage.next()
                        if func is None:
                            copy_op(evac_eng(), sg[:], ps[:], [pk], [sgk])
                        else:
                            P.add("act", lambda h: h.activation(out=sg[:], in_=ps[:], func=func), reads=[pk], writes=[sgk])
                        hidx = hidx_base + cb
                        P.add("pool", lambda h: h.dma_start(out=dst3[hidx, :, toff:toff + T], in_=sg[:]), reads=[sgk], writes=[(nm, ti, hidx)], dma=True)
                    return consume

                def store_tm(dst2, col0, roff, func=None, nm=None):
                    def consume(blk, ps, pk):
                        sg, sgk = stage.next()
                        if func is None:
                            copy_op(evac_eng(), sg[:], ps[:], [pk], [sgk])
                        else:
                            P.add("act", lambda h: h.activation(out=sg[:], in_=ps[:], func=func), reads=[pk], writes=[sgk])
                        r0 = roff + blk * 128
                        P.add("pool", lambda h: h.dma_start(out=dst2[r0:r0 + 128, col0:col0 + 512], in_=sg[:]), reads=[sgk], writes=[(nm, ti, col0 // 512, blk)], dma=True)
                    return consume

                if main:
                    for i in range(4):
                        fm_group(G_Q[i], store_fm(Qs, 4 * i, tm0, nm="Qs"))
                if halo:
                    for i in range(4):
                        fm_group(G_K[i], store_fm(Ks, 4 * i, tk0, nm="Ks"))
                    for i in range(4):
                        tm_group(G_V[i], store_tm(Vs, 512 * i, tk0, nm="Vs"))
                if main:
                    for i in range(4):
                        fm_group(G_ZA[i], store_fm(SZa, 4 * i, tm0, AF.Silu, nm="SZa"))
                    for i in range(8):
                        tm_group(G_ZS[i], store_tm(SZs, 512 * i, tm0, AF.Silu, nm="SZs"))

                pend_tr = None
                pend_silu = []
                for i in range(12):
                    xcs = []
                    def consume(cb, ps, pk, i=i, xcs=xcs):
                        cbi = 4 * i + cb
                        xr, xrk = XR.next()
                        ca, cak = cacc.next()
                        xc, xck = xcT.next()
                        P.add("act", lambda h: h.copy(out=xr[:, 3:515], in_=ps[:]), reads=[pk], writes=[(xrk, 1)])
                        while pend_silu:
                            pend_silu.pop(0)()
                        P.add("dve", lambda h: h.tensor_copy(out=xr[:, 0:3], in_=tail[:, cbi, :]), reads=[("tail", cbi)], writes=[(xrk, 0)])
                        P.add("dve", lambda h: h.tensor_copy(out=tail[:, cbi, :], in_=xr[:, 512:515]), reads=[(xrk, 1), (xrk, 0)], writes=[("tail", cbi)])
                        cw = lambda k: par[:, P_CW + cbi * 4 + k:P_CW + cbi * 4 + k + 1]
                        P.add("dve", lambda h: h.tensor_scalar(out=ca[:], in0=xr[:, 3:515], scalar1=cw(3), scalar2=par[:, P_CB + cbi:P_CB + cbi + 1],
                                                               op0=ALU.mult, op1=ALU.add), reads=[(xrk, 1), (xrk, 0), "par"], writes=[cak])
                        for k in (2, 1, 0):
                            P.add("dve", lambda h, k=k: h.scalar_tensor_tensor(out=ca[:], in0=xr[:, k:k + 512], scalar=cw(k), in1=ca[:],
                                                                                op0=ALU.mult, op1=ALU.add), reads=[cak, (xrk, 1), (xrk, 0)], writes=[cak])
                        def do_silu(xc=xc, ca=ca, cak=cak, xck=xck):
                            P.add("act", lambda h: h.activation(out=xc[:], in_=ca[:], func=AF.Silu), reads=[cak], writes=[xck])
                        pend_silu.append(do_silu)
                        xcs.append((xc, xck))
                        if i >= 8 and main:
                            gi = cbi - 32 if i < 10 else cbi - 40
                            dst = BTs if i < 10 else CTs
                            def do_store(xc=xc, xck=xck, dst=dst, gi=gi, tm0=tm0, ti=ti, cbi=cbi):
                                P.add("pool", lambda h: h.dma_start(out=dst[gi, :, tm0:tm0 + T], in_=xc[:]), reads=[xck], writes=[("BC", ti, cbi)], dma=True)
                            pend_silu.append(do_store)
                    fm_group(G_XBC[i], consume)
                    if pend_tr is not None:
                        pend_tr()
                        pend_tr = None
                    if i < 10:
                        def do_tr(i=i, xcs=xcs, t0=t0):
                            xt_, xtkk = xtk.next()
                            for ch in range(4):
                                pt, pk = ptr.next()
                                def tr(h, pt=pt, ch=ch, xcs=xcs):
                                    for j in range(4):
                                        ins = h.transpose(out=pt[:, j * 128:(j + 1) * 128], in_=xcs[j][0][:, ch * 128:(ch + 1) * 128], identity=idb[:])
                                    return ins
                                P.add("pe", tr, reads=[k for _, k in xcs] + ["idb"], writes=[pk])
                                copy_op(evac_eng(), xt_[:, ch, :], pt[:, 0:512], [pk], [(xtkk, ch)])
                            P.add("pool", lambda h, xt_=xt_, i=i, t0=t0: h.dma_start(
                                out=Xtok[t0:t0 + T, 512 * i:512 * (i + 1)].rearrange("(c p) f -> p c f", p=128), in_=xt_[:]),
                                reads=[(xtkk, c) for c in range(4)], writes=[("Xtok", ti, i)], dma=True)
                        pend_tr = do_tr
                    if i == 5 and tidx + 1 < len(tile_list):
                        hn_ready[tile_list[tidx + 1]] = emit_hn(tile_list[tidx + 1])
                    pass
                while pend_silu:
                    pend_silu.pop(0)()
                if pend_tr is not None:
                    pend_tr()
                    pend_tr = None

                dw, dwk = dtw.next()
                wt, wkeys = load_w(wring, G_DT)
                ps, pk = psb.next()
                def mmdt(h, wt=wt, ps=ps, hT=hT):
                    for blk in range(4):
                        for kc in range(16):
                            ins = h.matmul(ps[:, blk * 64:(blk + 1) * 64], hT[:, kc, blk * 128:(blk + 1) * 128], wt[:, kc, 0:64], start=(kc == 0), stop=(kc == 15))
                    return ins
                P.add("pe", mmdt, reads=wkeys + hT_keys, writes=[pk])
                dtb_bc = par[:, P_DTB:P_DTB + 64].unsqueeze(1).to_broadcast([128, 4, 64])
                an_bc = anegt[:, :].unsqueeze(1).to_broadcast([128, 4, 64])
                v3 = lambda a: a.rearrange("p (b j) -> p b j", b=4)
                z, nz, mn, ex, dtv, dav = (dw[:, i, :] for i in range(6))
                P.add("dve", lambda h, ps=ps, z=z: h.tensor_tensor(out=v3(z), in0=v3(ps[:, 0:256]), in1=dtb_bc, op=ALU.add), reads=[pk, "par"], writes=[(dwk, 0)])
                P.add("dve", lambda h, z=z, nz=nz: h.tensor_scalar(out=nz, in0=z, scalar1=-1.0, scalar2=None, op0=ALU.mult), reads=[(dwk, 0)], writes=[(dwk, 1)])
                P.add("dve", lambda h, z=z, nz=nz, mn=mn: h.tensor_tensor(out=mn, in0=z, in1=nz, op=ALU.min), reads=[(dwk, 0), (dwk, 1)], writes=[(dwk, 2)])
                P.add("act", lambda h, mn=mn, ex=ex: h.activation(out=ex, in_=mn, func=AF.Exp), reads=[(dwk, 2)], writes=[(dwk, 3)])
                P.add("act", lambda h, ex=ex: h.activation(out=ex, in_=ex, func=AF.Ln, bias=1.0), reads=[(dwk, 3)], writes=[(dwk, 3)])
                P.add("dve", lambda h, z=z, nz=nz: h.tensor_scalar(out=nz, in0=z, scalar1=0.0, scalar2=None, op0=ALU.max), reads=[(dwk, 0), (dwk, 2)], writes=[(dwk, 1)])
                P.add("dve", lambda h, nz=nz, ex=ex, dtv=dtv: h.tensor_tensor(out=dtv, in0=nz, in1=ex, op=ALU.add), reads=[(dwk, 1), (dwk, 3)], writes=[(dwk, 4)])
                P.add("dve", lambda h, dtv=dtv, dav=dav: h.tensor_tensor(out=v3(dav), in0=v3(dtv), in1=an_bc, op=ALU.mult), reads=[(dwk, 4), "aneg"], writes=[(dwk, 5)])
                P.add("pool", lambda h, dtv=dtv, t0=t0: h.dma_start(out=DTs[t0:t0 + T, :].rearrange("(b p) j -> p b j", p=128), in_=v3(dtv)), reads=[(dwk, 4)], writes=[("DT", ti)], dma=True)
                P.add("pool", lambda h, dav=dav, t0=t0: h.dma_start(out=DAs[t0:t0 + T, :].rearrange("(b p) j -> p b j", p=128), in_=v3(dav)), reads=[(dwk, 5)], writes=[("DA", ti)], dma=True)
            cv_flush()


        if stop == "p1a":
            P.emit(nc, es, {})
            return nc
        P.barrier()
        INV = 1.0 / float(np.sqrt(128.0))
        with ExitStack() as e1:
            s1 = lambda name, shape, dt=F32: e1.enter_context(nc.sbuf_tensor(name, shape, dt))
            p1 = lambda name, shape, dt=F32: e1.enter_context(nc.psum_tensor(name, shape, dt))
            xtr = Ring([(s1("bxt%d" % i, [128, 5120], BF16), "bxt%d" % i) for i in range(3)])
            ddr = Ring([(s1("bdd%d" % i, [128, 2, 64]), "bdd%d" % i) for i in range(3)])
            bcr = Ring([(s1("bbc%d" % i, [128, 2, 8, 128], BF16), "bbc%d" % i) for i in range(3)])
            szr = Ring([(s1("bsz%d" % i, [128, 4096], BF16), "bsz%d" % i) for i in range(2)])
            smr = Ring([(s1("bsm%d" % i, [128, 5, 64]), "bsm%d" % i) for i in range(3)])
            xdr = Ring([(s1("bxd%d" % i, [128, 4096], BF16), "bxd%d" % i) for i in range(2)])
            xwr = Ring([(s1("bxw%d" % i, [128, 4096], BF16), "bxw%d" % i) for i in range(2)])
            cbr = Ring([(s1("bcb%d" % i, [128, 8, 128], BF16), "bcb%d" % i) for i in range(2)])
            dur = Ring([(s1("bdu%d" % i, [128, 8, 128]), "bdu%d" % i) for i in range(2)])
            der = Ring([(s1("bde%d" % i, [128, 1024], BF16), "bde%d" % i) for i in range(2)])
            mtr = Ring([(s1("bmt%d" % i, [128, 8, 128], BF16), "bmt%d" % i) for i in range(2)])
            t1r = Ring([(s1("bt1%d" % i, [128, 512]), "bt1%d" % i) for i in range(2)])
            t2r = Ring([(s1("bt2%d" % i, [128, 512]), "bt2%d" % i) for i in range(2)])
            t3r = Ring([(s1("bt3%d" % i, [128, 512]), "bt3%d" % i) for i in range(2)])
            ygf = s1("bygf", [128, 4096])
            ynr = Ring([(s1("byn%d" % i, [128, 4096], BF16), "byn%d" % i) for i in range(1)])
            ssr = Ring([(s1("bss%d" % i, [128, 12]), "bss%d" % i) for i in range(2)])
            jk = s1("bjunk", [128, 512], BF16)
            stt = s1("bst", [128, 8, 512])
            stb = s1("bstb", [128, 8, 512], BF16)
            pa = p1("bpa", [128, 512])
            pcbr = Ring([(p1("bpcb%d" % i, [128, 512]), "bpcb%d" % i) for i in range(2)])
            pseg = p1("bpseg", [128, 1024])
            py = p1("bpy", [128, 512])
            po = p1("bpo", [128, 512])
            pst = p1("bpst", [128, 512])

            P.add("dve", lambda h: h.memset(stt[:], 0.0), writes=[("st", g) for g in range(8)])
            P.add("pool", lambda h: h.memset(stb[:], 0.0), writes=[("stb", g) for g in range(8)])
            dsk = par[:, P_DSK:P_DSK + 64]
            v8 = lambda a: a.rearrange("p (j q) -> p j q", j=8)

            b64 = lambda a: a.unsqueeze(2).to_broadcast([128, 64, 64])
            v64 = lambda a: a.rearrange("p (j q) -> p j q", j=64)
            b8 = lambda a: a.unsqueeze(2).to_broadcast([128, 8, 64])

            def header(ci):
                c = dict(ci=ci, main=ci >= 32, ti=ci // 4, r0=ci * 128, rm0=ci * 128 - 4096)
                ti, r0, rm0 = c["ti"], c["r0"], c["rm0"]
                xt, xk = xtr.next(); dd, dk = ddr.next(); sm, smk = smr.next(); xd, xdk = xdr.next(); xw, xwk = xwr.next()
                c.update(xt=xt, xk=xk, dd=dd, dk=dk, sm=sm, smk=smk, xd=xd, xdk=xdk, xw=xw, xwk=xwk)
                P.add("sp", lambda h: h.dma_start(out=xt[:], in_=Xtok[r0:r0 + 128, :]), reads=[("Xtok", ti, i) for i in range(10)], writes=[xk], dma=True)
                P.add("sp", lambda h: h.dma_start(out=dd[:, 0, :], in_=DTs[r0:r0 + 128, :]), reads=[("DT", ti)], writes=[(dk, 0)], dma=True)
                P.add("sp", lambda h: h.dma_start(out=dd[:, 1, :], in_=DAs[r0:r0 + 128, :]), reads=[("DA", ti)], writes=[(dk, 1)], dma=True)
                dtt = dd[:, 0, :]; dat = dd[:, 1, :]
                c.update(dtt=dtt, dat=dat)
                if c["main"]:
                    bc, bck = bcr.next(); sz, szk = szr.next()
                    c.update(bc=bc, bck=bck, sz=sz, szk=szk)
                    P.add("sp", lambda h: h.dma_start(out=bc[:, 0, :, :], in_=BTs[:, :, rm0:rm0 + 128].rearrange("g n t -> n g t")),
                          reads=[("BC", ti, cc) for cc in range(32, 40)], writes=[(bck, 0)], dma=True)
                    P.add("sp", lambda h: h.dma_start(out=bc[:, 1, :, :], in_=CTs[:, :, rm0:rm0 + 128].rearrange("g n t -> n g t")),
                          reads=[("BC", ti, cc) for cc in range(40, 48)], writes=[(bck, 1)], dma=True)
                    P.add("sp", lambda h: h.dma_start(out=sz[:], in_=SZs[rm0:rm0 + 128, :]),
                          reads=[("SZs", ti, i, ci % 4) for i in range(8)], writes=[szk], dma=True)
                def mm_ac(h):
                    h.matmul(pa[:, 0:64], Tm, dat, start=True, stop=True)
                    return h.matmul(pa[:, 64:128], ones32, dat, start=True, stop=True)
                P.add("pe", mm_ac, reads=[(dk, 1), "par"], writes=["pa"])
                acs, eac, ela, dws, wsv = (sm[:, i, :] for i in range(5))
                c.update(eac=eac, ela=ela)
                P.add("dve", lambda h: h.tensor_copy(out=acs, in_=pa[:, 0:64]), reads=["pa"], writes=[(smk, 0), "pa"])
                P.add("act", lambda h: h.activation(out=eac, in_=acs, func=AF.Exp), reads=[(smk, 0)], writes=[(smk, 1)])
                P.add("act", lambda h: h.activation(out=ela, in_=pa[:, 64:128], func=AF.Exp), reads=["pa"], writes=[(smk, 2), "pa"])
                P.add("dve", lambda h: h.tensor_tensor(out=dws, in0=pa[:, 64:128], in1=acs, op=ALU.subtract), reads=["pa", (smk, 0)], writes=[(smk, 3), "pa"])
                P.add("act", lambda h: h.activation(out=wsv, in_=dws, func=AF.Exp), reads=[(smk, 3)], writes=[(smk, 4)])
                if c["main"]:
                    P.add("dve", lambda h: h.tensor_tensor(out=v64(xd[:]), in0=v64(xt[:, 0:4096]), in1=b64(dtt), op=ALU.mult), reads=[xk, (dk, 0)], writes=[xdk])
                P.add("dve", lambda h: h.tensor_tensor(out=wsv, in0=wsv, in1=dtt, op=ALU.mult), reads=[(smk, 4), (dk, 0)], writes=[(smk, 4)])
                P.add("dve", lambda h: h.tensor_tensor(out=v64(xw[:]), in0=v64(xt[:, 0:4096]), in1=b64(wsv), op=ALU.mult), reads=[xk, (smk, 4)], writes=[xwk])
                if c["main"]:
                    c["ss"], c["ssk"] = ssr.next()
                    cb8, cb8k = cbr.next()
                    c.update(cb8=cb8, cb8k=cb8k)
                    for half in range(2):
                        pcb, pcbk = pcbr.next()
                        def mm_cb(h, pcb=pcb, half=half):
                            for gg in range(4):
                                g = half * 4 + gg
                                ins = h.matmul(pcb[:, gg * 128:(gg + 1) * 128], bc[:, 0, g, :], bc[:, 1, g, :], start=True, stop=True)
                            return ins
                        P.add("pe", mm_cb, reads=[(bck, 0), (bck, 1)], writes=[pcbk])
                        P.add("dve", lambda h, pcb=pcb, half=half: h.tensor_tensor(
                            out=cb8[:, half * 4:(half + 1) * 4, :], in0=pcb[:].rearrange("p (g l) -> p g l", g=4),
                            in1=Tm.unsqueeze(1).to_broadcast([128, 4, 128]), op=ALU.mult), reads=[pcbk, "par"], writes=[(cb8k, half)])
                return c

            def stageA(c, g):
                bc, bck, dat, dk = c["bc"], c["bck"], c["dat"], c["dk"]
                du, duk = dur.next(); de, dek = der.next()
                cbm = c["cb8"][:, g, :]; cbk = (c["cb8k"], g // 4)
                P.add("dve", lambda h: h.tensor_tensor(out=du[:], in0=Um.unsqueeze(1).to_broadcast([128, 8, 128]),
                                                       in1=dat[:, g * 8:(g + 1) * 8].unsqueeze(2).to_broadcast([128, 8, 128]), op=ALU.mult),
                      reads=[(dk, 1), "par"], writes=[duk])
                def mm_seg(h):
                    for j in range(8):
                        ins = h.matmul(pseg[:, j * 128:(j + 1) * 128], du[:, j, :], Tm, start=True, stop=True)
                    return ins
                P.add("pe", mm_seg, reads=[duk, "par"], writes=["pseg"])
                P.add("act", lambda h: h.activation(out=de[:], in_=pseg[:], func=AF.Exp), reads=["pseg"], writes=[dek])
                return dict(cbm=cbm, cbk=cbk, de=de, dek=dek)

            def stageB1(c, gc, g):
                bc, bck, xd, xdk = c["bc"], c["bck"], c["xd"], c["xdk"]
                cbm, cbk, de, dek = gc["cbm"], gc["cbk"], gc["de"], gc["dek"]
                mt, mtk = mtr.next()
                P.add("dve", lambda h: h.tensor_tensor(out=mt[:], in0=de[:].rearrange("p (j l) -> p j l", j=8), in1=cbm.unsqueeze(1).to_broadcast([128, 8, 128]), op=ALU.mult),
                      reads=[dek, cbk], writes=[mtk])
                def mm_y(h):
                    for j in range(8):
                        c0 = (g * 8 + j) * 64
                        ins = h.matmul(py[:, j * 64:(j + 1) * 64], mt[:, j, :], xd[:, c0:c0 + 64], start=True, stop=True)
                    return ins
                P.add("pe", lambda h: h.matmul(po[:], bc[:, 1, g, :], stb[:, g, :], start=True, stop=True), reads=[(bck, 1), ("stb", g)], writes=["po"])
                P.add("pe", mm_y, reads=[mtk, xdk], writes=["py"])

            def stageB2(c, g):
                xt, xk, sz, szk = c["xt"], c["xk"], c["sz"], c["szk"]
                eac, smk, ss, ssk = c["eac"], c["smk"], c["ss"], c["ssk"]
                t1, t1k = t1r.next(); t2, t2k = t2r.next(); t3, t3k = t3r.next()
                gs = slice(g * 512, (g + 1) * 512)
                P.add("dve", lambda h: h.tensor_tensor(out=v8(t1[:]), in0=v8(po[:]), in1=b8(eac[:, g * 8:(g + 1) * 8]), op=ALU.mult), reads=["po", (smk, 1)], writes=[t1k])
                P.add("pool", lambda h: h.tensor_tensor(out=v8(t3[:]), in0=v8(xt[:, gs]), in1=b8(dsk[:, g * 8:(g + 1) * 8]), op=ALU.mult), reads=[xk, "par"], writes=[t3k])
                P.add("dve", lambda h: h.tensor_tensor(out=t2[:], in0=py[:], in1=t1[:], op=ALU.add), reads=["py", t1k], writes=[t2k])
                P.add("pool", lambda h: h.tensor_tensor(out=t3[:], in0=t2[:], in1=t3[:], op=ALU.add), reads=[t2k, t3k], writes=[t3k])
                P.add("pool", lambda h: h.tensor_tensor(out=ygf[:, gs], in0=t3[:], in1=sz[:, gs], op=ALU.mult), reads=[t3k, szk], writes=[("ygf", g)])
                P.add("act", lambda h: h.activation(out=jk[:], in_=ygf[:, gs], func=AF.Square, accum_out=ss[:, g:g + 1]), reads=[("ygf", g)], writes=["bjunk", (ssk, g)])

            def finish(c):
                ss, ssk, rm0, ci = c["ss"], c["ssk"], c["rm0"], c["ci"]
                yn, ynk = ynr.next()
                P.add("dve", lambda h: h.tensor_reduce(out=ss[:, 8:9], in_=ss[:, 0:8], axis=mybir.AxisListType.X, op=ALU.add), reads=[(ssk, g) for g in range(8)], writes=[(ssk, 8)])
                P.add("dve", lambda h: h.tensor_scalar(out=ss[:, 9:10], in0=ss[:, 8:9], scalar1=1.0 / 4096, scalar2=EPS, op0=ALU.mult, op1=ALU.add), reads=[(ssk, 8)], writes=[(ssk, 9)])
                P.add("act", lambda h: h.activation(out=ss[:, 10:11], in_=ss[:, 9:10], func=AF.Sqrt), reads=[(ssk, 9)], writes=[(ssk, 10)])
                P.add("dve", lambda h: h.reciprocal(out=ss[:, 11:12], in_=ss[:, 10:11]), reads=[(ssk, 10)], writes=[(ssk, 11)])
                P.add("dve", lambda h: h.tensor_scalar(out=yn[:], in0=ygf[:], scalar1=ss[:, 11:12], scalar2=None, op0=ALU.mult), reads=[("ygf", g) for g in range(8)] + [(ssk, 11)], writes=[ynk])
                P.add("pool", lambda h: h.dma_start(out=Yn[rm0:rm0 + 128, :], in_=yn[:]), reads=[ynk], writes=[("Yn", ci)], dma=True)

            def state_update(c):
                xt, xk, xw, xwk, ela, smk, ci = c["xt"], c["xk"], c["xw"], c["xwk"], c["ela"], c["smk"], c["ci"]
                for g in range(DBG_SU):
                    gs = slice(g * 512, (g + 1) * 512)
                    P.add("pe", lambda h, g=g, gs=gs: h.matmul(pst[:], xt[:, 4096 + g * 128:4096 + (g + 1) * 128], xw[:, gs], start=True, stop=True), reads=[xk, xwk], writes=["pst"])
                    P.add("dve", lambda h, g=g: h.tensor_tensor(out=v8(stt[:, g, :]), in0=v8(stt[:, g, :]), in1=b8(ela[:, g * 8:(g + 1) * 8]), op=ALU.mult), reads=[("st", g), (smk, 2)], writes=[("st", g)])
                    P.add("dve", lambda h, g=g: h.tensor_tensor(out=stt[:, g, :], in0=pst[:], in1=stt[:, g, :], op=ALU.add), reads=["pst", ("st", g)], writes=[("st", g)])
                    if ci == 31:
                        P.add("dve", lambda h, g=g: h.tensor_scalar(out=stt[:, g, :], in0=stt[:, g, :], scalar1=par[:, P_PV:P_PV + 1], scalar2=None, op0=ALU.mult), reads=[("st", g), "par"], writes=[("st", g)])
                    if ci >= 31 and ci < 63:
                        P.add("act", lambda h, g=g: h.copy(out=stb[:, g, :], in_=stt[:, g, :]), reads=[("st", g)], writes=[("stb", g)])

            chunks = list(DBG_CHUNKS if DBG_CHUNKS is not None else range(64))
            ctxs = {}
            if chunks:
                ctxs[0] = header(chunks[0])
            for idx, ci in enumerate(chunks):
                c = ctxs.pop(idx)
                if idx + 1 < len(chunks):
                    ctxs[idx + 1] = header(chunks[idx + 1])
                if c["main"]:
                    gcs = {0: stageA(c, 0)}
                    for g in range(8):
                        if g + 1 < 8:
                            gcs[g + 1] = stageA(c, g + 1)
                        if g >= 1:
                            stageB2(c, g - 1)
                        stageB1(c, gcs.pop(g), g)
                    stageB2(c, 7)
                    state_update(c)
                    finish(c)
                else:
                    state_update(c)

        if stop == "p1b":
            P.emit(nc, es, {})
            return nc
        P.barrier()
        with ExitStack() as e2:
            s2 = lambda name, shape, dt=F32: e2.enter_context(nc.sbuf_tensor(name, shape, dt))
            p2 = lambda name, shape, dt=F32: e2.enter_context(nc.psum_tensor(name, shape, dt))
            qtr = Ring([(s2("cq%d" % i, [128, 2048], BF16), "cq%d" % i) for i in range(2)])
            ktr = Ring([(s2("ck%d" % i, [128, 4096], BF16), "ck%d" % i) for i in range(2)])
            zar = Ring([(s2("cz%d" % i, [128, 2048], BF16), "cz%d" % i) for i in range(2)])
            vpr = Ring([(s2("cv%d" % i, [128, 3, 32, 128], BF16), "cv%d" % i) for i in range(2)])
            bir = Ring([(s2("cb%d" % i, [128, 2, 768]), "cb%d" % i) for i in range(1)])
            nar = Ring([(s2("cn%d" % i, [128, 2048]), "cn%d" % i) for i in range(2)])
            dar = Ring([(s2("cd%d" % i, [128, 2048]), "cd%d" % i) for i in range(2)])
            ssb = Ring([(s2("cs%d" % i, [128, 1024], BF16), "cs%d" % i) for i in range(2)])
            bbr = Ring([(s2("cbb%d" % i, [128, 2, 768], BF16), "cbb%d" % i) for i in range(2)])
            ptb = Ring([(s2("cp%d" % i, [128, 1024], BF16), "cp%d" % i) for i in range(2)])
            oar = Ring([(s2("co%d" % i, [128, 2048], BF16), "co%d" % i) for i in range(2)])
            pSr = Ring([(p2("cpS%d" % i, [128, 1024]), "cpS%d" % i) for i in range(2)])
            pOr = Ring([(p2("cpO%d" % i, [128, 512]), "cpO%d" % i) for i in range(2)])
            pDr = Ring([(p2("cpD%d" % i, [128, 512]), "cpD%d" % i) for i in range(2)])

            cv2_tick, cv2_flush = make_converter(s2, [] if DBG_SKIP_P0 else P2_JOBS, "b")

            def strided(ap, start, step, n=128):
                return ap[:, start:start + step * (n - 1) + 1:step]

            def do_head(st_, hd, w0, tiles_q, tiles_k):
                qt, qk = qtr.next(); kt, kk = ktr.next(); za, zk = zar.next(); vp, vk = vpr.next(); bi, bk = bir.next()
                na, nk = nar.next(); da, dak = dar.next(); oa, ok = oar.next()
                P.add("sp", lambda h, qt=qt, hd=hd, w0=w0: h.dma_start(out=qt[:], in_=Qs[hd, :, w0:w0 + 2048]), reads=[("Qs", t, hd) for t in tiles_q], writes=[qk], dma=True)
                P.add("sp", lambda h, kt=kt, hd=hd, w0=w0: h.dma_start(out=kt[:], in_=Ks[hd, :, w0:w0 + 4096]), reads=[("Ks", t, hd) for t in tiles_k], writes=[kk], dma=True)
                P.add("sp", lambda h, za=za, hd=hd, w0=w0: h.dma_start(out=za[:], in_=SZa[hd, :, w0:w0 + 2048]), reads=[("SZa", t, hd) for t in tiles_q], writes=[zk], dma=True)
                vkeys = [("Vs", t, hd // 4, b) for t in tiles_k for b in range(4)]
                vsrc = Vs[w0:w0 + 4096, hd * 128:(hd + 1) * 128]
                P.add("sp", lambda h, vp=vp, vsrc=vsrc: h.dma_start(out=vp[:, 0, :, :], in_=vsrc.rearrange("(b p) e -> p b e", p=128)), reads=vkeys, writes=[(vk, 0)], dma=True)
                for r in range(4):
                    P.add("sp", lambda h, vp=vp, vsrc=vsrc, r=r: h.dma_start(
                        out=vp[:, 1, r * 8:(r + 1) * 8, :], in_=vsrc.rearrange("(i p r) e -> p r i e", p=128, r=4)[:, r, :, :]), reads=vkeys, writes=[(vk, 1, r)], dma=True)
                for r in range(16):
                    P.add("sp", lambda h, vp=vp, vsrc=vsrc, r=r: h.dma_start(
                        out=vp[:, 2, r * 2:(r + 1) * 2, :], in_=vsrc.rearrange("(i p r) e -> p r i e", p=128, r=16)[:, r, :, :]), reads=vkeys, writes=[(vk, 2, r)], dma=True)
                vallk = [(vk, 0)] + [(vk, 1, r) for r in range(4)] + [(vk, 2, r) for r in range(16)]
                P.add("sp", lambda h, bi=bi, hd=hd: h.dma_start(out=bi[:, 0, :], in_=biasA[hd]), writes=[(bk, 0)], dma=True)
                P.add("sp", lambda h, bi=bi, hd=hd: h.dma_start(out=bi[:, 1, :], in_=bias0[hd]), writes=[(bk, 1)], dma=True)
                bib, bbk = bbr.next()
                P.add("act", lambda h, bi=bi, bib=bib: h.copy(out=bib[:], in_=bi[:]), reads=[(bk, 0), (bk, 1)], writes=[bbk])
                def stageS(pi, d, gq):
                    units = []
                    for u in range(4):
                        if d == 1:
                            qb = 16 + 4 * gq + u
                            kp = kt[:, (qb - 1) * 128:qb * 128]; kc_ = kt[:, qb * 128:(qb + 1) * 128]
                            qa = qt[:, (qb - 16) * 128:(qb - 15) * 128]
                            units.append((kp, kc_, qa, qb - 1, qb, qb == 16))
                        elif d == 4:
                            i = 4 + gq; r = u
                            kp = strided(kt, (i - 1) * 512 + r, 4); kc_ = strided(kt, i * 512 + r, 4)
                            qa = strided(qt, (i - 4) * 512 + r, 4)
                            units.append((kp, kc_, qa, r * 8 + i - 1, r * 8 + i, i == 4))
                        else:
                            r = 4 * gq + u
                            kp = strided(kt, r, 16); kc_ = strided(kt, 2048 + r, 16)
                            qa = strided(qt, r, 16)
                            units.append((kp, kc_, qa, r * 2, r * 2 + 1, True))
                    pS, pSk = pSr.next(); sS, sSk = ssb.next(); pT, pTk = ptb.next()
                    def mm_s(h):
                        for u, (kp, kc_, qa, _, _, _) in enumerate(units):
                            h.matmul(pS[:, u * 256:u * 256 + 128], kp, qa, start=True, stop=True)
                            ins = h.matmul(pS[:, u * 256 + 128:u * 256 + 256], kc_, qa, start=True, stop=True)
                        return ins
                    P.add("pe", mm_s, reads=[qk, kk], writes=[pSk])
                    bsl = slice(pi * 256, (pi + 1) * 256)
                    halo_flags = [st_ == 0 and un[5] for un in units]
                    P.add("act", lambda h: h.activation(out=sS[:], in_=pS[:], func=AF.Exp, scale=INV), reads=[pSk], writes=[sSk])
                    if all(halo_flags) or not any(halo_flags):
                        wh = 1 if halo_flags[0] else 0
                        P.add("dve", lambda h: h.tensor_tensor(
                            out=pT[:].rearrange("p (u c) -> p u c", u=4), in0=sS[:].rearrange("p (u c) -> p u c", u=4),
                            in1=bib[:, wh, bsl].unsqueeze(1).to_broadcast([128, 4, 256]), op=ALU.mult),
                            reads=[sSk, bbk], writes=[pTk])
                    else:
                        P.add("dve", lambda h: h.tensor_tensor(out=pT[:, 0:256], in0=sS[:, 0:256], in1=bib[:, 1, bsl], op=ALU.mult),
                              reads=[sSk, bbk], writes=[(pTk, "a")])
                        P.add("dve", lambda h: h.tensor_tensor(
                            out=pT[:, 256:1024].rearrange("p (u c) -> p u c", u=3), in0=sS[:, 256:1024].rearrange("p (u c) -> p u c", u=3),
                            in1=bib[:, 0, bsl].unsqueeze(1).to_broadcast([128, 3, 256]), op=ALU.mult),
                            reads=[sSk, bbk, (pTk, "a")], writes=[pTk])
                    return dict(units=units, pT=pT, pTk=pTk, pi=pi, d=d, gq=gq)

                def stageV(sc):
                    units, pT, pTk, pi, d, gq = sc["units"], sc["pT"], sc["pTk"], sc["pi"], sc["d"], sc["gq"]
                    pO, pOk = pOr.next(); pD, pDk = pDr.next()
                    def mm_pv(h):
                        for u, (_, _, _, vb0, vb1, _) in enumerate(units):
                            h.matmul(pO[:, u * 128:(u + 1) * 128], vp[:, pi, vb0, :], pT[:, u * 256:u * 256 + 128], start=True, stop=False)
                            h.matmul(pO[:, u * 128:(u + 1) * 128], vp[:, pi, vb1, :], pT[:, u * 256 + 128:u * 256 + 256], start=False, stop=True)
                        for u in range(4):
                            h.matmul(pD[:, u * 128:(u + 1) * 128], oneb[:], pT[:, u * 256:u * 256 + 128], start=True, stop=False)
                            ins = h.matmul(pD[:, u * 128:(u + 1) * 128], oneb[:], pT[:, u * 256 + 128:u * 256 + 256], start=False, stop=True)
                        return ins
                    P.add("pe", mm_pv, reads=[pTk, "oneb"] + vallk, writes=[pOk, pDk])
                    if d == 1:
                        c0 = gq * 512
                        P.add("act", lambda h: h.copy(out=na[:, c0:c0 + 512], in_=pO[:]), reads=[pOk], writes=[(nk, gq)])
                        P.add("act", lambda h: h.copy(out=da[:, c0:c0 + 512], in_=pD[:]), reads=[pDk], writes=[(dak, gq)])
                    else:
                        if d == 4:
                            c0 = gq * 512
                            nv = lambda a: a[:, c0:c0 + 512].rearrange("p (m r) -> p r m", r=4)
                        else:
                            nv = lambda a: a[:, :].rearrange("p (m r) -> p r m", r=16)[:, 4 * gq:4 * gq + 4, :]
                        pv_ = lambda a: a[:].rearrange("p (r m) -> p r m", r=4)
                        wk = [(nk, gq)] if d == 4 else [(nk, q) for q in range(4)]
                        wdk = [(dak, gq)] if d == 4 else [(dak, q) for q in range(4)]
                        P.add("dve", lambda h: h.tensor_tensor(out=nv(na), in0=pv_(pO), in1=nv(na), op=ALU.add), reads=[pOk] + wk, writes=wk)
                        P.add("dve", lambda h: h.tensor_tensor(out=nv(da), in0=pv_(pD), in1=nv(da), op=ALU.add), reads=[pDk] + wdk, writes=wdk)

                glist = [(0, 1, 0), (0, 1, 1), (1, 4, 0), (0, 1, 2), (1, 4, 1), (0, 1, 3), (2, 16, 0), (1, 4, 2), (2, 16, 1), (1, 4, 3), (2, 16, 2), (2, 16, 3)]
                pend = stageS(*glist[0])
                yield
                for gi_ in range(len(glist)):
                    nxt = stageS(*glist[gi_ + 1]) if gi_ + 1 < len(glist) else None
                    stageV(pend)
                    pend = nxt
                yield
                allk = [(nk, q) for q in range(4)]
                alldk = [(dak, q) for q in range(4)]
                P.add("act", lambda h, da=da: h.activation(out=da[:], in_=da[:], func=AF.Ln), reads=alldk, writes=alldk)
                P.add("act", lambda h, da=da: h.activation(out=da[:], in_=da[:], func=AF.Exp, scale=-1.0), reads=alldk, writes=alldk)
                P.add("dve", lambda h, na=na, da=da: h.tensor_tensor(out=na[:], in0=na[:], in1=da[:], op=ALU.mult), reads=allk + alldk, writes=allk)
                P.add("pool", lambda h, na=na, za=za, oa=oa: h.tensor_tensor(out=oa[:], in0=na[:], in1=za[:], op=ALU.mult), reads=allk + [zk], writes=[ok])
                P.add("pool", lambda h, oa=oa, hd=hd, w0=w0: h.dma_start(out=OaT[hd, :, w0:w0 + 2048], in_=oa[:]), reads=[ok], writes=[("OaT", st_, hd)], dma=True)


            gens = []
            for st_ in range(2):
                w0 = st_ * 2048
                tiles_q = [8 + st_ * 4 + i for i in range(4)]
                tiles_k = [4 + st_ * 4 + i for i in range(8)]
                for hd in range(16):
                    gens.append(do_head(st_, hd, w0, tiles_q, tiles_k))
            next(gens[0])
            for gi2, gen in enumerate(gens):
                next(gen)
                if gi2 + 1 < len(gens):
                    next(gens[gi2 + 1])
                for _ in gen:
                    pass
                for _ in range(4):
                    cv2_tick()
            cv2_flush()

        if stop == "p2":
            P.emit(nc, es, {})
            return nc
        P.barrier()
        final_ops = []
        with ExitStack() as e3:
            s3 = lambda name, shape, dt=F32: e3.enter_context(nc.sbuf_tensor(name, shape, dt))
            p3 = lambda name, shape, dt=F32: e3.enter_context(nc.psum_tensor(name, shape, dt))
            wring = Ring([(s3("w3_%d" % i, [128, 16, 512], BF16), "w3_%d" % i) for i in range(3)])
            hT = s3("dhT", [128, 16, 512], BF16)
            oT = s3("doT", [128, 16, 512], BF16)
            yT = s3("dyT", [128, 32, 512], BF16)
            ynb = Ring([(s3("dyn%d" % i, [128, 4096], BF16), "dyn%d" % i) for i in range(2)])
            mT = s3("dmT", [128, 16, 512], BF16)
            xr4 = Ring([(s3("dxr%d" % i, [128, 2048]), "dxr%d" % i) for i in range(2)])
            sga = s3("dsga", [128, 4, 512], BF16)
            sgs = s3("dsgs", [128, 4, 512], BF16)
            m1 = s3("dm1", [128, 4, 512])
            m2r = Ring([(s3("dm2%d" % i, [128, 512]), "dm2%d" % i) for i in range(1)])
            junk3 = s3("djunk", [128, 2048], BF16)
            stat = Ring([(s3("dst%d" % i, [128, 4]), "dst%d" % i) for i in range(2)])
            psb = Ring([(p3("dps%d" % i, [128, 512]), "dps%d" % i) for i in range(6)])
            ptr = Ring([(p3("dpt%d" % i, [128, 1024], BF16), "dpt%d" % i) for i in range(2)])

            for mt_ in range(8):
                tm0 = mt_ * 512
                ti = 8 + mt_
                st_ = mt_ // 4
                P.add("sp", lambda h, tm0=tm0: h.dma_start(out=hT[:], in_=HnT[:, :, tm0:tm0 + T].rearrange("k p t -> p k t")), reads=[("HnT", ti)], writes=["dhT"], dma=True)
                P.add("sp", lambda h, tm0=tm0: h.dma_start(out=oT[:], in_=OaT[:, :, tm0:tm0 + T].rearrange("k p t -> p k t")), reads=[("OaT", st_, hd) for hd in range(16)], writes=["doT"], dma=True)
                for blk in range(4):
                    yb, ybk = ynb.next()
                    r0 = tm0 + blk * 128
                    P.add("sp", lambda h, yb=yb, r0=r0: h.dma_start(out=yb[:], in_=Yn[r0:r0 + 128, :]), reads=[("Yn", 32 + mt_ * 4 + blk)], writes=[ybk], dma=True)
                    for c4 in range(8):
                        pt, pk = ptr.next()
                        def tr(h, pt=pt, yb=yb, c4=c4):
                            for j in range(4):
                                cc = c4 * 4 + j
                                ins = h.transpose(out=pt[:, j * 128:(j + 1) * 128], in_=yb[:, cc * 128:(cc + 1) * 128], identity=idb[:])
                            return ins
                        P.add("pe", tr, reads=[ybk, "idb"], writes=[pk])
                        P.add("dve", lambda h, pt=pt, c4=c4, blk=blk: h.tensor_tensor(
                            out=yT[:, c4 * 4:(c4 + 1) * 4, blk * 128:(blk + 1) * 128], in0=pt[:, 0:512].rearrange("p (j t) -> p j t", j=4),
                            in1=par[:, P_SNW + c4 * 4:P_SNW + (c4 + 1) * 4].unsqueeze(2).to_broadcast([128, 4, 128]), op=ALU.mult),
                            reads=[pk, "par"], writes=[("dyT", blk, c4)])
                yT_keys = [("dyT", b, c) for b in range(4) for c in range(8)]

                def fm(glist, rhs_of, rkeys, consume):
                    wts = [load_w(wring, g) for g in glist]
                    nk = 16 * len(wts)
                    for cb in range(4):
                        ps, pk = psb.next()
                        def mm(h, wts=wts, ps=ps, cb=cb):
                            n = 0
                            for wi, (wt, _) in enumerate(wts):
                                for kc in range(16):
                                    ins = h.matmul(ps[:], wt[:, kc, cb * 128:(cb + 1) * 128], rhs_of(wi * 16 + kc), start=(n == 0), stop=(n == nk - 1))
                                    n += 1
                            return ins
                        P.add("pe", mm, reads=[k for _, ks in wts for k in ks] + rkeys, writes=[pk])
                        consume(cb, ps, pk)

                for dg in range(4):
                    def c_ga(cb, ps, pk):
                        P.add("act", lambda h: h.activation(out=sga[:, cb, :], in_=ps[:], func=AF.Sigmoid), reads=[pk], writes=[("dsga", cb)])
                    def c_gs(cb, ps, pk):
                        P.add("act", lambda h: h.activation(out=sgs[:, cb, :], in_=ps[:], func=AF.Sigmoid), reads=[pk], writes=[("dsgs", cb)])
                    def c_a(cb, ps, pk):
                        P.add("dve", lambda h: h.tensor_tensor(out=m1[:, cb, :], in0=ps[:], in1=sga[:, cb, :], op=ALU.mult), reads=[pk, ("dsga", cb)], writes=[("dm1", cb)])
                    def c_b(cb, ps, pk, dg=dg):
                        m2, m2k = m2r.next()
                        P.add("dve", lambda h: h.tensor_tensor(out=m2[:], in0=ps[:], in1=sgs[:, cb, :], op=ALU.mult), reads=[pk, ("dsgs", cb)], writes=[m2k])
                        P.add("pool", lambda h: h.tensor_tensor(out=mT[:, dg * 4 + cb, :], in0=m2[:], in1=m1[:, cb, :], op=ALU.add), reads=[m2k, ("dm1", cb)], writes=[("dmT", dg * 4 + cb)])
                    fm([G_GA[dg]], lambda k: hT[:, k, :], ["dhT"], c_ga)
                    fm([G_GS[dg]], lambda k: hT[:, k, :], ["dhT"], c_gs)
                    fm([G_WA[dg]], lambda k: oT[:, k, :], ["doT"], c_a)
                    fm(G_WS[dg], lambda k: yT[:, k, :], yT_keys, c_b)
                mT_keys = [("dmT", k) for k in range(16)]
                for blk in range(4):
                    xb, xbk = xr4.next()
                    r0 = tm0 + blk * 128
                    P.add("sp", lambda h, xb=xb, r0=r0: h.dma_start(out=xb[:], in_=xe[4096 + r0:4096 + r0 + 128, :]), writes=[xbk], dma=True)
                    for cg in range(4):
                        wt, wkeys = load_w(wring, G_WO[cg])
                        ps, pk = psb.next()
                        def mm(h, wt=wt, ps=ps, blk=blk):
                            for kc in range(16):
                                ins = h.matmul(ps[:], mT[:, kc, blk * 128:(blk + 1) * 128], wt[:, kc, :], start=(kc == 0), stop=(kc == 15))
                            return ins
                        P.add("pe", mm, reads=wkeys + mT_keys, writes=[pk])
                        P.add("dve", lambda h, ps=ps, xb=xb, cg=cg: h.tensor_tensor(out=xb[:, cg * 512:(cg + 1) * 512], in0=ps[:], in1=xb[:, cg * 512:(cg + 1) * 512], op=ALU.add),
                              reads=[pk, xbk], writes=[xbk])
                    stt_, sk = stat.next()
                    P.add("act", lambda h, stt_=stt_, xb=xb: h.activation(out=junk3[:], in_=xb[:], func=AF.Square, accum_out=stt_[:, 0:1]),
                          reads=[xbk], writes=["djunk", (sk, 0)])
                    P.add("dve", lambda h, stt_=stt_: h.tensor_scalar(out=stt_[:, 1:2], in0=stt_[:, 0:1], scalar1=1.0 / D, scalar2=EPS, op0=ALU.mult, op1=ALU.add), reads=[(sk, 0)], writes=[(sk, 1)])
                    P.add("act", lambda h, stt_=stt_: h.activation(out=stt_[:, 2:3], in_=stt_[:, 1:2], func=AF.Sqrt), reads=[(sk, 1)], writes=[(sk, 2)])
                    P.add("dve", lambda h, stt_=stt_: h.reciprocal(out=stt_[:, 3:4], in_=stt_[:, 2:3]), reads=[(sk, 2)], writes=[(sk, 3)])
                    P.add("dve", lambda h, stt_=stt_, xb=xb: h.scalar_tensor_tensor(out=xb[:], in0=xb[:], scalar=stt_[:, 3:4], in1=par[:, P_FNW:P_FNW + 2048], op0=ALU.mult, op1=ALU.mult),
                          reads=[xbk, (sk, 3), "par"], writes=[xbk])
                    final_ops.append(P.add("pool", lambda h, xb=xb, r0=r0: h.dma_start(out=out[r0:r0 + 128, :], in_=xb[:]), reads=[xbk], dma=True))

        P.emit(nc, es, {"pool": final_ops})
    return nc


_CACHE = {}


def _alibi_tables():
    tabs = np.zeros((16, 128, 768), np.float32)
    k = np.arange(128)[:, None]
    q = np.arange(128)[None, :]
    for hd in range(16):
        slope = np.float32(2.0 ** (-8.0 * (hd + 1) / 16))
        for pi, d in enumerate((1, 4, 16)):
            dist_p = (q - k + 128).astype(np.float32)
            prev = np.where(k >= q, -slope * dist_p * d, NEG)
            dist_c = (q - k).astype(np.float32)
            cur = np.where(k <= q, -slope * dist_c * d, NEG)
            tabs[hd, :, pi * 256:pi * 256 + 128] = prev
            tabs[hd, :, pi * 256 + 128:pi * 256 + 256] = cur
    return np.exp(tabs.astype(np.float64)).astype(np.float32)


def kernel(x, norm_w, w_in, conv_w, conv_b, dt_bias, a_log, d_skip, ssm_norm_w,
           w_attn_branch, w_ssm_branch, w_out, final_norm_w):
    f = lambda a: np.ascontiguousarray(np.asarray(a, dtype=np.float32))
    x = f(x)
    if "nc" not in _CACHE:
        _CACHE["nc"] = build_program()
    nc = _CACHE["nc"]
    par = np.zeros((128, NPAR), np.float32)
    par[:, P_NORMW:P_NORMW + 2048] = f(norm_w)[0][None, :]
    par[:, P_FNW:P_FNW + 2048] = f(final_norm_w)[None, :]
    cw = f(conv_w)[0]
    par[:, P_CW:P_CW + 192] = cw.reshape(4, 48, 128).transpose(2, 1, 0).reshape(128, 192)
    par[:, P_CB:P_CB + 48] = f(conv_b)[0].reshape(48, 128).T
    par[:, P_SNW:P_SNW + 32] = f(ssm_norm_w)[0].reshape(32, 128).T
    par[:, P_DTB:P_DTB + 64] = f(dt_bias)[0][None, :]
    par[:, P_ALOG:P_ALOG + 64] = f(a_log)[0][None, :]
    par[:, P_DSK:P_DSK + 64] = f(d_skip)[0][None, :]
    par[:, P_ID:P_ID + 128] = np.eye(128, dtype=np.float32)
    ki = np.arange(128)
    par[:, P_TM:P_TM + 128] = (ki[:, None] <= ki[None, :]).astype(np.float32)
    par[:, P_UM:P_UM + 128] = (ki[:, None] > ki[None, :]).astype(np.float32)
    par[:, P_ONE:P_ONE + 128] = 1.0
    tabs = _alibi_tables()
    tabs0 = tabs.copy()
    for pi in range(3):
        tabs0[:, :, pi * 256:pi * 256 + 128] = 0.0
    wi, wa, wss, wo = f(w_in)[0], f(w_attn_branch)[0], f(w_ssm_branch)[0], f(w_out)[0]
    in_maps = []
    for c in range(NCORE):
        b, hf = c // 2, c % 2
        if hf == 0:
            xe = np.concatenate([np.zeros((4096, D), np.float32), x[b, :4096]], axis=0)
        else:
            xe = x[b]
        p = par.copy()
        p[:, P_PV] = float(hf)
        in_maps.append({"xe": np.ascontiguousarray(xe), "w_in": wi, "w_attn": wa, "w_ssm": wss, "w_out": wo,
                        "params": p, "biasA": tabs, "bias0": tabs if hf == 1 else tabs0})
    if _CACHE.get("return_maps"):
        return in_maps
    res = run_bass_kernel_spmd(nc, in_maps, core_ids=list(range(NCORE)))
    outp = np.empty((4, SEQ, D), np.float32)
    for c in range(NCORE):
        b, hf = c // 2, c % 2
        outp[b, hf * 4096:(hf + 1) * 4096] = res.results[c]["out"]
    return outp
```
